# Optimizing a Trainium2 kernel written in Bass

```python
import math
import jax, jax.numpy as jnp
from jax import lax
import numpy as np

D_MODEL = 2048
BATCH = 4
SEQ = 8192
DEPTH = 1
DEC_BATCH = 8
DEC_SEQ = 4096
PAST_LEN = 128

N_META = 16
N_HEADS = 8
HEAD_DIM = 64
V_DIM = 2 * HEAD_DIM
D_ATT = N_HEADS * V_DIM
D_QK = N_HEADS * 2 * HEAD_DIM
D_CONV = D_MODEL - D_ATT
D_IN = 2 * D_QK + D_ATT + 2 * D_CONV
ROT_DIM = HEAD_DIM // 4
ROPE_THETA = 500000.0
CONV_WIDTH = 31
D_FF = 5632
FFN_CONV_WIDTH = 3
Q_BLOCK = 128
LN_EPS = 1e-5
DEEPNORM_ALPHA = (2.0 * DEPTH) ** 0.25
DEEPNORM_BETA = (8.0 * DEPTH) ** -0.25

kernel_name = "hybrid_diffattn_conformer_encoder"


def _lambda_init(layer_idx):
    return 0.8 - 0.6 * math.exp(-0.3 * layer_idx)


def _layer_norm(x, g, b):
    xf = x.astype(jnp.float32)
    mu = jnp.mean(xf, axis=-1, keepdims=True)
    var = jnp.mean(jnp.square(xf - mu), axis=-1, keepdims=True)
    y = (xf - mu) * lax.rsqrt(var + LN_EPS) * g.astype(jnp.float32) + b.astype(jnp.float32)
    return y.astype(x.dtype)


def _rms_norm(x, g):
    xf = x.astype(jnp.float32)
    y = xf * lax.rsqrt(jnp.mean(jnp.square(xf), axis=-1, keepdims=True) + LN_EPS) * g.astype(jnp.float32)
    return y.astype(x.dtype)


def _partial_rope(x, pos):
    inv_freq = ROPE_THETA ** (-jnp.arange(0, ROT_DIM, 2, dtype=jnp.float32) / ROT_DIM)
    ang = pos.astype(jnp.float32)[:, None] * inv_freq[None, :]
    ang = jnp.concatenate([ang, ang], axis=-1)[None, :, None, None, :]
    cos, sin = jnp.cos(ang), jnp.sin(ang)
    xr = x[..., :ROT_DIM].astype(jnp.float32)
    x1, x2 = xr[..., : ROT_DIM // 2], xr[..., ROT_DIM // 2:]
    rot = xr * cos + jnp.concatenate([-x2, x1], axis=-1) * sin
    return jnp.concatenate([rot.astype(x.dtype), x[..., ROT_DIM:]], axis=-1)


def _depthwise_conv(x, w, b):
    k = w.shape[0]
    y = lax.conv_general_dilated(
        x, w[:, None, :].astype(x.dtype), window_strides=(1,),
        padding=[(k // 2, k // 2)], dimension_numbers=("NWC", "WIO", "NWC"),
        feature_group_count=x.shape[-1])
    return y + b


def _diff_attention(q, k, v, lam):
    bsz, seq_len = q.shape[0], q.shape[1]
    n_blocks = -(-seq_len // Q_BLOCK)
    pad = n_blocks * Q_BLOCK - seq_len
    qp = jnp.pad(q, ((0, 0), (0, pad), (0, 0), (0, 0), (0, 0)))
    qb = qp.reshape(bsz, n_blocks, Q_BLOCK, N_HEADS, 2, HEAD_DIM).transpose(1, 0, 2, 3, 4, 5)
    scale = HEAD_DIM ** -0.5

    def block(q_blk):
        s = jnp.einsum("bqhcd,bkhcd->bhcqk", q_blk, k).astype(jnp.float32) * scale
        p = jax.nn.softmax(s, axis=-1)
        w = p[:, :, 0] - lam * p[:, :, 1]
        return jnp.einsum("bhqk,bkhe->bqhe", w.astype(v.dtype), v)

    out = lax.map(block, qb)
    out = out.transpose(1, 0, 2, 3, 4).reshape(bsz, n_blocks * Q_BLOCK, N_HEADS, V_DIM)
    return out[:, :seq_len]


def _trunk(x, meta_tokens, ln_emb_g, ln_emb_b, w_in, b_in, lambda_q1, lambda_k1, lambda_q2, lambda_k2,
           subln_g, conv_w, conv_b, conv_ln_g, conv_ln_b, w_out, b_out, ln1_g, ln1_b,
           w_up, b_up, ffn_conv_w, ffn_conv_b, w_down, b_down, ln2_g, ln2_b):
    bsz = x.shape[0]
    meta = jnp.broadcast_to(meta_tokens[None].astype(x.dtype), (bsz, N_META, D_MODEL))
    h = jnp.concatenate([meta, x], axis=1)
    h = _layer_norm(h, ln_emb_g, ln_emb_b)
    seq_len = h.shape[1]
    pos = jnp.arange(seq_len, dtype=jnp.int32)
    for l in range(DEPTH):
        lam_init = _lambda_init(l)
        proj = h @ w_in[l] + b_in[l]
        q = proj[..., :D_QK].reshape(bsz, seq_len, N_HEADS, 2, HEAD_DIM)
        k = proj[..., D_QK:2 * D_QK].reshape(bsz, seq_len, N_HEADS, 2, HEAD_DIM)
        v = proj[..., 2 * D_QK:2 * D_QK + D_ATT].reshape(bsz, seq_len, N_HEADS, V_DIM)
        u = proj[..., 2 * D_QK + D_ATT:]
        q = _partial_rope(q, pos)
        k = _partial_rope(k, pos)
        lam = (jnp.exp(jnp.sum(lambda_q1[l].astype(jnp.float32) * lambda_k1[l].astype(jnp.float32)))
               - jnp.exp(jnp.sum(lambda_q2[l].astype(jnp.float32) * lambda_k2[l].astype(jnp.float32)))
               + lam_init)
        att = _diff_attention(q, k, v, lam)
        att = (_rms_norm(att, subln_g[l]) * (1.0 - lam_init)).reshape(bsz, seq_len, D_ATT)
        c = u[..., :D_CONV] * jax.nn.sigmoid(u[..., D_CONV:])
        c = _depthwise_conv(c, conv_w[l], conv_b[l])
        c = jax.nn.silu(_layer_norm(c, conv_ln_g[l], conv_ln_b[l]))
        mix = jnp.concatenate([att, c], axis=-1) @ w_out[l] + b_out[l]
        h = _layer_norm(DEEPNORM_ALPHA * h + mix, ln1_g[l], ln1_b[l])
        up = _depthwise_conv(h @ w_up[l] + b_up[l], ffn_conv_w[l], ffn_conv_b[l])
        f = jax.nn.silu(up[..., :D_FF]) * up[..., D_FF:]
        ffn = f @ w_down[l] + b_down[l]
        h = _layer_norm(DEEPNORM_ALPHA * h + ffn, ln2_g[l], ln2_b[l])
    return h[:, N_META:]


def setup_inputs(seed: int = 0) -> dict:
    key = jax.random.key(seed)
    ks = jax.random.split(key, 32)
    f32 = jnp.float32

    def nrm(k, shape, scale):
        return jax.random.normal(k, shape, f32) * scale

    def gain(k, shape):
        return 1.0 + 0.01 * jax.random.normal(k, shape, f32)

    return {
        "x_prompt": nrm(ks[0], (BATCH, SEQ, D_MODEL), 1.0),
        "x_sample": nrm(ks[1], (DEC_BATCH, DEC_SEQ, D_MODEL), 1.0),
        "meta_tokens": nrm(ks[2], (N_META, D_MODEL), 1.0),
        "ln_emb_g": gain(ks[3], (D_MODEL,)),
        "ln_emb_b": nrm(ks[4], (D_MODEL,), 0.01),
        "w_in": nrm(ks[5], (DEPTH, D_MODEL, D_IN), D_MODEL ** -0.5),
        "b_in": nrm(ks[6], (DEPTH, D_IN), 0.01),
        "lambda_q1": nrm(ks[7], (DEPTH, HEAD_DIM), 0.1),
        "lambda_k1": nrm(ks[8], (DEPTH, HEAD_DIM), 0.1),
        "lambda_q2": nrm(ks[9], (DEPTH, HEAD_DIM), 0.1),
        "lambda_k2": nrm(ks[10], (DEPTH, HEAD_DIM), 0.1),
        "subln_g": gain(ks[11], (DEPTH, V_DIM)),
        "conv_w": nrm(ks[12], (DEPTH, CONV_WIDTH, D_CONV), CONV_WIDTH ** -0.5),
        "conv_b": nrm(ks[13], (DEPTH, D_CONV), 0.01),
        "conv_ln_g": gain(ks[14], (DEPTH, D_CONV)),
        "conv_ln_b": nrm(ks[15], (DEPTH, D_CONV), 0.01),
        "w_out": nrm(ks[16], (DEPTH, D_MODEL, D_MODEL), DEEPNORM_BETA * D_MODEL ** -0.5),
        "b_out": nrm(ks[17], (DEPTH, D_MODEL), 0.01),
        "ln1_g": gain(ks[18], (DEPTH, D_MODEL)),
        "ln1_b": nrm(ks[19], (DEPTH, D_MODEL), 0.01),
        "w_up": nrm(ks[20], (DEPTH, D_MODEL, 2 * D_FF), D_MODEL ** -0.5),
        "b_up": nrm(ks[21], (DEPTH, 2 * D_FF), 0.01),
        "ffn_conv_w": nrm(ks[22], (DEPTH, FFN_CONV_WIDTH, 2 * D_FF), FFN_CONV_WIDTH ** -0.5),
        "ffn_conv_b": nrm(ks[23], (DEPTH, 2 * D_FF), 0.01),
        "w_down": nrm(ks[24], (DEPTH, D_FF, D_MODEL), DEEPNORM_BETA * D_FF ** -0.5),
        "b_down": nrm(ks[25], (DEPTH, D_MODEL), 0.01),
        "ln2_g": gain(ks[26], (DEPTH, D_MODEL)),
        "ln2_b": nrm(ks[27], (DEPTH, D_MODEL), 0.01),
    }


def reference(x_prompt, x_sample, meta_tokens, ln_emb_g, ln_emb_b, w_in, b_in, lambda_q1, lambda_k1,
              lambda_q2, lambda_k2, subln_g, conv_w, conv_b, conv_ln_g, conv_ln_b, w_out, b_out,
              ln1_g, ln1_b, w_up, b_up, ffn_conv_w, ffn_conv_b, w_down, b_down, ln2_g, ln2_b):
    params = (meta_tokens, ln_emb_g, ln_emb_b, w_in, b_in, lambda_q1, lambda_k1, lambda_q2, lambda_k2,
              subln_g, conv_w, conv_b, conv_ln_g, conv_ln_b, w_out, b_out, ln1_g, ln1_b,
              w_up, b_up, ffn_conv_w, ffn_conv_b, w_down, b_down, ln2_g, ln2_b)
    y_prompt = _trunk(x_prompt, *params)
    y_sample = _trunk(x_sample, *params)
    return (y_prompt, y_sample)
```

```python
import math
from contextlib import ExitStack
import numpy as np
import concourse.bass as bass
import concourse.mybir as mybir
from concourse.bass_utils import run_bass_kernel_spmd

F32 = mybir.dt.float32
BF16 = mybir.dt.bfloat16
AF = mybir.ActivationFunctionType
ALU = mybir.AluOpType
AX = mybir.AxisListType

D = 2048
N_META = 16
NH = 8
HD = 64
D_ATT = 1024
D_QK = 1024
D_CONV = 1024
D_IN = 5120
CW = 31
D_FF = 5632
NFC = D_FF // 128
LN_EPS = 1e-5
ALPHA = 2.0 ** 0.25
LAM_INIT = 0.8 - 0.6 * math.exp(0.0)
ROPE_THETA = 500000.0
NCORES = 8
import os
PIPE_WIDTH = int(os.environ.get('K_PIPE', '2'))
PIPE_PH = os.environ.get('K_PH', '1268')
EPI_INTERLEAVE = int(os.environ.get('K_EPI', '1'))


class Buf:
    __slots__ = ("name", "lw", "rd")

    def __init__(self, name=""):
        self.name = name
        self.lw = None
        self.rd = {}


class DSem:
    def __init__(self, h, key):
        self.h = h
        self.key = key
        self.count = 0


class T:
    def __init__(self, t, name, ds=None):
        self.t = t
        self.b = Buf(name)
        self.ds = ds

    def __getitem__(self, idx):
        return self.t[idx]


class FW:
    ENG = ("pe", "act", "dve", "pool", "sp")
    STRICT = True

    def __init__(self, nc, es):
        self.nc = nc
        self.es = es
        self.eng = {"pe": nc.tensor, "act": nc.scalar, "dve": nc.vector, "pool": nc.gpsimd, "sp": nc.sync}
        self.sem = {e: es.enter_context(nc.semaphore("s_" + e)) for e in ("pe", "act", "dve", "pool")}
        self.cnt = {e: 0 for e in self.sem}
        self.waited = {e: {} for e in self.ENG}
        self.dsems = []
        self.nops = 0

    def new_dsem(self, name):
        h = self.es.enter_context(self.nc.semaphore("d_" + name))
        d = DSem(h, "d_" + name + str(len(self.dsems)))
        self.dsems.append(d)
        return d

    def tile(self, es, shape, dtype, name, dma=False, psum=False):
        self.ntile = getattr(self, "ntile", 0) + 1
        name = "t%d_%s" % (self.ntile, name)
        if psum:
            t = es.enter_context(self.nc.psum_tensor(name, list(shape), dtype))
        else:
            t = es.enter_context(self.nc.sbuf_tensor(name, list(shape), dtype))
        return T(t, name, self.new_dsem(name) if dma else None)

    def _need(self, e, dep):
        key, h, val = dep
        if self.waited[e].get(key, 0) >= val:
            return
        if key == e:
            if val > self.cnt[e]:
                assert FW.STRICT, "same-engine wait on future value"
                return
        elif key in self.cnt:
            assert val <= self.cnt[key], "wait on a not-yet-emitted signal (deadlock hazard)"
        self.eng[e].wait_ge(h, val)
        self.waited[e][key] = val

    def _deps(self, e, reads, writes, own):
        for t in reads:
            b = t.b
            if b.lw is not None:
                self._need(e, b.lw)
        strict = FW.STRICT
        for t in writes:
            b = t.b
            if b.lw is not None and (strict or b.lw[0] != own):
                self._need(e, b.lw)
            for key, (h, val) in b.rd.items():
                if strict or key != own:
                    self._need(e, (key, h, val))

    def _record(self, dep, reads, writes):
        key, h, val = dep
        for t in reads:
            t.b.rd[key] = (h, val)
        for t in writes:
            t.b.lw = dep
            t.b.rd = {}

    def op(self, e, fn, reads=(), writes=(), signal=True):
        self._deps(e, reads, writes, e)
        inst = fn()
        self.nops += 1
        if signal:
            self.cnt[e] += 1
            inst.then_inc(self.sem[e], 1)
            v = self.cnt[e]
        else:
            v = self.cnt[e] + 1
        self._record((e, self.sem[e], v), reads, writes)

    def dma(self, q, out, in_, reads, writes, ds, **kw):
        self._deps(q, reads, writes, None)
        self._need(q, (ds.key, ds.h, ds.count))
        inst = self.eng[q].dma_start(out=out, in_=in_, **kw)
        self.nops += 1
        ds.count += 16
        inst.then_inc(ds.h, 16)
        self._record((ds.key, ds.h, ds.count), reads, writes)

    def barrier(self):
        for e in self.ENG:
            for k in self.sem:
                if self.cnt[k] > 0:
                    self._need(e, (k, self.sem[k], self.cnt[k])) if k != e else None
            for d in self.dsems:
                if d.count > 0:
                    self._need(e, (d.key, d.h, d.count))


class Cfg:
    def __init__(self, NE, NO, NE2, L2, debug=(), half_p=None):
        self.NE, self.NO, self.NE2 = NE, NO, NE2
        self.vhi = {"p": 32 + half_p, "s": 16 + L2}
        self.rhi = {"p": 32 + half_p, "s": 16 + L2}
        self.E, self.O, self.E2 = NE * 128, NO * 128, NE2 * 128
        self.debug = set(debug)


def blocks512(ntiles):
    out = []
    t = 0
    while t < ntiles:
        n = min(4, ntiles - t)
        out.append((t * 128, n))
        t += n
    return out


def ffn_blocks(rhi):
    out = []
    s = 31
    end = rhi + 1
    while s + 1 < rhi:
        n = min(512, end - s)
        out.append((s, n))
        s += n - 2
    return out


def build(cfg):
    nc = bass.Bass("TRN2", target_bir_lowering=False)
    NE, NO, NE2 = cfg.NE, cfg.NO, cfg.NE2
    E, O, E2 = cfg.E, cfg.O, cfg.E2
    segs = [("p", NE, NO), ("s", NE2, 0)]

    def din(name, shape, dt=F32):
        return nc.dram_tensor(name, list(shape), dt, kind="ExternalInput").ap()

    def dscr(name, shape, dt):
        kind = "ExternalOutput" if name in cfg.debug else "Internal"
        return nc.dram_tensor(name, list(shape), dt, kind=kind).ap()

    X = {"p": din("xe_p", [E + O, D]), "s": din("xe_s", [E2, D])}
    MASK8 = {"p": din("mask8_p", [E + O, 8]), "s": din("mask8_s", [E2, 8])}
    MROW = {"p": din("mrow_p", [1, E]), "s": din("mrow_s", [1, E2])}
    ROPE = {"p": din("rope_p", [E + O, 256]), "s": din("rope_s", [E2, 256])}
    w_in = din("w_in", [D, D_IN])
    w_out = din("w_out", [D, D])
    w_up = din("w_up", [D, 2 * D_FF])
    w_down = din("w_down", [D_FF, D])
    ident_in = din("ident", [128, 128])
    rowp = din("rowp", [16, D])
    b_qkv = din("b_qkv", [1, 3072])
    lamv = din("lamv", [4, 64])
    subg = din("subg", [1, 128])
    colp = din("colp", [128, 16 + 8 * 31 + 8 * 3 + 88 * 5 + 32])
    ROWS = {"emb_g": 0, "emb_b": 1, "b_out": 2, "ln1_g": 3, "ln1_b": 4, "b_down": 5, "ln2_g": 6, "ln2_b": 7}

    Y = {"p": nc.dram_tensor("y_p", [E, D], F32, kind="ExternalOutput").ap(),
         "s": nc.dram_tensor("y_s", [E2, D], F32, kind="ExternalOutput").ap()}

    S = {}
    for sg, ne, no in segs:
        nt = ne + no
        nb = len(blocks512(ne))
        S[sg] = dict(
            h0s=dscr("h0s_" + sg, [ne, 128, D], F32),
            h0T=dscr("h0T_" + sg, [nt, 128, 16, 128], BF16),
            qT=dscr("qT_" + sg, [NH, 128, ne * 128], BF16),
            kT=dscr("kT_" + sg, [NH, 128, nt * 128], BF16),
            v=dscr("v_" + sg, [NH, 128, nt, 129], BF16),
            c=dscr("c_" + sg, [8, 128, ne * 128], BF16),
            mixT=dscr("mixT_" + sg, [nb, 128, 16, 512], BF16),
            h1s=dscr("h1s_" + sg, [ne, 128, D], F32),
            h1T=dscr("h1T_" + sg, [128, 16, ne * 128], BF16),
            fT=dscr("fT_" + sg, [128, NFC, ne * 128], BF16),
            ffa=dscr("ffa_" + sg, [ne, 128, 1024], F32),
        )

    with ExitStack() as es0:
        fw = FW(nc, es0)
        ps_t = es0.enter_context(nc.psum_tensor("ps", [128, 8, 512], F32))
        ps_bf = ps_t.bitcast(BF16)
        PB = [T(None, "psb%d" % i) for i in range(8)]

        def ps(b, n=512, nb=1):
            if nb == 1:
                return ps_t[:, b, 0:n]
            return ps_t[:, b:b + nb, :]

        ident_f = fw.tile(es0, [128, 128], F32, "ident_f", dma=True)
        ident_b = fw.tile(es0, [128, 128], BF16, "ident_b")
        colp_t = fw.tile(es0, [128, colp.shape[1]], F32, "colp", dma=True)
        neglam = fw.tile(es0, [128, 1], F32, "neglam")
        gsub = fw.tile(es0, [128, 128], F32, "gsub", dma=True)
        mhalf = fw.tile(es0, [128, 1], F32, "mhalf")
        epsc = fw.tile(es0, [128, 1], F32, "epsc")
        fw.dma("sp", ident_f[:], ident_in, [], [ident_f], ident_f.ds)
        fw.dma("sp", colp_t[:], colp, [], [colp_t], colp_t.ds)
        fw.dma("sp", gsub[:], subg.partition_broadcast(128), [], [gsub], gsub.ds)
        fw.op("dve", lambda: nc.vector.tensor_copy(out=ident_b[:], in_=ident_f[:]), [ident_f], [ident_b])
        fw.op("dve", lambda: nc.vector.memset(mhalf[:], -0.5), [], [mhalf])
        fw.op("dve", lambda: nc.vector.memset(epsc[:], LN_EPS), [], [epsc])
        fw.op("dve", lambda: nc.vector.tensor_scalar(out=gsub[:], in0=gsub[:], scalar1=1.0 - LAM_INIT, scalar2=None,
                                                    op0=ALU.mult), [gsub], [gsub])
        CP_BU = 0
        CP_CW = 16
        CP_C3 = CP_CW + 8 * 31
        CP_F = CP_C3 + 8 * 3
        CP_EG = CP_F + 88 * 5
        CP_EB = CP_EG + 16

        def cp(i):
            return colp_t[:, i:i + 1]

        if True:
            es = es0
            lt = fw.tile(es, [128, 4, 64], F32, "lamt", dma=True)
            lp = fw.tile(es, [128, 2, 64], F32, "lamp")
            ls = fw.tile(es, [128, 2], F32, "lams")
            le = fw.tile(es, [128, 2], F32, "lame")
            fw.dma("sp", lt[:].rearrange("p a b -> p (a b)"), lamv.rearrange("(o a) b -> o (a b)", o=1).partition_broadcast(128),
                   [], [lt], lt.ds)
            fw.op("dve", lambda: nc.vector.tensor_tensor(out=lp[:, 0, :], in0=lt[:, 0, :], in1=lt[:, 1, :], op=ALU.mult), [lt], [lp])
            fw.op("dve", lambda: nc.vector.tensor_tensor(out=lp[:, 1, :], in0=lt[:, 2, :], in1=lt[:, 3, :], op=ALU.mult), [lt], [lp])
            fw.op("dve", lambda: nc.vector.tensor_reduce(out=ls[:], in_=lp[:], axis=AX.X, op=ALU.add), [lp], [ls])
            fw.op("act", lambda: nc.scalar.activation(out=le[:], in_=ls[:], func=AF.Exp), [ls], [le])
            fw.op("dve", lambda: nc.vector.tensor_tensor(out=neglam[:], in0=le[:, 1:2], in1=le[:, 0:1], op=ALU.subtract), [le], [neglam])
            fw.op("dve", lambda: nc.vector.tensor_scalar(out=neglam[:], in0=neglam[:], scalar1=-LAM_INIT, scalar2=None,
                                                        op0=ALU.add), [neglam], [neglam])

        def load_w(es, name, src, r0, nk, c0, ncols, dst=None, dstc0=0):
            if dst is None:
                dst = fw.tile(es, [128, nk, ncols], BF16, name, dma=True)
            KG = int(os.environ.get('K_WKG', '16'))
            c = 0
            while c < ncols:
                n = min(1024, ncols - c)
                k0 = 0
                while k0 < nk:
                    kk = min(KG, nk - k0)
                    fw.dma("pool", dst[:, k0:k0 + kk, dstc0 + c:dstc0 + c + n],
                           src[r0 + k0 * 128:r0 + (k0 + kk) * 128, c0 + c:c0 + c + n].rearrange("(k p) c -> p k c", p=128),
                           [], [dst], dst.ds, max_dma_last_dim=4096)
                    k0 += kk
                c += n
            return dst

        def bcast_row(es, name, r):
            t = fw.tile(es, [128, D], F32, name, dma=True)
            fw.dma("sp", t[:], rowp[r:r + 1, :].partition_broadcast(128), [], [t], t.ds)
            return t

        def layer_norm_tm(es_unused, src_ap, src_bufs, bufs, g_bc, b_bc, out_t, mul_eng="pool"):
            st, mv, rs, nm, xn = bufs["st"], bufs["mv"], bufs["rs"], bufs["nm"], bufs["xn"]
            for i in range(4):
                fw.op("dve", lambda i=i: nc.vector.bn_stats(out=st[:, i * 6:(i + 1) * 6], in_=src_ap[:, i * 512:(i + 1) * 512]),
                      src_bufs, [st])
            yield
            fw.op("dve", lambda: nc.vector.bn_aggr(out=mv[:], in_=st[:]), [st], [mv])
            yield
            fw.op("dve", lambda: nc.vector.tensor_scalar(out=rs[:], in0=mv[:, 1:2], scalar1=LN_EPS, scalar2=None, op0=ALU.add),
                  [mv], [rs])
            yield
            fw.op("pool", lambda: nc.gpsimd.tensor_tensor(out=rs[:], in0=rs[:], in1=mhalf[:], op=ALU.pow), [rs, mhalf], [rs])
            yield
            fw.op("dve", lambda: nc.vector.tensor_scalar(out=nm[:], in0=mv[:, 0:1], scalar1=rs[:], scalar2=-1.0,
                                                        op0=ALU.mult, op1=ALU.mult), [mv, rs], [nm])
            yield
            fw.op("act", lambda: nc.scalar.activation(out=xn[:], in_=src_ap, func=AF.Identity, bias=nm[:], scale=rs[:]),
                  list(src_bufs) + [nm, rs], [xn])
            yield
            me = fw.eng[mul_eng]
            fw.op(mul_eng, lambda: me.tensor_tensor(out=xn[:], in0=xn[:], in1=g_bc[:], op=ALU.mult), [xn, g_bc], [xn])
            yield
            fw.op("dve", lambda: nc.vector.tensor_tensor(out=out_t[:], in0=xn[:], in1=b_bc[:], op=ALU.add), [xn, b_bc], [out_t])
            yield

        def ln_bufs(es, tag, dma=False):
            return dict(st=fw.tile(es, [128, 24], F32, "st" + tag), mv=fw.tile(es, [128, 2], F32, "mv" + tag),
                        rs=fw.tile(es, [128, 1], F32, "rs" + tag), nm=fw.tile(es, [128, 1], F32, "nm" + tag),
                        xn=fw.tile(es, [128, D], F32, "xn" + tag, dma=dma))

        def transpose16(src_bf, src_bufs, dstT, pbanks, evac_eng):
            for half in range(2):
                pb = pbanks[half]
                for k in range(8):
                    kc = half * 8 + k
                    fw.op("pe", lambda kc=kc, k=k, pb=pb: nc.tensor.transpose(out=ps_bf[:, pb, k * 128:(k + 1) * 128],
                                                                             in_=src_bf[:, kc * 128:(kc + 1) * 128], identity=ident_b[:]),
                          list(src_bufs) + [ident_b], [PB[pb]], signal=(k == 7))
                yield
                if evac_eng == "act":
                    fw.op("act", lambda half=half, pb=pb: nc.scalar.copy(out=dstT[:, half * 8:(half + 1) * 8, :].rearrange("p a b -> p (a b)"),
                                                                         in_=ps_bf[:, pb, :]), [PB[pb]], [dstT])
                else:
                    fw.op("dve", lambda half=half, pb=pb: nc.vector.tensor_copy(out=dstT[:, half * 8:(half + 1) * 8, :].rearrange("p a b -> p (a b)"),
                                                                               in_=ps_bf[:, pb, :]), [PB[pb]], [dstT])
                yield

        def pipeline(factories, lag, width=None, loads=None, pf=0):
            width = PIPE_WIDTH if width is None else width
            nitems = len(factories)
            active = []
            state = {"next": 0}
            if loads is not None:
                for n in range(min(pf, nitems)):
                    loads(n)

            def start():
                n = state["next"]
                state["next"] += 1
                if loads is not None and n + pf < nitems:
                    loads(n + pf)
                active.append([factories[n](), 0])

            start()
            while active or state["next"] < nitems:
                for g in list(active):
                    try:
                        next(g[0])
                        g[1] += 1
                    except StopIteration:
                        active.remove(g)
                if state["next"] < nitems and len(active) < width and (not active or active[-1][1] >= lag):
                    start()

        es12 = ExitStack()
        wq = fw.tile(es12, [128, 16, 3072], BF16, "wqkv", dma=True)
        with ExitStack() as es:
            g_bc = bcast_row(es, "embg", ROWS["emb_g"])
            b_bc = bcast_row(es, "embb", ROWS["emb_b"])
            bo_bc = bcast_row(es, "bout", ROWS["b_out"])
            fw.op("dve", lambda: nc.vector.tensor_scalar(out=g_bc[:], in0=g_bc[:], scalar1=ALPHA, scalar2=None, op0=ALU.mult), [g_bc], [g_bc])
            fw.op("dve", lambda: nc.vector.scalar_tensor_tensor(out=b_bc[:], in0=b_bc[:], scalar=ALPHA, in1=bo_bc[:],
                                                                op0=ALU.mult, op1=ALU.add), [b_bc, bo_bc], [b_bc])
            W1 = int(os.environ.get('K_W1', '3'))
            NX = W1 + 2
            xt = [fw.tile(es, [128, D], F32, "xt%d" % i, dma=True) for i in range(NX)]
            h0s = [fw.tile(es, [128, D], F32, "h0s%d" % i, dma=True) for i in range(W1)]
            hT = [fw.tile(es, [128, 16, 128], BF16, "hT%d" % i, dma=True) for i in range(W1)]
            sm = [dict(st=fw.tile(es, [128, 24], F32, "st1_%d" % i), mv=fw.tile(es, [128, 2], F32, "mv1_%d" % i),
                       rs=fw.tile(es, [128, 1], F32, "rs1_%d" % i), nm=fw.tile(es, [128, 1], F32, "nm1_%d" % i)) for i in range(W1)]
            tiles = [(sg, i, i < ne) for sg, ne, no in segs for i in range(ne + no)]

            def p1_load(n):
                sg, i, isE = tiles[n]
                x_t = xt[n % NX]
                fw.dma("sp", x_t[:], X[sg][i * 128:(i + 1) * 128, :], [], [x_t], x_t.ds)

            def p1_tile(n):
                sg, i, isE = tiles[n]
                s = n % W1
                if n == min(6, len(tiles) - 1):
                    load_w(es12, "wqkv", w_in, 0, 16, 0, 3072, dst=wq)
                xn = xt[n % NX]
                st, mv, rs, nm = sm[s]["st"], sm[s]["mv"], sm[s]["rs"], sm[s]["nm"]
                for q4 in range(4):
                    fw.op("dve", lambda q4=q4: nc.vector.bn_stats(out=st[:, q4 * 6:(q4 + 1) * 6], in_=xn[:, q4 * 512:(q4 + 1) * 512]), [xn], [st])
                yield
                fw.op("dve", lambda: nc.vector.bn_aggr(out=mv[:], in_=st[:]), [st], [mv])
                yield
                fw.op("dve", lambda: nc.vector.tensor_scalar(out=rs[:], in0=mv[:, 1:2], scalar1=LN_EPS, scalar2=None, op0=ALU.add), [mv], [rs])
                yield
                fw.op("pool", lambda: nc.gpsimd.tensor_tensor(out=rs[:], in0=rs[:], in1=mhalf[:], op=ALU.pow), [rs, mhalf], [rs])
                yield
                fw.op("dve", lambda: nc.vector.tensor_scalar(out=nm[:], in0=mv[:, 0:1], scalar1=rs[:], scalar2=-1.0,
                                                            op0=ALU.mult, op1=ALU.mult), [mv, rs], [nm])
                yield
                fw.op("act", lambda: nc.scalar.activation(out=xn[:], in_=xn[:], func=AF.Identity, bias=nm[:], scale=rs[:]), [xn, nm, rs], [xn])
                yield
                pb0 = 4 * (n % 2)
                for half in range(2):
                    for k4 in range(2):
                        bank = pb0 + 2 * half + k4
                        for k in range(4):
                            kc = half * 8 + k4 * 4 + k
                            fw.op("pe", lambda kc=kc, k=k, bank=bank: nc.tensor.transpose(out=ps_t[:, bank, k * 128:(k + 1) * 128],
                                                                                         in_=xn[:, kc * 128:(kc + 1) * 128], identity=ident_f[:]),
                                  [xn, ident_f], [PB[bank]], signal=(k == 3))
                        yield
                        for k in range(4):
                            kc = half * 8 + k4 * 4 + k
                            if k % 2 == 0:
                                fw.op("dve", lambda kc=kc, k=k, bank=bank: nc.vector.tensor_scalar(
                                    out=hT[s][:, kc, :], in0=ps_t[:, bank, k * 128:(k + 1) * 128], scalar1=cp(CP_EG + kc), scalar2=cp(CP_EB + kc),
                                    op0=ALU.mult, op1=ALU.add), [PB[bank], colp_t], [hT[s]])
                            else:
                                fw.op("act", lambda kc=kc, k=k, bank=bank: nc.scalar.activation(
                                    out=hT[s][:, kc, :], in_=ps_t[:, bank, k * 128:(k + 1) * 128], func=AF.Identity, bias=cp(CP_EB + kc), scale=cp(CP_EG + kc)),
                                    [PB[bank], colp_t], [hT[s]])
                        yield
                fw.dma("sp", S[sg]["h0T"][i], hT[s][:], [hT[s]], [], hT[s].ds)
                yield
                if isE:
                    fw.op("pool", lambda: nc.gpsimd.tensor_tensor(out=h0s[s][:], in0=xn[:], in1=g_bc[:], op=ALU.mult), [xn, g_bc], [h0s[s]])
                    yield
                    fw.op("pool", lambda: nc.gpsimd.tensor_tensor(out=h0s[s][:], in0=h0s[s][:], in1=b_bc[:], op=ALU.add), [h0s[s], b_bc], [h0s[s]])
                    fw.dma("sp", S[sg]["h0s"][i], h0s[s][:], [h0s[s]], [], h0s[s].ds)
                    yield

            pipeline([(lambda n=n: p1_tile(n)) for n in range(len(tiles))], lag=6, width=(W1 if '1' in PIPE_PH else 1), loads=p1_load, pf=2)
            fw.barrier()

        with ExitStack() as es:
            bq_bc = fw.tile(es, [128, 3072], F32, "bqkv", dma=True)
            fw.dma("sp", bq_bc[:], b_qkv.partition_broadcast(128), [], [bq_bc], bq_bc.ds)
            hT = [fw.tile(es, [128, 16, 128], BF16, "p2hT%d" % i, dma=True) for i in range(4)]
            rp = [fw.tile(es, [128, 256], F32, "rp%d" % i, dma=True) for i in range(4)]
            m8 = [fw.tile(es, [128, 8], F32, "m8_%d" % i, dma=True) for i in range(4)]
            qf = [fw.tile(es, [128, 1024], F32, "qf%d" % i) for i in range(2)]
            kf = [fw.tile(es, [128, 1024], F32, "kf%d" % i) for i in range(2)]
            vf = [fw.tile(es, [128, 1024], F32, "vf%d" % i) for i in range(2)]
            tq = [fw.tile(es, [128, 4, 16, 8], F32, "tq%d" % i) for i in range(2)]
            tk = [fw.tile(es, [128, 4, 16, 8], F32, "tk%d" % i) for i in range(2)]
            qb = [fw.tile(es, [128, 1024], BF16, "qb%d" % i) for i in range(2)]
            kb = [fw.tile(es, [128, 1024], BF16, "kb%d" % i) for i in range(2)]
            qT = [fw.tile(es, [128, 8, 128], BF16, "qTt%d" % i, dma=True) for i in range(2)]
            kT = [fw.tile(es, [128, 8, 128], BF16, "kTt%d" % i, dma=True) for i in range(2)]
            va = [fw.tile(es, [128, 8, 129], BF16, "va%d" % i, dma=True) for i in range(2)]
            tiles = [(sg, i, i < ne) for sg, ne, no in segs for i in range(ne + no)]

            def rope(eng, x, tmp, rps):
                e = fw.eng[eng]
                xv = x[:].rearrange("p (g d) -> p g d", d=64)
                x1, x2 = xv[:, :, 0:8], xv[:, :, 8:16]
                C = rps[:, 0:128].rearrange("p (g d) -> p g d", d=8)
                Sn = rps[:, 128:256].rearrange("p (g d) -> p g d", d=8)
                fw.op(eng, lambda: e.tensor_tensor(out=tmp[:, 0], in0=x1, in1=C, op=ALU.mult), [x, rps], [tmp])
                fw.op(eng, lambda: e.tensor_tensor(out=tmp[:, 1], in0=x2, in1=Sn, op=ALU.mult), [x, rps], [tmp])
                fw.op(eng, lambda: e.tensor_tensor(out=tmp[:, 2], in0=x2, in1=C, op=ALU.mult), [x, rps], [tmp])
                fw.op(eng, lambda: e.tensor_tensor(out=tmp[:, 3], in0=x1, in1=Sn, op=ALU.mult), [x, rps], [tmp])
                yield
                fw.op(eng, lambda: e.tensor_tensor(out=x1, in0=tmp[:, 0], in1=tmp[:, 1], op=ALU.subtract), [tmp], [x])
                fw.op(eng, lambda: e.tensor_tensor(out=x2, in0=tmp[:, 2], in1=tmp[:, 3], op=ALU.add), [tmp], [x])
                yield

            def p2_load(n):
                sg, i, isE = tiles[n]
                l = n % 4
                h_t, r_t, m_t = hT[l], rp[l], m8[l]
                fw.dma("sp", h_t[:], S[sg]["h0T"][i], [], [h_t], h_t.ds)
                fw.dma("sp", r_t[:], ROPE[sg][i * 128:(i + 1) * 128, :], [], [r_t], r_t.ds)
                fw.dma("sp", m_t[:], MASK8[sg][i * 128:(i + 1) * 128, :], [], [m_t], m_t.ds)

            def p2_tile(n):
                sg, i, isE = tiles[n]
                s = n % 2
                l = n % 4
                h_t, r_t, m_t = hT[l], rp[l], m8[l]
                groups = [2, 3, 4, 5, 0, 1] if isE else [2, 3, 4, 5]
                for g in groups:
                    for kc in range(16):
                        fw.op("pe", lambda g=g, kc=kc: nc.tensor.matmul(ps(g), lhsT=h_t[:, kc, :], rhs=wq[:, kc, g * 512:(g + 1) * 512],
                                                                        start=(kc == 0), stop=(kc == 15)),
                              [h_t, wq], [PB[g]], signal=(kc == 15))
                    yield
                    if g == 3:
                        fw.op("dve", lambda: nc.vector.tensor_tensor(out=kf[s][:].rearrange("p (a b) -> p a b", a=2), in0=ps(2, nb=2),
                                                                    in1=bq_bc[:, 1024:2048].rearrange("p (a b) -> p a b", a=2), op=ALU.add),
                              [PB[2], PB[3], bq_bc], [kf[s]])
                    if g == 5:
                        fw.op("dve", lambda: nc.vector.tensor_tensor(out=vf[s][:].rearrange("p (a b) -> p a b", a=2), in0=ps(4, nb=2),
                                                                    in1=bq_bc[:, 2048:3072].rearrange("p (a b) -> p a b", a=2), op=ALU.add),
                              [PB[4], PB[5], bq_bc], [vf[s]])
                    if g == 1:
                        fw.op("dve", lambda: nc.vector.tensor_tensor(out=qf[s][:].rearrange("p (a b) -> p a b", a=2), in0=ps(0, nb=2),
                                                                    in1=bq_bc[:, 0:1024].rearrange("p (a b) -> p a b", a=2), op=ALU.add),
                              [PB[0], PB[1], bq_bc], [qf[s]])
                fw.op("pool", lambda: nc.gpsimd.tensor_scalar(out=va[s][:, :, 0:128], in0=vf[s][:].rearrange("p (h e) -> p h e", e=128),
                                                             scalar1=m_t[:, 0:1], scalar2=1.0, op0=ALU.mult, op1=ALU.mult), [vf[s], m_t], [va[s]])
                fw.op("pool", lambda: nc.gpsimd.tensor_copy(out=va[s][:, :, 128:129], in_=m_t[:].unsqueeze(2)), [m_t], [va[s]])
                fw.dma("sp", S[sg]["v"][:, :, i, :].rearrange("h p e -> p h e"), va[s][:], [va[s]], [], va[s].ds)
                yield
                yield from rope("pool", kf[s], tk[s], r_t)
                fw.op("act", lambda: nc.scalar.copy(out=kb[s][:], in_=kf[s][:]), [kf[s]], [kb[s]])
                yield
                if isE:
                    yield from rope("dve", qf[s], tq[s], r_t)
                    fw.op("act", lambda: nc.scalar.activation(out=qb[s][:], in_=qf[s][:], func=AF.Copy, scale=HD ** -0.5), [qf[s]], [qb[s]])
                    yield
                for (src, dstT, pb, dkey, on) in ((kb[s], kT[s], 6, "kT", True), (qb[s], qT[s], 7, "qT", isE)):
                    if not on:
                        continue
                    for h in range(8):
                        fw.op("pe", lambda h=h, src=src, pb=pb: nc.tensor.transpose(out=ps_bf[:, pb, h * 128:(h + 1) * 128],
                                                                                   in_=src[:, h * 128:(h + 1) * 128], identity=ident_b[:]),
                              [src, ident_b], [PB[pb]], signal=(h == 7))
                    yield
                    fw.op("act", lambda dstT=dstT, pb=pb: nc.scalar.copy(out=dstT[:].rearrange("p a b -> p (a b)"), in_=ps_bf[:, pb, :]),
                          [PB[pb]], [dstT])
                    fw.dma("sp", S[sg][dkey][:, :, i * 128:(i + 1) * 128].rearrange("h p t -> p h t"), dstT[:], [dstT], [], dstT.ds)
                    yield

            pipeline([(lambda n=n: p2_tile(n)) for n in range(len(tiles))], lag=8, width=(None if '2' in PIPE_PH else 1), loads=p2_load, pf=2)
            fw.barrier()

        es12.close()
        es34 = ExitStack()
        dg = fw.tile(es34, [128, 8, 31, 128], BF16, "dg")
        onesm = fw.tile(es34, [128, 128], F32, "onesm")
        with ExitStack() as es:
            wu = load_w(es, "wu", w_in, 0, 16, 3072, 2048)
            fw.op("pool", lambda: nc.gpsimd.memset(onesm[:], 1.0 / D_CONV), [], [onesm])
            for j in range(8):
                for tp in range(31):
                    fw.op("pool", lambda j=j, tp=tp: nc.gpsimd.tensor_scalar(out=dg[:, j, tp, :], in0=ident_f[:], scalar1=cp(CP_CW + j * 31 + tp),
                                                                            scalar2=1.0, op0=ALU.mult, op1=ALU.mult), [ident_f, colp_t], [dg])
            hb = [fw.tile(es, [128, 4, 16, 128], BF16, "p3h%d" % i, dma=True) for i in range(2)]
            mr = [fw.tile(es, [128, 512], F32, "p3m%d" % i, dma=True) for i in range(2)]
            sig = [fw.tile(es, [128, 512], F32, "sig%d" % i) for i in range(2)]
            cf = [fw.tile(es, [128, 512], F32, "cf%d" % i) for i in range(2)]
            ct = [fw.tile(es, [128, 512], BF16, "ct%d" % i, dma=True) for i in range(3)]
            blks = [(sg, t0, n, ne) for sg, ne, no in segs for (t0, n) in blocks512(ne)]

            def p3_load(bi):
                sg, t0, n, ne = blks[bi]
                s = bi % 2
                fw.dma("sp", hb[s][:, 0:n], S[sg]["h0T"][t0 // 128:t0 // 128 + n].rearrange("t p k c -> p t k c"), [], [hb[s]], hb[s].ds)

            p3_load(0)
            it = 0
            for bi, (sg, t0, n, ne) in enumerate(blks):
                if bi + 1 < len(blks):
                    p3_load(bi + 1)
                s = bi % 2
                N = n * 128
                Etot = ne * 128
                edge = (t0 < 16) or (t0 + N > cfg.vhi[sg])
                if edge:
                    fw.dma("sp", mr[s][:, 0:N], MROW[sg][0:1, t0:t0 + N].partition_broadcast(128), [], [mr[s]], mr[s].ds)
                for j in range(8):
                    ba, bg = (it % 2) * 2, (it % 2) * 2 + 1
                    for part, bank in ((j, ba), (8 + j, bg)):
                        for kc in range(16):
                            fw.op("pe", lambda part=part, bank=bank, kc=kc: nc.tensor.matmul(
                                ps_t[:, bank, 0:N].rearrange("p (t c) -> p t c", c=128), lhsT=wu[:, kc, part * 128:(part + 1) * 128],
                                rhs=hb[s][:, 0:n, kc, :], start=(kc == 0), stop=(kc == 15)),
                                [wu, hb[s]], [PB[bank]], signal=(kc == 15))
                    sg_t, cf_t, ct_t = sig[it % 2], cf[it % 2], ct[it % 3]
                    fw.op("act", lambda: nc.scalar.activation(out=sg_t[:, 0:N], in_=ps(bg, N), func=AF.Sigmoid, bias=cp(CP_BU + 8 + j), scale=1.0),
                          [PB[bg], colp_t], [sg_t])
                    dst = cf_t if edge else ct_t
                    fw.op("dve", lambda: nc.vector.scalar_tensor_tensor(out=dst[:, 0:N], in0=ps(ba, N), scalar=cp(CP_BU + j), in1=sg_t[:, 0:N],
                                                                        op0=ALU.add, op1=ALU.mult), [PB[ba], colp_t, sg_t], [dst])
                    if edge:
                        fw.op("dve", lambda: nc.vector.tensor_tensor(out=ct_t[:, 0:N], in0=cf_t[:, 0:N], in1=mr[s][:, 0:N], op=ALU.mult),
                              [cf_t, mr[s]], [ct_t])
                    fw.dma("sp", S[sg]["c"][j, :, t0:t0 + N], ct_t[:, 0:N], [ct_t], [], ct_t.ds)
                    it += 1
            fw.barrier()

        with ExitStack() as es:
            cb = [fw.tile(es, [128, 542], BF16, "cb%d" % i, dma=True) for i in range(3)]
            xcs = [fw.tile(es, [128, 8, 512], F32, "xc%d" % i) for i in range(2)]
            xqs = [fw.tile(es, [128, 8, 512], F32, "xq%d" % i) for i in range(2)]
            msq = fw.tile(es, [128, 512], F32, "msq")
            var = fw.tile(es, [128, 512], F32, "var")
            tt = [fw.tile(es, [128, 512], F32, "tt%d" % i) for i in range(2)]
            co = [fw.tile(es, [128, 512], BF16, "co%d" % i, dma=True) for i in range(2)]
            blks = [(sg, bi, t0, n, ne) for sg, ne, no in segs for bi, (t0, n) in enumerate(blocks512(ne))]
            it = 0
            for bn, (sg, bidx, t0, n, ne) in enumerate(blks):
                N = n * 128
                Etot = ne * 128
                xc, xq = xcs[bn % 2], xqs[bn % 2]
                bm, bv = 2 + 2 * (bn % 2), 3 + 2 * (bn % 2)
                for j in range(8):
                    c_t = cb[it % 3]
                    lo, hi = t0 - 15, t0 + N + 15
                    clo, chi = max(lo, 0), min(hi, Etot)
                    if clo > lo:
                        fw.op("dve", lambda: nc.vector.memset(c_t[:, 0:clo - lo], 0.0), [], [c_t])
                    if chi < hi:
                        fw.op("dve", lambda: nc.vector.memset(c_t[:, chi - lo:hi - lo], 0.0), [], [c_t])
                    fw.dma("sp", c_t[:, clo - lo:chi - lo], S[sg]["c"][j, :, clo:chi], [], [c_t], c_t.ds)
                    bank = it % 2
                    for tp in range(31):
                        fw.op("pe", lambda tp=tp: nc.tensor.matmul(ps(bank, N), lhsT=dg[:, j, tp, :], rhs=c_t[:, tp:tp + N],
                                                                   start=(tp == 0), stop=(tp == 30)), [dg, c_t], [PB[bank]], signal=(tp == 30))
                    fw.op("act", lambda: nc.scalar.activation(out=xc[:, j, 0:N], in_=ps(bank, N), func=AF.Identity, bias=cp(CP_C3 + j * 3), scale=1.0),
                          [PB[bank], colp_t], [xc])
                    fw.op("act", lambda: nc.scalar.activation(out=xq[:, j, 0:N], in_=ps(bank, N), func=AF.Square, bias=cp(CP_C3 + j * 3), scale=1.0),
                          [PB[bank], colp_t], [xq])
                    it += 1
                for j in range(8):
                    fw.op("pe", lambda: nc.tensor.matmul(ps(bm, N), lhsT=onesm[:], rhs=xc[:, j, 0:N], start=(j == 0), stop=(j == 7)),
                          [onesm, xc], [PB[bm]], signal=(j == 7))
                for j in range(8):
                    fw.op("pe", lambda: nc.tensor.matmul(ps(bv, N), lhsT=onesm[:], rhs=xq[:, j, 0:N], start=(j == 0), stop=(j == 7)),
                          [onesm, xq], [PB[bv]], signal=(j == 7))
                fw.op("act", lambda: nc.scalar.activation(out=msq[:, 0:N], in_=ps(bm, N), func=AF.Square), [PB[bm]], [msq])
                fw.op("dve", lambda: nc.vector.tensor_tensor(out=var[:, 0:N], in0=ps(bv, N), in1=msq[:, 0:N], op=ALU.subtract), [PB[bv], msq], [var])
                fw.op("act", lambda: nc.scalar.activation(out=var[:, 0:N], in_=var[:, 0:N], func=AF.Sqrt, bias=epsc[:], scale=1.0), [var, epsc], [var])
                fw.op("dve", lambda: nc.vector.reciprocal(out=var[:, 0:N], in_=var[:, 0:N]), [var], [var])
                for j in range(8):
                    t_t, o_t = tt[j % 2], co[j % 2]
                    fw.op("dve", lambda: nc.vector.tensor_tensor(out=t_t[:, 0:N], in0=xc[:, j, 0:N], in1=ps(bm, N), op=ALU.subtract), [xc, PB[bm]], [t_t])
                    fw.op("dve", lambda: nc.vector.tensor_tensor(out=t_t[:, 0:N], in0=t_t[:, 0:N], in1=var[:, 0:N], op=ALU.mult), [t_t, var], [t_t])
                    fw.op("act", lambda: nc.scalar.activation(out=o_t[:, 0:N], in_=t_t[:, 0:N], func=AF.Silu, bias=cp(CP_C3 + j * 3 + 2),
                                                              scale=cp(CP_C3 + j * 3 + 1)), [t_t, colp_t], [o_t])
                    fw.dma("sp", S[sg]["mixT"][bidx, :, 8 + j, 0:N], o_t[:, 0:N], [o_t], [], o_t.ds)
            fw.barrier()

        es34.close()
        es56 = ExitStack()
        wo = load_w(es56, "wo", w_out, 0, 16, 0, 2048)
        with ExitStack() as es:
            zt = fw.tile(es, [128, NFC, 96], BF16, "zt", dma=True)
            fw.op("pool", lambda: nc.gpsimd.memset(zt[:], 0.0), [], [zt])
            for sg, ne, no in segs:
                fw.dma("sp", S[sg]["fT"][:, :, 0:32], zt[:, :, 0:32], [zt], [], zt.ds)
                r = cfg.rhi[sg]
                while r < ne * 128:
                    w_ = min(96, ne * 128 - r)
                    fw.dma("sp", S[sg]["fT"][:, :, r:r + w_], zt[:, :, 0:w_], [zt], [], zt.ds)
                    r += w_
            LKmax = max(ne + no for _, ne, no in segs)
            kTh = [fw.tile(es, [128, LKmax * 128], BF16, "kTh%d" % i, dma=True) for i in range(2)]
            vh = [fw.tile(es, [128, LKmax, 129], BF16, "vh%d" % i, dma=True) for i in range(2)]
            qblk = [fw.tile(es, [128, 512], BF16, "qblk%d" % i, dma=True) for i in range(3)]
            NP = 3
            P = [fw.tile(es, [128, 2, 512], BF16, "P%d" % i) for i in range(NP)]
            accS = [fw.tile(es, [128, 3, 396], F32, "accS%d" % i) for i in range(2)]
            rr = [fw.tile(es, [128, 4], F32, "rr%d" % i) for i in range(2)]
            ot = [fw.tile(es, [128, 128], F32, "ot%d" % i) for i in range(2)]
            osq = [fw.tile(es, [128, 128], F32, "osq%d" % i) for i in range(2)]
            ab = [fw.tile(es, [128, 128], BF16, "ab%d" % i) for i in range(2)]
            aT = [fw.tile(es, [128, 512], BF16, "aT%d" % i, dma=True) for i in range(2)]

            def acc_loc(c, j):
                idx = c * 4 + j
                return idx // 3, (idx % 3) * 132

            def epilogue(sg, h, t0, n, a_sb, a_s, ep_n):
                N = n * 128
                for j in range(n):
                    e2 = (ep_n * 4 + j) % 2
                    bk0, o0 = acc_loc(0, j)
                    bk1, o1 = acc_loc(1, j)
                    r_t, o_t, q_t, b_t = rr[e2], ot[e2], osq[e2], ab[e2]
                    fw.op("dve", lambda: nc.vector.reciprocal(out=r_t[:, 0:1], in_=a_sb[:, bk0, o0 + 128:o0 + 129]), [a_sb], [r_t])
                    fw.op("dve", lambda: nc.vector.reciprocal(out=r_t[:, 1:2], in_=a_sb[:, bk1, o1 + 128:o1 + 129]), [a_sb], [r_t])
                    yield
                    fw.op("dve", lambda: nc.vector.tensor_tensor(out=r_t[:, 1:2], in0=r_t[:, 1:2], in1=neglam[:], op=ALU.mult), [r_t, neglam], [r_t])
                    yield
                    fw.op("dve", lambda: nc.vector.tensor_scalar(out=o_t[:], in0=a_sb[:, bk0, o0:o0 + 128], scalar1=r_t[:, 0:1], scalar2=None, op0=ALU.mult),
                          [a_sb, r_t], [o_t])
                    yield
                    fw.op("dve", lambda: nc.vector.scalar_tensor_tensor(out=o_t[:], in0=a_sb[:, bk1, o1:o1 + 128], scalar=r_t[:, 1:2], in1=o_t[:],
                                                                        op0=ALU.mult, op1=ALU.add), [a_sb, r_t, o_t], [o_t])
                    yield
                    fw.op("dve", lambda: nc.vector.tensor_tensor(out=q_t[:], in0=o_t[:], in1=o_t[:], op=ALU.mult), [o_t], [q_t])
                    yield
                    fw.op("dve", lambda: nc.vector.tensor_reduce(out=r_t[:, 2:3], in_=q_t[:], axis=AX.X, op=ALU.add), [q_t], [r_t])
                    yield
                    fw.op("dve", lambda: nc.vector.tensor_scalar(out=r_t[:, 2:3], in0=r_t[:, 2:3], scalar1=1.0 / 128, scalar2=LN_EPS,
                                                                op0=ALU.mult, op1=ALU.add), [r_t], [r_t])
                    yield
                    fw.op("pool", lambda: nc.gpsimd.tensor_tensor(out=r_t[:, 3:4], in0=r_t[:, 2:3], in1=mhalf[:], op=ALU.pow), [r_t, mhalf], [r_t])
                    yield
                    fw.op("dve", lambda: nc.vector.scalar_tensor_tensor(out=b_t[:], in0=o_t[:], scalar=r_t[:, 3:4], in1=gsub[:],
                                                                        op0=ALU.mult, op1=ALU.mult), [o_t, r_t, gsub], [b_t])
                    yield
                    yield
                    fw.op("pe", lambda: nc.tensor.transpose(out=ps_bf[:, 7, j * 128:(j + 1) * 128], in_=b_t[:], identity=ident_b[:]),
                          [b_t, ident_b], [PB[7]])
                    yield
                fw.op("dve", lambda: nc.vector.tensor_copy(out=a_s[:, 0:N], in_=ps_bf[:, 7, 0:N]), [PB[7]], [a_s])
                fw.dma("sp", S[sg]["mixT"][t0 // 512, :, h, 0:N], a_s[:, 0:N], [a_s], [], a_s.ds)
                yield

            hi_n = 0
            qi_n = 0
            ep_n = 0
            pi_n = 0
            epi = None
            for sg, ne, no in segs:
                nt = ne + no
                for h in range(NH):
                    hs = hi_n % 2
                    hi_n += 1
                    fw.dma("sp", kTh[hs][:, 0:nt * 128], S[sg]["kT"][h], [], [kTh[hs]], kTh[hs].ds)
                    fw.dma("sp", vh[hs][:, 0:nt, :], S[sg]["v"][h], [], [vh[hs]], vh[hs].ds)
                    for (t0, n) in blocks512(ne):
                        N = n * 128
                        qs = qi_n % 3
                        qi_n += 1
                        fw.dma("sp", qblk[qs][:, 0:N], S[sg]["qT"][h, :, t0:t0 + N], [], [qblk[qs]], qblk[qs].ds)

                        def scores(kt):
                            sl = kt % 2
                            for c in range(2):
                                fw.op("pe", lambda c=c: nc.tensor.matmul(ps(2 * sl + c, N), lhsT=kTh[hs][c * 64:(c + 1) * 64, kt * 128:(kt + 1) * 128],
                                                                         rhs=qblk[qs][c * 64:(c + 1) * 64, 0:N], start=True, stop=True),
                                      [kTh[hs], qblk[qs]], [PB[2 * sl + c]], signal=(c == 1))

                        scores(0)
                        started = set()
                        for kt in range(nt):
                            if kt + 1 < nt:
                                scores(kt + 1)
                            sl = kt % 2
                            p_t = P[pi_n % NP]
                            pi_n += 1
                            fw.op("act", lambda: nc.scalar.activation(out=p_t[:, :, 0:N], in_=ps_t[:, 2 * sl:2 * sl + 2, 0:N], func=AF.Exp),
                                  [PB[2 * sl], PB[2 * sl + 1]], [p_t])
                            for c in range(2):
                                for j in range(n):
                                    bk, o = acc_loc(c, j)
                                    b = 4 + bk
                                    first = b not in started
                                    started.add(b)
                                    last = (c == 1 and j == n - 1)
                                    fw.op("pe", lambda: nc.tensor.matmul(ps_t[:, b, o:o + 129], lhsT=p_t[:, c, j * 128:(j + 1) * 128], rhs=vh[hs][:, kt, :],
                                                                         start=first, stop=(kt == nt - 1), skip_group_check=True),
                                          [p_t, vh[hs]], [PB[b]], signal=last)
                            if epi is not None and EPI_INTERLEAVE:
                                for _ in range(2):
                                    if next(epi, "end") == "end":
                                        epi = None
                                        break
                        if epi is not None:
                            for _ in epi:
                                pass
                            epi = None
                        a_sb = accS[ep_n % 2]
                        used = sorted({acc_loc(c, j)[0] for c in range(2) for j in range(n)})
                        for bk in used:
                            fw.op("dve", lambda bk=bk: nc.vector.tensor_copy(out=a_sb[:, bk, :], in_=ps_t[:, 4 + bk, 0:396]), [PB[4 + bk]], [a_sb])
                        epi = epilogue(sg, h, t0, n, a_sb, aT[ep_n % 2], ep_n)
                        ep_n += 1
            if epi is not None:
                for _ in epi:
                    pass
            fw.barrier()

        with ExitStack() as es:
            g_bc = bcast_row(es, "ln1g", ROWS["ln1_g"])
            b_bc = bcast_row(es, "ln1b", ROWS["ln1_b"])
            bd_bc = bcast_row(es, "bdn", ROWS["b_down"])
            mx = [fw.tile(es, [128, 16, 512], BF16, "mx%d" % i, dma=True) for i in range(3)]
            hs_ = [fw.tile(es, [128, D], F32, "p6hs%d" % i, dma=True) for i in range(4)]
            h1b = [fw.tile(es, [128, D], BF16, "h1b%d" % i) for i in range(2)]
            hT = [fw.tile(es, [128, 16, 128], BF16, "p6hT%d" % i, dma=True) for i in range(2)]
            lb = [ln_bufs(es, "b%d" % i) for i in range(2)]
            blks = [(sg, bi, t0, n) for sg, ne, no in segs for bi, (t0, n) in enumerate(blocks512(ne))]
            work = [(bi, tl) for bi, (sg, bidx, t0, n) in enumerate(blks) for tl in range(n)]

            def p6_load(wn):
                bi, tl = work[wn]
                sg, bidx, t0, n = blks[bi]
                m_t = mx[bi % 3]
                if tl == 0:
                    fw.dma("sp", m_t[:, :, 0:n * 128], S[sg]["mixT"][bidx, :, :, 0:n * 128], [], [m_t], m_t.ds)
                hs_t = hs_[wn % 4]
                fw.dma("sp", hs_t[:], S[sg]["h0s"][t0 // 128 + tl], [], [hs_t], hs_t.ds)

            def p6_tile(wn):
                bi, tl = work[wn]
                sg, bidx, t0, n = blks[bi]
                m_t = mx[bi % 3]
                s = wn % 2
                ti = t0 // 128 + tl
                hs_t = hs_[wn % 4]
                for nb in range(4):
                    for kc in range(16):
                        fw.op("pe", lambda nb=nb, kc=kc: nc.tensor.matmul(ps(nb), lhsT=m_t[:, kc, tl * 128:(tl + 1) * 128],
                                                                          rhs=wo[:, kc, nb * 512:(nb + 1) * 512], start=(kc == 0), stop=(kc == 15)),
                              [m_t, wo], [PB[nb]], signal=(kc == 15))
                    yield
                    if nb % 2 == 1:
                        hf = nb // 2
                        fw.op("dve", lambda hf=hf: nc.vector.tensor_tensor(out=hs_t[:, hf * 1024:(hf + 1) * 1024].rearrange("p (a b) -> p a b", a=2),
                                                                          in0=ps(2 * hf, nb=2),
                                                                          in1=hs_t[:, hf * 1024:(hf + 1) * 1024].rearrange("p (a b) -> p a b", a=2),
                                                                          op=ALU.add), [PB[2 * hf], PB[2 * hf + 1], hs_t], [hs_t])
                h1_t = lb[s]["xn"]
                yield from layer_norm_tm(es, hs_t[:], [hs_t], lb[s], g_bc, b_bc, h1_t, mul_eng="dve")
                fw.op("act", lambda: nc.scalar.copy(out=h1b[s][:], in_=h1_t[:]), [h1_t], [h1b[s]])
                yield
                fw.op("dve", lambda: nc.vector.scalar_tensor_tensor(out=hs_t[:], in0=h1_t[:], scalar=ALPHA, in1=bd_bc[:],
                                                                    op0=ALU.mult, op1=ALU.add), [h1_t, bd_bc], [hs_t])
                fw.dma("sp", S[sg]["h1s"][ti], hs_t[:], [hs_t], [], hs_t.ds)
                yield
                yield from transpose16(h1b[s], [h1b[s]], hT[s], [4 + 2 * s, 5 + 2 * s], "act")
                fw.dma("sp", S[sg]["h1T"][:, :, ti * 128:(ti + 1) * 128], hT[s][:], [hT[s]], [], hT[s].ds)
                yield

            pipeline([(lambda wn=wn: p6_tile(wn)) for wn in range(len(work))], lag=7, width=(None if '6' in PIPE_PH else 1), loads=p6_load, pf=2)
            fw.barrier()

        es56.close()
        es78 = ExitStack()
        wd = fw.tile(es78, [128, NFC, 1024], BF16, "wdn", dma=True)
        with ExitStack() as es:
            NG = 11
            CPG = NFC // NG
            wgs = [fw.tile(es, [128, 16, 2 * CPG * 128], BF16, "wup%d" % i, dma=True) for i in range(2)]

            def p7_wload(g):
                load_w(es, "wup", w_up, 0, 16, g * CPG * 128, CPG * 128, dst=wgs[g % 2], dstc0=0)
                load_w(es, "wup", w_up, 0, 16, D_FF + g * CPG * 128, CPG * 128, dst=wgs[g % 2], dstc0=CPG * 128)

            p7_wload(0)
            hb = [fw.tile(es, [128, 16, 512], BF16, "p7h%d" % i, dma=True) for i in range(2)]
            mr = [fw.tile(es, [128, 512], F32, "p7m%d" % i, dma=True) for i in range(1)] * 2
            ag = [fw.tile(es, [128, 512], F32, "ag%d" % i) for i in range(2)]
            au = [fw.tile(es, [128, 512], F32, "au%d" % i) for i in range(2)]
            sgt = [fw.tile(es, [128, 512], F32, "sgt%d" % i) for i in range(2)]
            fo = [fw.tile(es, [128, 512], BF16, "fo%d" % i, dma=True) for i in range(3)]
            ccn = fw.tile(es, [128, 88], F32, "ccn")
            fv = colp_t[:, CP_F:CP_F + 440].rearrange("p (c k) -> p c k", k=5)
            fw.op("dve", lambda: nc.vector.tensor_tensor(out=ccn[:].unsqueeze(2), in0=fv[:, :, 1:2], in1=fv[:, :, 2:3], op=ALU.add), [colp_t], [ccn])
            fw.op("dve", lambda: nc.vector.tensor_tensor(out=ccn[:].unsqueeze(2), in0=ccn[:].unsqueeze(2), in1=fv[:, :, 3:4], op=ALU.add), [colp_t, ccn], [ccn])
            fw.op("dve", lambda: nc.vector.tensor_tensor(out=ccn[:].unsqueeze(2), in0=ccn[:].unsqueeze(2), in1=fv[:, :, 0:1], op=ALU.mult), [colp_t, ccn], [ccn])
            fw.op("dve", lambda: nc.vector.tensor_tensor(out=ccn[:].unsqueeze(2), in0=ccn[:].unsqueeze(2), in1=fv[:, :, 4:5], op=ALU.add), [colp_t, ccn], [ccn])

            def fcp(c, k):
                return colp_t[:, CP_F + c * 5 + k:CP_F + c * 5 + k + 1]

            it = 0
            ld = 0
            for g in range(NG):
                wg = wgs[g % 2]
                if g + 1 < NG:
                    p7_wload(g + 1)
                else:
                    load_w(es78, "wdn", w_down, 0, NFC, 0, 1024, dst=wd)
                blks = [(sg, s0, n, ne) for sg, ne, no in segs for (s0, n) in ffn_blocks(cfg.rhi[sg])]

                def p7_load(bi, ld):
                    sg, s0, n, ne = blks[bi]
                    fw.dma("sp", hb[ld % 2][:, :, 0:n], S[sg]["h1T"][:, :, s0:s0 + n], [], [hb[ld % 2]], hb[ld % 2].ds)

                p7_load(0, ld)
                for bi, (sg, s0, n, ne) in enumerate(blks):
                    if bi + 1 < len(blks):
                        p7_load(bi + 1, ld + 1)
                    h_t = hb[ld % 2]
                    m_t = mr[ld % 2]
                    ld += 1
                    Etot = ne * 128
                    edge = (s0 < 16) or (s0 + n > cfg.vhi[sg])
                    M = n - 2
                    if edge:
                        fw.dma("sp", m_t[:, 0:n], MROW[sg][0:1, s0:s0 + n].partition_broadcast(128), [], [m_t], m_t.ds)
                    for jj in range(CPG):
                        cg = g * CPG + jj
                        cu = NFC + cg
                        bg_, bu_ = (it % 2) * 2, (it % 2) * 2 + 1
                        for (col0, bank) in ((jj * 128, bg_), (CPG * 128 + jj * 128, bu_)):
                            for kc in range(16):
                                fw.op("pe", lambda col0=col0, bank=bank, kc=kc: nc.tensor.matmul(ps(bank, n), lhsT=wg[:, kc, col0:col0 + 128],
                                                                                                 rhs=h_t[:, kc, 0:n], start=(kc == 0), stop=(kc == 15)),
                                      [wg, h_t], [PB[bank]], signal=(kc == 15))
                        a_g, a_u, s_t, f_t = ag[it % 2], au[it % 2], sgt[it % 2], fo[it % 3]
                        for (cc_, bank, acc) in ((cg, bg_, a_g), (cu, bu_, a_u)):
                            if edge:
                                u_t = sgt[it % 2]
                                fw.op("dve", lambda: nc.vector.scalar_tensor_tensor(out=u_t[:, 0:n], in0=ps(bank, n), scalar=fcp(cc_, 0), in1=m_t[:, 0:n],
                                                                                    op0=ALU.add, op1=ALU.mult), [PB[bank], colp_t, m_t], [u_t])
                                src0, src1, src2 = u_t[:, 0:M], u_t[:, 1:M + 1], u_t[:, 2:M + 2]
                                sb = [u_t]
                            else:
                                pp = ps(bank, n)
                                src0, src1, src2 = pp[:, 0:M], pp[:, 1:M + 1], pp[:, 2:M + 2]
                                sb = [PB[bank]]
                            fw.op("dve", lambda: nc.vector.tensor_scalar(out=acc[:, 0:M], in0=src0, scalar1=fcp(cc_, 1), scalar2=None, op0=ALU.mult),
                                  sb + [colp_t], [acc])
                            fw.op("dve", lambda: nc.vector.scalar_tensor_tensor(out=acc[:, 0:M], in0=src1, scalar=fcp(cc_, 2), in1=acc[:, 0:M],
                                                                                op0=ALU.mult, op1=ALU.add), sb + [colp_t, acc], [acc])
                            fw.op("dve", lambda: nc.vector.scalar_tensor_tensor(out=acc[:, 0:M], in0=src2, scalar=fcp(cc_, 3), in1=acc[:, 0:M],
                                                                                op0=ALU.mult, op1=ALU.add), sb + [colp_t, acc], [acc])
                        bias_g = fcp(cg, 4) if edge else ccn[:, cg:cg + 1]
                        bias_u = fcp(cu, 4) if edge else ccn[:, cu:cu + 1]
                        fw.op("act", lambda: nc.scalar.activation(out=s_t[:, 0:M], in_=a_g[:, 0:M], func=AF.Silu, bias=bias_g, scale=1.0),
                              [a_g, colp_t, ccn], [s_t])
                        fw.op("dve", lambda: nc.vector.scalar_tensor_tensor(out=f_t[:, 0:M], in0=a_u[:, 0:M], scalar=bias_u, in1=s_t[:, 0:M],
                                                                            op0=ALU.add, op1=ALU.mult), [a_u, colp_t, ccn, s_t], [f_t])
                        fw.dma("sp", S[sg]["fT"][:, cg, s0 + 1:s0 + 1 + M], f_t[:, 0:M], [f_t], [], f_t.ds)
                        it += 1
            fw.barrier()

        with ExitStack() as es:
            g_bc = bcast_row(es, "ln2g", ROWS["ln2_g"])
            b_bc = bcast_row(es, "ln2b", ROWS["ln2_b"])
            fb = [fw.tile(es, [128, NFC, 128], BF16, "fb%d" % i, dma=True) for i in range(4)]
            pa = [fw.tile(es, [128, 1024], F32, "pa%d" % i, dma=True) for i in range(3)]
            hs_ = [fw.tile(es, [128, D], F32, "p8hs%d" % i, dma=True) for i in range(3)]
            lb = [ln_bufs(es, "c%d" % i, dma=True) for i in range(2)]
            yo = [lb[i]["xn"] for i in range(2)]
            blks = []
            for sg, ne, no in segs:
                t = 0
                while t < ne:
                    n = min(2, ne - t)
                    blks.append((sg, t, n))
                    t += n
            work = [(bi, tl) for bi, (sg, t, n) in enumerate(blks) for tl in range(n)]
            for half in range(2):
                if half == 1:
                    load_w(es, "wdn", w_down, 0, NFC, half * 1024, 1024, dst=wd)

                def p8_load(wn, half=half):
                    bi, tl = work[wn]
                    sg, t, n = blks[bi]
                    ti = t + tl
                    f_t = fb[wn % 4]
                    fw.dma("sp", f_t[:], S[sg]["fT"][:, :, ti * 128:(ti + 1) * 128], [], [f_t], f_t.ds)
                    if half == 1:
                        fw.dma("sp", hs_[wn % 3][:], S[sg]["h1s"][ti], [], [hs_[wn % 3]], hs_[wn % 3].ds)
                        fw.dma("sp", pa[wn % 3][:], S[sg]["ffa"][ti], [], [pa[wn % 3]], pa[wn % 3].ds)

                def p8_tile(wn, half=half):
                    bi, tl = work[wn]
                    sg, t, n = blks[bi]
                    f_t = fb[wn % 4]
                    s = wn % 2
                    ti = t + tl
                    b0 = 2 * s + (4 if half else 0)
                    hs_t, pa_t = hs_[wn % 3], pa[wn % 3]
                    for nb in range(2):
                        for kc in range(NFC):
                            fw.op("pe", lambda nb=nb, kc=kc: nc.tensor.matmul(ps(b0 + nb), lhsT=f_t[:, kc, :],
                                                                              rhs=wd[:, kc, nb * 512:(nb + 1) * 512], start=(kc == 0), stop=(kc == NFC - 1)),
                                  [f_t, wd], [PB[b0 + nb]], signal=(kc == NFC - 1))
                        yield
                    if half == 0:
                        fw.op("act", lambda: nc.scalar.copy(out=pa_t[:].rearrange("p (a b) -> p a b", a=2), in_=ps(b0, nb=2)),
                              [PB[b0], PB[b0 + 1]], [pa_t])
                        fw.dma("sp", S[sg]["ffa"][ti], pa_t[:], [pa_t], [], pa_t.ds)
                        yield
                    else:
                        fw.op("pool", lambda: nc.gpsimd.tensor_tensor(out=hs_t[:, 0:1024], in0=pa_t[:], in1=hs_t[:, 0:1024], op=ALU.add),
                              [pa_t, hs_t], [hs_t])
                        fw.op("dve", lambda: nc.vector.tensor_tensor(out=hs_t[:, 1024:2048].rearrange("p (a b) -> p a b", a=2), in0=ps(b0, nb=2),
                                                                    in1=hs_t[:, 1024:2048].rearrange("p (a b) -> p a b", a=2), op=ALU.add),
                              [PB[b0], PB[b0 + 1], hs_t], [hs_t])
                        yield
                        yield from layer_norm_tm(es, hs_t[:], [hs_t], lb[s], g_bc, b_bc, yo[s])
                        fw.dma("sp", Y[sg][ti * 128:(ti + 1) * 128, :], yo[s][:], [yo[s]], [], yo[s].ds)
                        yield

                pipeline([(lambda wn=wn: p8_tile(wn)) for wn in range(len(work))], lag=(2 if half == 0 else 4), width=(None if '8' in PIPE_PH else 1), loads=p8_load, pf=1)
                fw.barrier()
        es78.close()
        fw.barrier()
    build.nops = fw.nops
    return nc


def geometry(S_p, S_s):
    L = S_p + N_META
    W = L // 2
    E = -(-(W + 32) // 128) * 128
    O_valid = max(L - (E - 16), S_p // 2 - 16)
    O = -(-O_valid // 128) * 128
    L2 = S_s + N_META
    E2 = -(-(L2 + 32) // 128) * 128
    return dict(L=L, W=W, E=E, O=O, O_valid=O_valid, L2=L2, E2=E2)


def rope_table(pos):
    rot = HD // 4
    inv = ROPE_THETA ** (-np.arange(0, rot, 2, dtype=np.float32) / rot)
    ang = pos.astype(np.float32)[:, None] * inv[None, :]
    cos, sin = np.cos(ang).astype(np.float32), np.sin(ang).astype(np.float32)
    return np.concatenate([np.tile(cos, (1, 16)), np.tile(sin, (1, 16))], axis=1).astype(np.float32)


_CACHE = {}


def kernel(x_prompt, x_sample, meta_tokens, ln_emb_g, ln_emb_b, w_in, b_in, lambda_q1, lambda_k1, lambda_q2, lambda_k2,
           subln_g, conv_w, conv_b, conv_ln_g, conv_ln_b, w_out, b_out, ln1_g, ln1_b, w_up, b_up, ffn_conv_w, ffn_conv_b,
           w_down, b_down, ln2_g, ln2_b, _debug=(), _return_raw=False):
    f = lambda a: np.ascontiguousarray(np.asarray(a, dtype=np.float32))
    x_prompt, x_sample, meta = f(x_prompt), f(x_sample), f(meta_tokens)
    B, S_p, _ = x_prompt.shape
    B2, S_s, _ = x_sample.shape
    assert B * 2 == NCORES and B2 == NCORES
    G = geometry(S_p, S_s)
    L, E, O, L2, E2 = G["L"], G["E"], G["O"], G["L2"], G["E2"]
    cfg = Cfg(E // 128, O // 128, E2 // 128, L2, debug=_debug, half_p=S_p // 2)
    key = (E, O, E2, tuple(sorted(_debug)))
    if key not in _CACHE:
        _CACHE[key] = build(cfg)
    nc = _CACHE[key]

    w_in0, w_out0, w_up0, w_down0 = f(w_in)[0], f(w_out)[0], f(w_up)[0], f(w_down)[0]
    b_in0 = f(b_in)[0]
    rowp = np.zeros((16, D), np.float32)
    for i, a in enumerate([ln_emb_g, ln_emb_b, f(b_out)[0], f(ln1_g)[0], f(ln1_b)[0], f(b_down)[0], f(ln2_g)[0], f(ln2_b)[0]]):
        rowp[i] = f(a)
    b_qkv = b_in0[None, :3072].copy()
    lamv = np.stack([f(lambda_q1)[0], f(lambda_k1)[0], f(lambda_q2)[0], f(lambda_k2)[0]]).astype(np.float32)
    subg = f(subln_g)[0][None, :].copy()
    colp = np.zeros((128, 16 + 8 * 31 + 8 * 3 + 88 * 5 + 32), np.float32)
    colp[:, 0:16] = b_in0[3072:].reshape(16, 128).T
    cw = f(conv_w)[0]
    colp[:, 16:16 + 248] = cw.reshape(31, 8, 128).transpose(2, 1, 0).reshape(128, 248)
    c3 = np.stack([f(conv_b)[0], f(conv_ln_g)[0], f(conv_ln_b)[0]])
    colp[:, 264:264 + 24] = c3.reshape(3, 8, 128).transpose(2, 1, 0).reshape(128, 24)
    f5 = np.concatenate([f(b_up)[0][None], f(ffn_conv_w)[0], f(ffn_conv_b)[0][None]], axis=0)
    colp[:, 288:288 + 440] = f5.reshape(5, 88, 128).transpose(2, 1, 0).reshape(128, 440)
    colp[:, 728:744] = f(ln_emb_g).reshape(16, 128).T
    colp[:, 744:760] = f(ln_emb_b).reshape(16, 128).T
    ident = np.eye(128, dtype=np.float32)

    in_maps = []
    info = []
    for c in range(NCORES):
        b, half = c // 2, c % 2
        seq = np.concatenate([meta, x_prompt[b]], axis=0)
        start = -16 if half == 0 else S_p // 2 - 16
        pos_e = np.arange(start, start + E)
        val_e = (pos_e >= 0) & (pos_e < L)
        xe = np.zeros((E + O, D), np.float32)
        xe[:E][val_e] = seq[pos_e[val_e]]
        other = np.arange(E - 16, L) if half == 0 else np.arange(0, S_p // 2 - 16)
        pos_o = np.zeros(O, np.int64)
        val_o = np.zeros(O, bool)
        pos_o[:len(other)] = other
        val_o[:len(other)] = True
        xe[E:E + len(other)] = seq[other]
        pos_all = np.concatenate([np.where(val_e, pos_e, 0), pos_o])
        val_all = np.concatenate([val_e, val_o])
        seq2 = np.concatenate([meta, x_sample[c]], axis=0)
        pos_s = np.arange(-16, -16 + E2)
        val_s = (pos_s >= 0) & (pos_s < L2)
        xs = np.zeros((E2, D), np.float32)
        xs[val_s] = seq2[pos_s[val_s]]
        in_maps.append({
            "xe_p": xe, "xe_s": xs,
            "mask8_p": np.repeat(val_all[:, None], 8, axis=1).astype(np.float32),
            "mask8_s": np.repeat(val_s[:, None], 8, axis=1).astype(np.float32),
            "mrow_p": val_e[None, :].astype(np.float32), "mrow_s": val_s[None, :].astype(np.float32),
            "rope_p": rope_table(pos_all), "rope_s": rope_table(np.where(val_s, pos_s, 0)),
            "w_in": w_in0, "w_out": w_out0, "w_up": w_up0, "w_down": w_down0, "ident": ident, "rowp": rowp,
            "b_qkv": b_qkv, "lamv": lamv, "subg": subg, "colp": colp,
        })
        r0 = 32
        info.append((b, half, r0))
    res = run_bass_kernel_spmd(nc, in_maps, core_ids=list(range(NCORES)))
    if _return_raw:
        return res.results, info, G
    y_p = np.empty((B, S_p, D), np.float32)
    y_s = np.empty((B2, S_s, D), np.float32)
    for c in range(NCORES):
        b, half, r0 = info[c]
        r = res.results[c]
        y_p[b, half * (S_p // 2):(half + 1) * (S_p // 2)] = r["y_p"][r0:r0 + S_p // 2]
        y_s[c] = r["y_s"][32:32 + S_s]
    return (y_p, y_s)
```

```python
import math
from contextlib import ExitStack
import numpy as np
import concourse.bass as bass
import concourse.mybir as mybir
from concourse.bass_utils import run_bass_kernel_spmd

F32 = mybir.dt.float32
BF16 = mybir.dt.bfloat16
AF = mybir.ActivationFunctionType
ALU = mybir.AluOpType
AX = mybir.AxisListType

D = 2048
N_META = 16
NH = 8
HD = 64
D_ATT = 1024
D_QK = 1024
D_CONV = 1024
D_IN = 5120
CW = 31
D_FF = 5632
NFC = D_FF // 128
LN_EPS = 1e-5
ALPHA = 2.0 ** 0.25
LAM_INIT = 0.8 - 0.6 * math.exp(0.0)
ROPE_THETA = 500000.0
NCORES = 8
import os
PIPE_WIDTH = int(os.environ.get('K_PIPE', '2'))
PIPE_PH = os.environ.get('K_PH', '1268')
EPI_INTERLEAVE = int(os.environ.get('K_EPI', '1'))


class Buf:
    __slots__ = ("name", "lw", "rd")

    def __init__(self, name=""):
        self.name = name
        self.lw = None
        self.rd = {}


class DSem:
    def __init__(self, h, key):
        self.h = h
        self.key = key
        self.count = 0


class T:
    def __init__(self, t, name, ds=None):
        self.t = t
        self.b = Buf(name)
        self.ds = ds

    def __getitem__(self, idx):
        return self.t[idx]


class FW:
    ENG = ("pe", "act", "dve", "pool", "sp")
    STRICT = True

    def __init__(self, nc, es):
        self.nc = nc
        self.es = es
        self.eng = {"pe": nc.tensor, "act": nc.scalar, "dve": nc.vector, "pool": nc.gpsimd, "sp": nc.sync}
        self.sem = {e: es.enter_context(nc.semaphore("s_" + e)) for e in ("pe", "act", "dve", "pool")}
        self.cnt = {e: 0 for e in self.sem}
        self.waited = {e: {} for e in self.ENG}
        self.dsems = []
        self.nops = 0

    def new_dsem(self, name):
        h = self.es.enter_context(self.nc.semaphore("d_" + name))
        d = DSem(h, "d_" + name + str(len(self.dsems)))
        self.dsems.append(d)
        return d

    def tile(self, es, shape, dtype, name, dma=False, psum=False):
        self.ntile = getattr(self, "ntile", 0) + 1
        name = "t%d_%s" % (self.ntile, name)
        if psum:
            t = es.enter_context(self.nc.psum_tensor(name, list(shape), dtype))
        else:
            t = es.enter_context(self.nc.sbuf_tensor(name, list(shape), dtype))
        return T(t, name, self.new_dsem(name) if dma else None)

    def _need(self, e, dep):
        key, h, val = dep
        if self.waited[e].get(key, 0) >= val:
            return
        if key == e:
            if val > self.cnt[e]:
                assert FW.STRICT, "same-engine wait on future value"
                return
        elif key in self.cnt:
            assert val <= self.cnt[key], "wait on a not-yet-emitted signal (deadlock hazard)"
        self.eng[e].wait_ge(h, val)
        self.waited[e][key] = val

    def _deps(self, e, reads, writes, own):
        for t in reads:
            b = t.b
            if b.lw is not None:
                self._need(e, b.lw)
        strict = FW.STRICT
        for t in writes:
            b = t.b
            if b.lw is not None and (strict or b.lw[0] != own):
                self._need(e, b.lw)
            for key, (h, val) in b.rd.items():
                if strict or key != own:
                    self._need(e, (key, h, val))

    def _record(self, dep, reads, writes):
        key, h, val = dep
        for t in reads:
            t.b.rd[key] = (h, val)
        for t in writes:
            t.b.lw = dep
            t.b.rd = {}

    def op(self, e, fn, reads=(), writes=(), signal=True):
        self._deps(e, reads, writes, e)
        inst = fn()
        self.nops += 1
        if signal:
            self.cnt[e] += 1
            inst.then_inc(self.sem[e], 1)
            v = self.cnt[e]
        else:
            v = self.cnt[e] + 1
        self._record((e, self.sem[e], v), reads, writes)

    def dma(self, q, out, in_, reads, writes, ds, **kw):
        self._deps(q, reads, writes, None)
        self._need(q, (ds.key, ds.h, ds.count))
        inst = self.eng[q].dma_start(out=out, in_=in_, **kw)
        self.nops += 1
        ds.count += 16
        inst.then_inc(ds.h, 16)
        self._record((ds.key, ds.h, ds.count), reads, writes)

    def barrier(self):
        for e in self.ENG:
            for k in self.sem:
                if self.cnt[k] > 0:
                    self._need(e, (k, self.sem[k], self.cnt[k])) if k != e else None
            for d in self.dsems:
                if d.count > 0:
                    self._need(e, (d.key, d.h, d.count))


class Cfg:
    def __init__(self, NE, NO, NE2, L2, debug=(), half_p=None):
        self.NE, self.NO, self.NE2 = NE, NO, NE2
        self.vhi = {"p": 32 + half_p, "s": 16 + L2}
        self.rhi = {"p": 32 + half_p, "s": 16 + L2}
        self.E, self.O, self.E2 = NE * 128, NO * 128, NE2 * 128
        self.debug = set(debug)


def blocks512(ntiles):
    out = []
    t = 0
    while t < ntiles:
        n = min(4, ntiles - t)
        out.append((t * 128, n))
        t += n
    return out


def ffn_blocks(rhi):
    out = []
    s = 31
    end = rhi + 1
    while s + 1 < rhi:
        n = min(512, end - s)
        out.append((s, n))
        s += n - 2
    return out


def build(cfg):
    nc = bass.Bass("TRN2", target_bir_lowering=False)
    NE, NO, NE2 = cfg.NE, cfg.NO, cfg.NE2
    E, O, E2 = cfg.E, cfg.O, cfg.E2
    segs = [("p", NE, NO), ("s", NE2, 0)]

    def din(name, shape, dt=F32):
        return nc.dram_tensor(name, list(shape), dt, kind="ExternalInput").ap()

    def dscr(name, shape, dt):
        kind = "ExternalOutput" if name in cfg.debug else "Internal"
        return nc.dram_tensor(name, list(shape), dt, kind=kind).ap()

    X = {"p": din("xe_p", [E + O, D]), "s": din("xe_s", [E2, D])}
    MASK8 = {"p": din("mask8_p", [E + O, 8]), "s": din("mask8_s", [E2, 8])}
    MROW = {"p": din("mrow_p", [1, E]), "s": din("mrow_s", [1, E2])}
    ROPE = {"p": din("rope_p", [E + O, 256]), "s": din("rope_s", [E2, 256])}
    w_in = din("w_in", [D, D_IN])
    w_out = din("w_out", [D, D])
    w_up = din("w_up", [D, 2 * D_FF])
    w_down = din("w_down", [D_FF, D])
    ident_in = din("ident", [128, 128])
    rowp = din("rowp", [16, D])
    b_qkv = din("b_qkv", [1, 3072])
    lamv = din("lamv", [4, 64])
    subg = din("subg", [1, 128])
    colp = din("colp", [128, 16 + 8 * 31 + 8 * 3 + 88 * 5 + 32])
    ROWS = {"emb_g": 0, "emb_b": 1, "b_out": 2, "ln1_g": 3, "ln1_b": 4, "b_down": 5, "ln2_g": 6, "ln2_b": 7}

    Y = {"p": nc.dram_tensor("y_p", [E, D], F32, kind="ExternalOutput").ap(),
         "s": nc.dram_tensor("y_s", [E2, D], F32, kind="ExternalOutput").ap()}

    S = {}
    for sg, ne, no in segs:
        nt = ne + no
        nb = len(blocks512(ne))
        S[sg] = dict(
            h0s=dscr("h0s_" + sg, [ne, 128, D], F32),
            h0T=dscr("h0T_" + sg, [nt, 128, 16, 128], BF16),
            qT=dscr("qT_" + sg, [NH, 128, ne * 128], BF16),
            kT=dscr("kT_" + sg, [NH, 128, nt * 128], BF16),
            v=dscr("v_" + sg, [NH, 128, nt, 129], BF16),
            c=dscr("c_" + sg, [8, 128, ne * 128], BF16),
            mixT=dscr("mixT_" + sg, [nb, 128, 16, 512], BF16),
            h1s=dscr("h1s_" + sg, [ne, 128, D], F32),
            h1T=dscr("h1T_" + sg, [128, 16, ne * 128], BF16),
            fT=dscr("fT_" + sg, [128, NFC, ne * 128], BF16),
            ffa=dscr("ffa_" + sg, [ne, 128, 1024], F32),
        )

    with ExitStack() as es0:
        fw = FW(nc, es0)
        ps_t = es0.enter_context(nc.psum_tensor("ps", [128, 8, 512], F32))
        ps_bf = ps_t.bitcast(BF16)
        PB = [T(None, "psb%d" % i) for i in range(8)]

        def ps(b, n=512, nb=1):
            if nb == 1:
                return ps_t[:, b, 0:n]
            return ps_t[:, b:b + nb, :]

        ident_f = fw.tile(es0, [128, 128], F32, "ident_f", dma=True)
        ident_b = fw.tile(es0, [128, 128], BF16, "ident_b")
        colp_t = fw.tile(es0, [128, colp.shape[1]], F32, "colp", dma=True)
        neglam = fw.tile(es0, [128, 1], F32, "neglam")
        gsub = fw.tile(es0, [128, 128], F32, "gsub", dma=True)
        mhalf = fw.tile(es0, [128, 1], F32, "mhalf")
        epsc = fw.tile(es0, [128, 1], F32, "epsc")
        fw.dma("sp", ident_f[:], ident_in, [], [ident_f], ident_f.ds)
        fw.dma("sp", colp_t[:], colp, [], [colp_t], colp_t.ds)
        fw.dma("sp", gsub[:], subg.partition_broadcast(128), [], [gsub], gsub.ds)
        fw.op("dve", lambda: nc.vector.tensor_copy(out=ident_b[:], in_=ident_f[:]), [ident_f], [ident_b])
        fw.op("dve", lambda: nc.vector.memset(mhalf[:], -0.5), [], [mhalf])
        fw.op("dve", lambda: nc.vector.memset(epsc[:], LN_EPS), [], [epsc])
        fw.op("dve", lambda: nc.vector.tensor_scalar(out=gsub[:], in0=gsub[:], scalar1=1.0 - LAM_INIT, scalar2=None,
                                                    op0=ALU.mult), [gsub], [gsub])
        CP_BU = 0
        CP_CW = 16
        CP_C3 = CP_CW + 8 * 31
        CP_F = CP_C3 + 8 * 3
        CP_EG = CP_F + 88 * 5
        CP_EB = CP_EG + 16

        def cp(i):
            return colp_t[:, i:i + 1]

        if True:
            es = es0
            lt = fw.tile(es, [128, 4, 64], F32, "lamt", dma=True)
            lp = fw.tile(es, [128, 2, 64], F32, "lamp")
            ls = fw.tile(es, [128, 2], F32, "lams")
            le = fw.tile(es, [128, 2], F32, "lame")
            fw.dma("sp", lt[:].rearrange("p a b -> p (a b)"), lamv.rearrange("(o a) b -> o (a b)", o=1).partition_broadcast(128),
                   [], [lt], lt.ds)
            fw.op("dve", lambda: nc.vector.tensor_tensor(out=lp[:, 0, :], in0=lt[:, 0, :], in1=lt[:, 1, :], op=ALU.mult), [lt], [lp])
            fw.op("dve", lambda: nc.vector.tensor_tensor(out=lp[:, 1, :], in0=lt[:, 2, :], in1=lt[:, 3, :], op=ALU.mult), [lt], [lp])
            fw.op("dve", lambda: nc.vector.tensor_reduce(out=ls[:], in_=lp[:], axis=AX.X, op=ALU.add), [lp], [ls])
            fw.op("act", lambda: nc.scalar.activation(out=le[:], in_=ls[:], func=AF.Exp), [ls], [le])
            fw.op("dve", lambda: nc.vector.tensor_tensor(out=neglam[:], in0=le[:, 1:2], in1=le[:, 0:1], op=ALU.subtract), [le], [neglam])
            fw.op("dve", lambda: nc.vector.tensor_scalar(out=neglam[:], in0=neglam[:], scalar1=-LAM_INIT, scalar2=None,
                                                        op0=ALU.add), [neglam], [neglam])

        def load_w(es, name, src, r0, nk, c0, ncols, dst=None, dstc0=0):
            if dst is None:
                dst = fw.tile(es, [128, nk, ncols], BF16, name, dma=True)
            KG = int(os.environ.get('K_WKG', '16'))
            c = 0
            while c < ncols:
                n = min(1024, ncols - c)
                k0 = 0
                while k0 < nk:
                    kk = min(KG, nk - k0)
                    fw.dma("pool", dst[:, k0:k0 + kk, dstc0 + c:dstc0 + c + n],
                           src[r0 + k0 * 128:r0 + (k0 + kk) * 128, c0 + c:c0 + c + n].rearrange("(k p) c -> p k c", p=128),
                           [], [dst], dst.ds, max_dma_last_dim=4096)
                    k0 += kk
                c += n
            return dst

        def bcast_row(es, name, r):
            t = fw.tile(es, [128, D], F32, name, dma=True)
            fw.dma("sp", t[:], rowp[r:r + 1, :].partition_broadcast(128), [], [t], t.ds)
            return t

        def layer_norm_tm(es_unused, src_ap, src_bufs, bufs, g_bc, b_bc, out_t, mul_eng="pool"):
            st, mv, rs, nm, xn = bufs["st"], bufs["mv"], bufs["rs"], bufs["nm"], bufs["xn"]
            for i in range(4):
                fw.op("dve", lambda i=i: nc.vector.bn_stats(out=st[:, i * 6:(i + 1) * 6], in_=src_ap[:, i * 512:(i + 1) * 512]),
                      src_bufs, [st])
            yield
            fw.op("dve", lambda: nc.vector.bn_aggr(out=mv[:], in_=st[:]), [st], [mv])
            yield
            fw.op("dve", lambda: nc.vector.tensor_scalar(out=rs[:], in0=mv[:, 1:2], scalar1=LN_EPS, scalar2=None, op0=ALU.add),
                  [mv], [rs])
            yield
            fw.op("pool", lambda: nc.gpsimd.tensor_tensor(out=rs[:], in0=rs[:], in1=mhalf[:], op=ALU.pow), [rs, mhalf], [rs])
            yield
            fw.op("dve", lambda: nc.vector.tensor_scalar(out=nm[:], in0=mv[:, 0:1], scalar1=rs[:], scalar2=-1.0,
                                                        op0=ALU.mult, op1=ALU.mult), [mv, rs], [nm])
            yield
            fw.op("act", lambda: nc.scalar.activation(out=xn[:], in_=src_ap, func=AF.Identity, bias=nm[:], scale=rs[:]),
                  list(src_bufs) + [nm, rs], [xn])
            yield
            me = fw.eng[mul_eng]
            fw.op(mul_eng, lambda: me.tensor_tensor(out=xn[:], in0=xn[:], in1=g_bc[:], op=ALU.mult), [xn, g_bc], [xn])
            yield
            fw.op("dve", lambda: nc.vector.tensor_tensor(out=out_t[:], in0=xn[:], in1=b_bc[:], op=ALU.add), [xn, b_bc], [out_t])
            yield

        def ln_bufs(es, tag, dma=False):
            return dict(st=fw.tile(es, [128, 24], F32, "st" + tag), mv=fw.tile(es, [128, 2], F32, "mv" + tag),
                        rs=fw.tile(es, [128, 1], F32, "rs" + tag), nm=fw.tile(es, [128, 1], F32, "nm" + tag),
                        xn=fw.tile(es, [128, D], F32, "xn" + tag, dma=dma))

        def transpose16(src_bf, src_bufs, dstT, pbanks, evac_eng):
            for half in range(2):
                pb = pbanks[half]
                for k in range(8):
                    kc = half * 8 + k
                    fw.op("pe", lambda kc=kc, k=k, pb=pb: nc.tensor.transpose(out=ps_bf[:, pb, k * 128:(k + 1) * 128],
                                                                             in_=src_bf[:, kc * 128:(kc + 1) * 128], identity=ident_b[:]),
                          list(src_bufs) + [ident_b], [PB[pb]], signal=(k == 7))
                yield
                if evac_eng == "act":
                    fw.op("act", lambda half=half, pb=pb: nc.scalar.copy(out=dstT[:, half * 8:(half + 1) * 8, :].rearrange("p a b -> p (a b)"),
                                                                         in_=ps_bf[:, pb, :]), [PB[pb]], [dstT])
                else:
                    fw.op("dve", lambda half=half, pb=pb: nc.vector.tensor_copy(out=dstT[:, half * 8:(half + 1) * 8, :].rearrange("p a b -> p (a b)"),
                                                                               in_=ps_bf[:, pb, :]), [PB[pb]], [dstT])
                yield

        def pipeline(factories, lag, width=None, loads=None, pf=0):
            width = PIPE_WIDTH if width is None else width
            nitems = len(factories)
            active = []
            state = {"next": 0}
            if loads is not None:
                for n in range(min(pf, nitems)):
                    loads(n)

            def start():
                n = state["next"]
                state["next"] += 1
                if loads is not None and n + pf < nitems:
                    loads(n + pf)
                active.append([factories[n](), 0])

            start()
            while active or state["next"] < nitems:
                for g in list(active):
                    try:
                        next(g[0])
                        g[1] += 1
                    except StopIteration:
                        active.remove(g)
                if state["next"] < nitems and len(active) < width and (not active or active[-1][1] >= lag):
                    start()

        es12 = ExitStack()
        wq = fw.tile(es12, [128, 16, 3072], BF16, "wqkv", dma=True)
        with ExitStack() as es:
            g_bc = bcast_row(es, "embg", ROWS["emb_g"])
            b_bc = bcast_row(es, "embb", ROWS["emb_b"])
            bo_bc = bcast_row(es, "bout", ROWS["b_out"])
            fw.op("dve", lambda: nc.vector.tensor_scalar(out=g_bc[:], in0=g_bc[:], scalar1=ALPHA, scalar2=None, op0=ALU.mult), [g_bc], [g_bc])
            fw.op("dve", lambda: nc.vector.scalar_tensor_tensor(out=b_bc[:], in0=b_bc[:], scalar=ALPHA, in1=bo_bc[:],
                                                                op0=ALU.mult, op1=ALU.add), [b_bc, bo_bc], [b_bc])
            W1 = int(os.environ.get('K_W1', '3'))
            NX = W1 + 2
            xt = [fw.tile(es, [128, D], F32, "xt%d" % i, dma=True) for i in range(NX)]
            h0s = [fw.tile(es, [128, D], F32, "h0s%d" % i, dma=True) for i in range(W1)]
            hT = [fw.tile(es, [128, 16, 128], BF16, "hT%d" % i, dma=True) for i in range(W1)]
            sm = [dict(st=fw.tile(es, [128, 24], F32, "st1_%d" % i), mv=fw.tile(es, [128, 2], F32, "mv1_%d" % i),
                       rs=fw.tile(es, [128, 1], F32, "rs1_%d" % i), nm=fw.tile(es, [128, 1], F32, "nm1_%d" % i)) for i in range(W1)]
            tiles = [(sg, i, i < ne) for sg, ne, no in segs for i in range(ne + no)]

            def p1_load(n):
                sg, i, isE = tiles[n]
                x_t = xt[n % NX]
                fw.dma("sp", x_t[:], X[sg][i * 128:(i + 1) * 128, :], [], [x_t], x_t.ds)

            def p1_tile(n):
                sg, i, isE = tiles[n]
                s = n % W1
                if n == min(6, len(tiles) - 1):
                    load_w(es12, "wqkv", w_in, 0, 16, 0, 3072, dst=wq)
                xn = xt[n % NX]
                st, mv, rs, nm = sm[s]["st"], sm[s]["mv"], sm[s]["rs"], sm[s]["nm"]
                for q4 in range(4):
                    fw.op("dve", lambda q4=q4: nc.vector.bn_stats(out=st[:, q4 * 6:(q4 + 1) * 6], in_=xn[:, q4 * 512:(q4 + 1) * 512]), [xn], [st])
                yield
                fw.op("dve", lambda: nc.vector.bn_aggr(out=mv[:], in_=st[:]), [st], [mv])
                yield
                fw.op("dve", lambda: nc.vector.tensor_scalar(out=rs[:], in0=mv[:, 1:2], scalar1=LN_EPS, scalar2=None, op0=ALU.add), [mv], [rs])
                yield
                fw.op("pool", lambda: nc.gpsimd.tensor_tensor(out=rs[:], in0=rs[:], in1=mhalf[:], op=ALU.pow), [rs, mhalf], [rs])
                yield
                fw.op("dve", lambda: nc.vector.tensor_scalar(out=nm[:], in0=mv[:, 0:1], scalar1=rs[:], scalar2=-1.0,
                                                            op0=ALU.mult, op1=ALU.mult), [mv, rs], [nm])
                yield
                fw.op("act", lambda: nc.scalar.activation(out=xn[:], in_=xn[:], func=AF.Identity, bias=nm[:], scale=rs[:]), [xn, nm, rs], [xn])
                yield
                pb0 = 4 * (n % 2)
                for half in range(2):
                    for k4 in range(2):
                        bank = pb0 + 2 * half + k4
                        for k in range(4):
                            kc = half * 8 + k4 * 4 + k
                            fw.op("pe", lambda kc=kc, k=k, bank=bank: nc.tensor.transpose(out=ps_t[:, bank, k * 128:(k + 1) * 128],
                                                                                         in_=xn[:, kc * 128:(kc + 1) * 128], identity=ident_f[:]),
                                  [xn, ident_f], [PB[bank]], signal=(k == 3))
                        yield
                        for k in range(4):
                            kc = half * 8 + k4 * 4 + k
                            if k % 2 == 0:
                                fw.op("dve", lambda kc=kc, k=k, bank=bank: nc.vector.tensor_scalar(
                                    out=hT[s][:, kc, :], in0=ps_t[:, bank, k * 128:(k + 1) * 128], scalar1=cp(CP_EG + kc), scalar2=cp(CP_EB + kc),
                                    op0=ALU.mult, op1=ALU.add), [PB[bank], colp_t], [hT[s]])
                            else:
                                fw.op("act", lambda kc=kc, k=k, bank=bank: nc.scalar.activation(
                                    out=hT[s][:, kc, :], in_=ps_t[:, bank, k * 128:(k + 1) * 128], func=AF.Identity, bias=cp(CP_EB + kc), scale=cp(CP_EG + kc)),
                                    [PB[bank], colp_t], [hT[s]])
                        yield
                fw.dma("sp", S[sg]["h0T"][i], hT[s][:], [hT[s]], [], hT[s].ds)
                yield
                if isE:
                    fw.op("pool", lambda: nc.gpsimd.tensor_tensor(out=h0s[s][:], in0=xn[:], in1=g_bc[:], op=ALU.mult), [xn, g_bc], [h0s[s]])
                    yield
                    fw.op("pool", lambda: nc.gpsimd.tensor_tensor(out=h0s[s][:], in0=h0s[s][:], in1=b_bc[:], op=ALU.add), [h0s[s], b_bc], [h0s[s]])
                    fw.dma("sp", S[sg]["h0s"][i], h0s[s][:], [h0s[s]], [], h0s[s].ds)
                    yield

            pipeline([(lambda n=n: p1_tile(n)) for n in range(len(tiles))], lag=6, width=(W1 if '1' in PIPE_PH else 1), loads=p1_load, pf=2)
            fw.barrier()

        with ExitStack() as es:
            bq_bc = fw.tile(es, [128, 3072], F32, "bqkv", dma=True)
            fw.dma("sp", bq_bc[:], b_qkv.partition_broadcast(128), [], [bq_bc], bq_bc.ds)
            hT = [fw.tile(es, [128, 16, 128], BF16, "p2hT%d" % i, dma=True) for i in range(4)]
            rp = [fw.tile(es, [128, 256], F32, "rp%d" % i, dma=True) for i in range(4)]
            m8 = [fw.tile(es, [128, 8], F32, "m8_%d" % i, dma=True) for i in range(4)]
            qf = [fw.tile(es, [128, 1024], F32, "qf%d" % i) for i in range(2)]
            kf = [fw.tile(es, [128, 1024], F32, "kf%d" % i) for i in range(2)]
            vf = [fw.tile(es, [128, 1024], F32, "vf%d" % i) for i in range(2)]
            tq = [fw.tile(es, [128, 4, 16, 8], F32, "tq%d" % i) for i in range(2)]
            tk = [fw.tile(es, [128, 4, 16, 8], F32, "tk%d" % i) for i in range(2)]
            qb = [fw.tile(es, [128, 1024], BF16, "qb%d" % i) for i in range(2)]
            kb = [fw.tile(es, [128, 1024], BF16, "kb%d" % i) for i in range(2)]
            qT = [fw.tile(es, [128, 8, 128], BF16, "qTt%d" % i, dma=True) for i in range(2)]
            kT = [fw.tile(es, [128, 8, 128], BF16, "kTt%d" % i, dma=True) for i in range(2)]
            va = [fw.tile(es, [128, 8, 129], BF16, "va%d" % i, dma=True) for i in range(2)]
            tiles = [(sg, i, i < ne) for sg, ne, no in segs for i in range(ne + no)]

            def rope(eng, x, tmp, rps):
                e = fw.eng[eng]
                xv = x[:].rearrange("p (g d) -> p g d", d=64)
                x1, x2 = xv[:, :, 0:8], xv[:, :, 8:16]
                C = rps[:, 0:128].rearrange("p (g d) -> p g d", d=8)
                Sn = rps[:, 128:256].rearrange("p (g d) -> p g d", d=8)
                fw.op(eng, lambda: e.tensor_tensor(out=tmp[:, 0], in0=x1, in1=C, op=ALU.mult), [x, rps], [tmp])
                fw.op(eng, lambda: e.tensor_tensor(out=tmp[:, 1], in0=x2, in1=Sn, op=ALU.mult), [x, rps], [tmp])
                fw.op(eng, lambda: e.tensor_tensor(out=tmp[:, 2], in0=x2, in1=C, op=ALU.mult), [x, rps], [tmp])
                fw.op(eng, lambda: e.tensor_tensor(out=tmp[:, 3], in0=x1, in1=Sn, op=ALU.mult), [x, rps], [tmp])
                yield
                fw.op(eng, lambda: e.tensor_tensor(out=x1, in0=tmp[:, 0], in1=tmp[:, 1], op=ALU.subtract), [tmp], [x])
                fw.op(eng, lambda: e.tensor_tensor(out=x2, in0=tmp[:, 2], in1=tmp[:, 3], op=ALU.add), [tmp], [x])
                yield

            def p2_load(n):
                sg, i, isE = tiles[n]
                l = n % 4
                h_t, r_t, m_t = hT[l], rp[l], m8[l]
                fw.dma("sp", h_t[:], S[sg]["h0T"][i], [], [h_t], h_t.ds)
                fw.dma("sp", r_t[:], ROPE[sg][i * 128:(i + 1) * 128, :], [], [r_t], r_t.ds)
                fw.dma("sp", m_t[:], MASK8[sg][i * 128:(i + 1) * 128, :], [], [m_t], m_t.ds)

            def p2_tile(n):
                sg, i, isE = tiles[n]
                s = n % 2
                l = n % 4
                h_t, r_t, m_t = hT[l], rp[l], m8[l]
                groups = [2, 3, 4, 5, 0, 1] if isE else [2, 3, 4, 5]
                for g in groups:
                    for kc in range(16):
                        fw.op("pe", lambda g=g, kc=kc: nc.tensor.matmul(ps(g), lhsT=h_t[:, kc, :], rhs=wq[:, kc, g * 512:(g + 1) * 512],
                                                                        start=(kc == 0), stop=(kc == 15)),
                              [h_t, wq], [PB[g]], signal=(kc == 15))
                    yield
                    if g == 3:
                        fw.op("dve", lambda: nc.vector.tensor_tensor(out=kf[s][:].rearrange("p (a b) -> p a b", a=2), in0=ps(2, nb=2),
                                                                    in1=bq_bc[:, 1024:2048].rearrange("p (a b) -> p a b", a=2), op=ALU.add),
                              [PB[2], PB[3], bq_bc], [kf[s]])
                    if g == 5:
                        fw.op("dve", lambda: nc.vector.tensor_tensor(out=vf[s][:].rearrange("p (a b) -> p a b", a=2), in0=ps(4, nb=2),
                                                                    in1=bq_bc[:, 2048:3072].rearrange("p (a b) -> p a b", a=2), op=ALU.add),
                              [PB[4], PB[5], bq_bc], [vf[s]])
                    if g == 1:
                        fw.op("dve", lambda: nc.vector.tensor_tensor(out=qf[s][:].rearrange("p (a b) -> p a b", a=2), in0=ps(0, nb=2),
                                                                    in1=bq_bc[:, 0:1024].rearrange("p (a b) -> p a b", a=2), op=ALU.add),
                              [PB[0], PB[1], bq_bc], [qf[s]])
                fw.op("pool", lambda: nc.gpsimd.tensor_scalar(out=va[s][:, :, 0:128], in0=vf[s][:].rearrange("p (h e) -> p h e", e=128),
                                                             scalar1=m_t[:, 0:1], scalar2=1.0, op0=ALU.mult, op1=ALU.mult), [vf[s], m_t], [va[s]])
                fw.op("pool", lambda: nc.gpsimd.tensor_copy(out=va[s][:, :, 128:129], in_=m_t[:].unsqueeze(2)), [m_t], [va[s]])
                fw.dma("sp", S[sg]["v"][:, :, i, :].rearrange("h p e -> p h e"), va[s][:], [va[s]], [], va[s].ds)
                yield
                yield from rope("pool", kf[s], tk[s], r_t)
                fw.op("act", lambda: nc.scalar.copy(out=kb[s][:], in_=kf[s][:]), [kf[s]], [kb[s]])
                yield
                if isE:
                    yield from rope("dve", qf[s], tq[s], r_t)
                    fw.op("act", lambda: nc.scalar.activation(out=qb[s][:], in_=qf[s][:], func=AF.Copy, scale=HD ** -0.5), [qf[s]], [qb[s]])
                    yield
                for (src, dstT, pb, dkey, on) in ((kb[s], kT[s], 6, "kT", True), (qb[s], qT[s], 7, "qT", isE)):
                    if not on:
                        continue
                    for h in range(8):
                        fw.op("pe", lambda h=h, src=src, pb=pb: nc.tensor.transpose(out=ps_bf[:, pb, h * 128:(h + 1) * 128],
                                                                                   in_=src[:, h * 128:(h + 1) * 128], identity=ident_b[:]),
                              [src, ident_b], [PB[pb]], signal=(h == 7))
                    yield
                    fw.op("act", lambda dstT=dstT, pb=pb: nc.scalar.copy(out=dstT[:].rearrange("p a b -> p (a b)"), in_=ps_bf[:, pb, :]),
                          [PB[pb]], [dstT])
                    fw.dma("sp", S[sg][dkey][:, :, i * 128:(i + 1) * 128].rearrange("h p t -> p h t"), dstT[:], [dstT], [], dstT.ds)
                    yield

            pipeline([(lambda n=n: p2_tile(n)) for n in range(len(tiles))], lag=8, width=(None if '2' in PIPE_PH else 1), loads=p2_load, pf=2)
            fw.barrier()

        es12.close()
        es34 = ExitStack()
        dg = fw.tile(es34, [128, 8, 31, 128], BF16, "dg")
        onesm = fw.tile(es34, [128, 128], F32, "onesm")
        with ExitStack() as es:
            wu = load_w(es, "wu", w_in, 0, 16, 3072, 2048)
            fw.op("pool", lambda: nc.gpsimd.memset(onesm[:], 1.0 / D_CONV), [], [onesm])
            for j in range(8):
                for tp in range(31):
                    fw.op("pool", lambda j=j, tp=tp: nc.gpsimd.tensor_scalar(out=dg[:, j, tp, :], in0=ident_f[:], scalar1=cp(CP_CW + j * 31 + tp),
                                                                            scalar2=1.0, op0=ALU.mult, op1=ALU.mult), [ident_f, colp_t], [dg])
            hb = [fw.tile(es, [128, 4, 16, 128], BF16, "p3h%d" % i, dma=True) for i in range(2)]
            mr = [fw.tile(es, [128, 512], F32, "p3m%d" % i, dma=True) for i in range(2)]
            sig = [fw.tile(es, [128, 512], F32, "sig%d" % i) for i in range(2)]
            cf = [fw.tile(es, [128, 512], F32, "cf%d" % i) for i in range(2)]
            ct = [fw.tile(es, [128, 512], BF16, "ct%d" % i, dma=True) for i in range(3)]
            blks = [(sg, t0, n, ne) for sg, ne, no in segs for (t0, n) in blocks512(ne)]

            def p3_load(bi):
                sg, t0, n, ne = blks[bi]
                s = bi % 2
                fw.dma("sp", hb[s][:, 0:n], S[sg]["h0T"][t0 // 128:t0 // 128 + n].rearrange("t p k c -> p t k c"), [], [hb[s]], hb[s].ds)

            p3_load(0)
            it = 0
            for bi, (sg, t0, n, ne) in enumerate(blks):
                if bi + 1 < len(blks):
                    p3_load(bi + 1)
                s = bi % 2
                N = n * 128
                Etot = ne * 128
                edge = (t0 < 16) or (t0 + N > cfg.vhi[sg])
                if edge:
                    fw.dma("sp", mr[s][:, 0:N], MROW[sg][0:1, t0:t0 + N].partition_broadcast(128), [], [mr[s]], mr[s].ds)
                for j in range(8):
                    ba, bg = (it % 2) * 2, (it % 2) * 2 + 1
                    for part, bank in ((j, ba), (8 + j, bg)):
                        for kc in range(16):
                            fw.op("pe", lambda part=part, bank=bank, kc=kc: nc.tensor.matmul(
                                ps_t[:, bank, 0:N].rearrange("p (t c) -> p t c", c=128), lhsT=wu[:, kc, part * 128:(part + 1) * 128],
                                rhs=hb[s][:, 0:n, kc, :], start=(kc == 0), stop=(kc == 15)),
                                [wu, hb[s]], [PB[bank]], signal=(kc == 15))
                    sg_t, cf_t, ct_t = sig[it % 2], cf[it % 2], ct[it % 3]
                    fw.op("act", lambda: nc.scalar.activation(out=sg_t[:, 0:N], in_=ps(bg, N), func=AF.Sigmoid, bias=cp(CP_BU + 8 + j), scale=1.0),
                          [PB[bg], colp_t], [sg_t])
                    dst = cf_t if edge else ct_t
                    fw.op("dve", lambda: nc.vector.scalar_tensor_tensor(out=dst[:, 0:N], in0=ps(ba, N), scalar=cp(CP_BU + j), in1=sg_t[:, 0:N],
                                                                        op0=ALU.add, op1=ALU.mult), [PB[ba], colp_t, sg_t], [dst])
                    if edge:
                        fw.op("dve", lambda: nc.vector.tensor_tensor(out=ct_t[:, 0:N], in0=cf_t[:, 0:N], in1=mr[s][:, 0:N], op=ALU.mult),
                              [cf_t, mr[s]], [ct_t])
                    fw.dma("sp", S[sg]["c"][j, :, t0:t0 + N], ct_t[:, 0:N], [ct_t], [], ct_t.ds)
                    it += 1
            fw.barrier()

        with ExitStack() as es:
            cb = [fw.tile(es, [128, 542], BF16, "cb%d" % i, dma=True) for i in range(3)]
            xcs = [fw.tile(es, [128, 8, 512], F32, "xc%d" % i) for i in range(2)]
            xqs = [fw.tile(es, [128, 8, 512], F32, "xq%d" % i) for i in range(2)]
            msq = fw.tile(es, [128, 512], F32, "msq")
            var = fw.tile(es, [128, 512], F32, "var")
            tt = [fw.tile(es, [128, 512], F32, "tt%d" % i) for i in range(2)]
            co = [fw.tile(es, [128, 512], BF16, "co%d" % i, dma=True) for i in range(2)]
            blks = [(sg, bi, t0, n, ne) for sg, ne, no in segs for bi, (t0, n) in enumerate(blocks512(ne))]
            it = 0
            for bn, (sg, bidx, t0, n, ne) in enumerate(blks):
                N = n * 128
                Etot = ne * 128
                xc, xq = xcs[bn % 2], xqs[bn % 2]
                bm, bv = 2 + 2 * (bn % 2), 3 + 2 * (bn % 2)
                for j in range(8):
                    c_t = cb[it % 3]
                    lo, hi = t0 - 15, t0 + N + 15
                    clo, chi = max(lo, 0), min(hi, Etot)
                    if clo > lo:
                        fw.op("dve", lambda: nc.vector.memset(c_t[:, 0:clo - lo], 0.0), [], [c_t])
                    if chi < hi:
                        fw.op("dve", lambda: nc.vector.memset(c_t[:, chi - lo:hi - lo], 0.0), [], [c_t])
                    fw.dma("sp", c_t[:, clo - lo:chi - lo], S[sg]["c"][j, :, clo:chi], [], [c_t], c_t.ds)
                    bank = it % 2
                    for tp in range(31):
                        fw.op("pe", lambda tp=tp: nc.tensor.matmul(ps(bank, N), lhsT=dg[:, j, tp, :], rhs=c_t[:, tp:tp + N],
                                                                   start=(tp == 0), stop=(tp == 30)), [dg, c_t], [PB[bank]], signal=(tp == 30))
                    fw.op("act", lambda: nc.scalar.activation(out=xc[:, j, 0:N], in_=ps(bank, N), func=AF.Identity, bias=cp(CP_C3 + j * 3), scale=1.0),
                          [PB[bank], colp_t], [xc])
                    fw.op("act", lambda: nc.scalar.activation(out=xq[:, j, 0:N], in_=ps(bank, N), func=AF.Square, bias=cp(CP_C3 + j * 3), scale=1.0),
                          [PB[bank], colp_t], [xq])
                    it += 1
                for j in range(8):
                    fw.op("pe", lambda: nc.tensor.matmul(ps(bm, N), lhsT=onesm[:], rhs=xc[:, j, 0:N], start=(j == 0), stop=(j == 7)),
                          [onesm, xc], [PB[bm]], signal=(j == 7))
                for j in range(8):
                    fw.op("pe", lambda: nc.tensor.matmul(ps(bv, N), lhsT=onesm[:], rhs=xq[:, j, 0:N], start=(j == 0), stop=(j == 7)),
                          [onesm, xq], [PB[bv]], signal=(j == 7))
                fw.op("act", lambda: nc.scalar.activation(out=msq[:, 0:N], in_=ps(bm, N), func=AF.Square), [PB[bm]], [msq])
                fw.op("dve", lambda: nc.vector.tensor_tensor(out=var[:, 0:N], in0=ps(bv, N), in1=msq[:, 0:N], op=ALU.subtract), [PB[bv], msq], [var])
                fw.op("act", lambda: nc.scalar.activation(out=var[:, 0:N], in_=var[:, 0:N], func=AF.Sqrt, bias=epsc[:], scale=1.0), [var, epsc], [var])
                fw.op("dve", lambda: nc.vector.reciprocal(out=var[:, 0:N], in_=var[:, 0:N]), [var], [var])
                for j in range(8):
                    t_t, o_t = tt[j % 2], co[j % 2]
                    fw.op("dve", lambda: nc.vector.tensor_tensor(out=t_t[:, 0:N], in0=xc[:, j, 0:N], in1=ps(bm, N), op=ALU.subtract), [xc, PB[bm]], [t_t])
                    fw.op("dve", lambda: nc.vector.tensor_tensor(out=t_t[:, 0:N], in0=t_t[:, 0:N], in1=var[:, 0:N], op=ALU.mult), [t_t, var], [t_t])
                    fw.op("act", lambda: nc.scalar.activation(out=o_t[:, 0:N], in_=t_t[:, 0:N], func=AF.Silu, bias=cp(CP_C3 + j * 3 + 2),
                                                              scale=cp(CP_C3 + j * 3 + 1)), [t_t, colp_t], [o_t])
                    fw.dma("sp", S[sg]["mixT"][bidx, :, 8 + j, 0:N], o_t[:, 0:N], [o_t], [], o_t.ds)
            fw.barrier()

        es34.close()
        es56 = ExitStack()
        wo = load_w(es56, "wo", w_out, 0, 16, 0, 2048)
        with ExitStack() as es:
            zt = fw.tile(es, [128, NFC, 96], BF16, "zt", dma=True)
            fw.op("pool", lambda: nc.gpsimd.memset(zt[:], 0.0), [], [zt])
            for sg, ne, no in segs:
                fw.dma("pool", S[sg]["fT"][:, :, 0:32], zt[:, :, 0:32], [zt], [], zt.ds)
                r = cfg.rhi[sg]
                while r < ne * 128:
                    w_ = min(96, ne * 128 - r)
                    fw.dma("pool", S[sg]["fT"][:, :, r:r + w_], zt[:, :, 0:w_], [zt], [], zt.ds)
                    r += w_
            LKmax = max(ne + no for _, ne, no in segs)
            kTh = [fw.tile(es, [128, LKmax * 128], BF16, "kTh%d" % i, dma=True) for i in range(2)]
            vh = [fw.tile(es, [128, LKmax, 129], BF16, "vh%d" % i, dma=True) for i in range(2)]
            qblk = [fw.tile(es, [128, 512], BF16, "qblk%d" % i, dma=True) for i in range(3)]
            NP = 3
            P = [fw.tile(es, [128, 2, 512], BF16, "P%d" % i) for i in range(NP)]
            accS = [fw.tile(es, [128, 3, 396], F32, "accS%d" % i) for i in range(2)]
            rr = [fw.tile(es, [128, 4], F32, "rr%d" % i) for i in range(2)]
            ot = [fw.tile(es, [128, 128], F32, "ot%d" % i) for i in range(2)]
            osq = [fw.tile(es, [128, 128], F32, "osq%d" % i) for i in range(2)]
            ab = [fw.tile(es, [128, 128], BF16, "ab%d" % i) for i in range(2)]
            aT = [fw.tile(es, [128, 512], BF16, "aT%d" % i, dma=True) for i in range(2)]

            def acc_loc(c, j):
                idx = c * 4 + j
                return idx // 3, (idx % 3) * 132

            def epilogue(sg, h, t0, n, a_sb, a_s, ep_n):
                N = n * 128
                for j in range(n):
                    e2 = (ep_n * 4 + j) % 2
                    bk0, o0 = acc_loc(0, j)
                    bk1, o1 = acc_loc(1, j)
                    r_t, o_t, q_t, b_t = rr[e2], ot[e2], osq[e2], ab[e2]
                    fw.op("dve", lambda: nc.vector.reciprocal(out=r_t[:, 0:1], in_=a_sb[:, bk0, o0 + 128:o0 + 129]), [a_sb], [r_t])
                    fw.op("dve", lambda: nc.vector.reciprocal(out=r_t[:, 1:2], in_=a_sb[:, bk1, o1 + 128:o1 + 129]), [a_sb], [r_t])
                    yield
                    fw.op("dve", lambda: nc.vector.tensor_tensor(out=r_t[:, 1:2], in0=r_t[:, 1:2], in1=neglam[:], op=ALU.mult), [r_t, neglam], [r_t])
                    yield
                    fw.op("dve", lambda: nc.vector.tensor_scalar(out=o_t[:], in0=a_sb[:, bk0, o0:o0 + 128], scalar1=r_t[:, 0:1], scalar2=None, op0=ALU.mult),
                          [a_sb, r_t], [o_t])
                    yield
                    fw.op("dve", lambda: nc.vector.scalar_tensor_tensor(out=o_t[:], in0=a_sb[:, bk1, o1:o1 + 128], scalar=r_t[:, 1:2], in1=o_t[:],
                                                                        op0=ALU.mult, op1=ALU.add), [a_sb, r_t, o_t], [o_t])
                    yield
                    fw.op("dve", lambda: nc.vector.tensor_tensor(out=q_t[:], in0=o_t[:], in1=o_t[:], op=ALU.mult), [o_t], [q_t])
                    yield
                    fw.op("dve", lambda: nc.vector.tensor_reduce(out=r_t[:, 2:3], in_=q_t[:], axis=AX.X, op=ALU.add), [q_t], [r_t])
                    yield
                    fw.op("dve", lambda: nc.vector.tensor_scalar(out=r_t[:, 2:3], in0=r_t[:, 2:3], scalar1=1.0 / 128, scalar2=LN_EPS,
                                                                op0=ALU.mult, op1=ALU.add), [r_t], [r_t])
                    yield
                    fw.op("pool", lambda: nc.gpsimd.tensor_tensor(out=r_t[:, 3:4], in0=r_t[:, 2:3], in1=mhalf[:], op=ALU.pow), [r_t, mhalf], [r_t])
                    yield
                    fw.op("dve", lambda: nc.vector.scalar_tensor_tensor(out=b_t[:], in0=o_t[:], scalar=r_t[:, 3:4], in1=gsub[:],
                                                                        op0=ALU.mult, op1=ALU.mult), [o_t, r_t, gsub], [b_t])
                    yield
                    yield
                    fw.op("pe", lambda: nc.tensor.transpose(out=ps_bf[:, 7, j * 128:(j + 1) * 128], in_=b_t[:], identity=ident_b[:]),
                          [b_t, ident_b], [PB[7]])
                    yield
                fw.op("dve", lambda: nc.vector.tensor_copy(out=a_s[:, 0:N], in_=ps_bf[:, 7, 0:N]), [PB[7]], [a_s])
                fw.dma("sp", S[sg]["mixT"][t0 // 512, :, h, 0:N], a_s[:, 0:N], [a_s], [], a_s.ds)
                yield

            hi_n = 0
            qi_n = 0
            ep_n = 0
            pi_n = 0
            epi = None
            for sg, ne, no in segs:
                nt = ne + no
                for h in range(NH):
                    hs = hi_n % 2
                    hi_n += 1
                    fw.dma("sp", kTh[hs][:, 0:nt * 128], S[sg]["kT"][h], [], [kTh[hs]], kTh[hs].ds)
                    fw.dma("sp", vh[hs][:, 0:nt, :], S[sg]["v"][h], [], [vh[hs]], vh[hs].ds)
                    for (t0, n) in blocks512(ne):
                        N = n * 128
                        qs = qi_n % 3
                        qi_n += 1
                        fw.dma("sp", qblk[qs][:, 0:N], S[sg]["qT"][h, :, t0:t0 + N], [], [qblk[qs]], qblk[qs].ds)

                        def scores(kt):
                            sl = kt % 2
                            for c in range(2):
                                fw.op("pe", lambda c=c: nc.tensor.matmul(ps(2 * sl + c, N), lhsT=kTh[hs][c * 64:(c + 1) * 64, kt * 128:(kt + 1) * 128],
                                                                         rhs=qblk[qs][c * 64:(c + 1) * 64, 0:N], start=True, stop=True),
                                      [kTh[hs], qblk[qs]], [PB[2 * sl + c]], signal=(c == 1))

                        scores(0)
                        started = set()
                        for kt in range(nt):
                            if kt + 1 < nt:
                                scores(kt + 1)
                            sl = kt % 2
                            p_t = P[pi_n % NP]
                            pi_n += 1
                            fw.op("act", lambda: nc.scalar.activation(out=p_t[:, :, 0:N], in_=ps_t[:, 2 * sl:2 * sl + 2, 0:N], func=AF.Exp),
                                  [PB[2 * sl], PB[2 * sl + 1]], [p_t])
                            for c in range(2):
                                for j in range(n):
                                    bk, o = acc_loc(c, j)
                                    b = 4 + bk
                                    first = b not in started
                                    started.add(b)
                                    last = (c == 1 and j == n - 1)
                                    fw.op("pe", lambda: nc.tensor.matmul(ps_t[:, b, o:o + 129], lhsT=p_t[:, c, j * 128:(j + 1) * 128], rhs=vh[hs][:, kt, :],
                                                                         start=first, stop=(kt == nt - 1), skip_group_check=True),
                                          [p_t, vh[hs]], [PB[b]], signal=last)
                            if epi is not None and EPI_INTERLEAVE:
                                for _ in range(2):
                                    if next(epi, "end") == "end":
                                        epi = None
                                        break
                        if epi is not None:
                            for _ in epi:
                                pass
                            epi = None
                        a_sb = accS[ep_n % 2]
                        used = sorted({acc_loc(c, j)[0] for c in range(2) for j in range(n)})
                        for bk in used:
                            fw.op("dve", lambda bk=bk: nc.vector.tensor_copy(out=a_sb[:, bk, :], in_=ps_t[:, 4 + bk, 0:396]), [PB[4 + bk]], [a_sb])
                        epi = epilogue(sg, h, t0, n, a_sb, aT[ep_n % 2], ep_n)
                        ep_n += 1
            if epi is not None:
                for _ in epi:
                    pass
            fw.barrier()

        with ExitStack() as es:
            g_bc = bcast_row(es, "ln1g", ROWS["ln1_g"])
            b_bc = bcast_row(es, "ln1b", ROWS["ln1_b"])
            bd_bc = bcast_row(es, "bdn", ROWS["b_down"])
            mx = [fw.tile(es, [128, 16, 512], BF16, "mx%d" % i, dma=True) for i in range(3)]
            hs_ = [fw.tile(es, [128, D], F32, "p6hs%d" % i, dma=True) for i in range(4)]
            h1b = [fw.tile(es, [128, D], BF16, "h1b%d" % i) for i in range(2)]
            hT = [fw.tile(es, [128, 16, 128], BF16, "p6hT%d" % i, dma=True) for i in range(2)]
            lb = [ln_bufs(es, "b%d" % i) for i in range(2)]
            blks = [(sg, bi, t0, n) for sg, ne, no in segs for bi, (t0, n) in enumerate(blocks512(ne))]
            work = [(bi, tl) for bi, (sg, bidx, t0, n) in enumerate(blks) for tl in range(n)]

            def p6_load(wn):
                bi, tl = work[wn]
                sg, bidx, t0, n = blks[bi]
                m_t = mx[bi % 3]
                if tl == 0:
                    fw.dma("sp", m_t[:, :, 0:n * 128], S[sg]["mixT"][bidx, :, :, 0:n * 128], [], [m_t], m_t.ds)
                hs_t = hs_[wn % 4]
                fw.dma("sp", hs_t[:], S[sg]["h0s"][t0 // 128 + tl], [], [hs_t], hs_t.ds)

            def p6_tile(wn):
                bi, tl = work[wn]
                sg, bidx, t0, n = blks[bi]
                m_t = mx[bi % 3]
                s = wn % 2
                ti = t0 // 128 + tl
                hs_t = hs_[wn % 4]
                for nb in range(4):
                    for kc in range(16):
                        fw.op("pe", lambda nb=nb, kc=kc: nc.tensor.matmul(ps(nb), lhsT=m_t[:, kc, tl * 128:(tl + 1) * 128],
                                                                          rhs=wo[:, kc, nb * 512:(nb + 1) * 512], start=(kc == 0), stop=(kc == 15)),
                              [m_t, wo], [PB[nb]], signal=(kc == 15))
                    yield
                    if nb % 2 == 1:
                        hf = nb // 2
                        fw.op("dve", lambda hf=hf: nc.vector.tensor_tensor(out=hs_t[:, hf * 1024:(hf + 1) * 1024].rearrange("p (a b) -> p a b", a=2),
                                                                          in0=ps(2 * hf, nb=2),
                                                                          in1=hs_t[:, hf * 1024:(hf + 1) * 1024].rearrange("p (a b) -> p a b", a=2),
                                                                          op=ALU.add), [PB[2 * hf], PB[2 * hf + 1], hs_t], [hs_t])
                h1_t = lb[s]["xn"]
                yield from layer_norm_tm(es, hs_t[:], [hs_t], lb[s], g_bc, b_bc, h1_t, mul_eng="dve")
                fw.op("act", lambda: nc.scalar.copy(out=h1b[s][:], in_=h1_t[:]), [h1_t], [h1b[s]])
                yield
                fw.op("dve", lambda: nc.vector.scalar_tensor_tensor(out=hs_t[:], in0=h1_t[:], scalar=ALPHA, in1=bd_bc[:],
                                                                    op0=ALU.mult, op1=ALU.add), [h1_t, bd_bc], [hs_t])
                fw.dma("sp", S[sg]["h1s"][ti], hs_t[:], [hs_t], [], hs_t.ds)
                yield
                yield from transpose16(h1b[s], [h1b[s]], hT[s], [4 + 2 * s, 5 + 2 * s], "act")
                fw.dma("sp", S[sg]["h1T"][:, :, ti * 128:(ti + 1) * 128], hT[s][:], [hT[s]], [], hT[s].ds)
                yield

            pipeline([(lambda wn=wn: p6_tile(wn)) for wn in range(len(work))], lag=7, width=(None if '6' in PIPE_PH else 1), loads=p6_load, pf=2)
            fw.barrier()

        es56.close()
        es78 = ExitStack()
        wd = fw.tile(es78, [128, NFC, 1024], BF16, "wdn", dma=True)
        with ExitStack() as es:
            NG = 11
            CPG = NFC // NG
            wgs = [fw.tile(es, [128, 16, 2 * CPG * 128], BF16, "wup%d" % i, dma=True) for i in range(2)]

            def p7_wload(g):
                load_w(es, "wup", w_up, 0, 16, g * CPG * 128, CPG * 128, dst=wgs[g % 2], dstc0=0)
                load_w(es, "wup", w_up, 0, 16, D_FF + g * CPG * 128, CPG * 128, dst=wgs[g % 2], dstc0=CPG * 128)

            p7_wload(0)
            hb = [fw.tile(es, [128, 16, 512], BF16, "p7h%d" % i, dma=True) for i in range(2)]
            mr = [fw.tile(es, [128, 512], F32, "p7m%d" % i, dma=True) for i in range(1)] * 2
            ag = [fw.tile(es, [128, 512], F32, "ag%d" % i) for i in range(2)]
            au = [fw.tile(es, [128, 512], F32, "au%d" % i) for i in range(2)]
            sgt = [fw.tile(es, [128, 512], F32, "sgt%d" % i) for i in range(2)]
            fo = [fw.tile(es, [128, 512], BF16, "fo%d" % i, dma=True) for i in range(3)]
            ccn = fw.tile(es, [128, 88], F32, "ccn")
            fv = colp_t[:, CP_F:CP_F + 440].rearrange("p (c k) -> p c k", k=5)
            fw.op("dve", lambda: nc.vector.tensor_tensor(out=ccn[:].unsqueeze(2), in0=fv[:, :, 1:2], in1=fv[:, :, 2:3], op=ALU.add), [colp_t], [ccn])
            fw.op("dve", lambda: nc.vector.tensor_tensor(out=ccn[:].unsqueeze(2), in0=ccn[:].unsqueeze(2), in1=fv[:, :, 3:4], op=ALU.add), [colp_t, ccn], [ccn])
            fw.op("dve", lambda: nc.vector.tensor_tensor(out=ccn[:].unsqueeze(2), in0=ccn[:].unsqueeze(2), in1=fv[:, :, 0:1], op=ALU.mult), [colp_t, ccn], [ccn])
            fw.op("dve", lambda: nc.vector.tensor_tensor(out=ccn[:].unsqueeze(2), in0=ccn[:].unsqueeze(2), in1=fv[:, :, 4:5], op=ALU.add), [colp_t, ccn], [ccn])

            def fcp(c, k):
                return colp_t[:, CP_F + c * 5 + k:CP_F + c * 5 + k + 1]

            it = 0
            ld = 0
            for g in range(NG):
                wg = wgs[g % 2]
                if g + 1 < NG:
                    p7_wload(g + 1)
                else:
                    load_w(es78, "wdn", w_down, 0, NFC, 0, 1024, dst=wd)
                blks = [(sg, s0, n, ne) for sg, ne, no in segs for (s0, n) in ffn_blocks(cfg.rhi[sg])]

                def p7_load(bi, ld):
                    sg, s0, n, ne = blks[bi]
                    fw.dma("sp", hb[ld % 2][:, :, 0:n], S[sg]["h1T"][:, :, s0:s0 + n], [], [hb[ld % 2]], hb[ld % 2].ds)

                p7_load(0, ld)
                for bi, (sg, s0, n, ne) in enumerate(blks):
                    if bi + 1 < len(blks):
                        p7_load(bi + 1, ld + 1)
                    h_t = hb[ld % 2]
                    m_t = mr[ld % 2]
                    ld += 1
                    Etot = ne * 128
                    edge = (s0 < 16) or (s0 + n > cfg.vhi[sg])
                    M = n - 2
                    if edge:
                        fw.dma("sp", m_t[:, 0:n], MROW[sg][0:1, s0:s0 + n].partition_broadcast(128), [], [m_t], m_t.ds)
                    for jj in range(CPG):
                        cg = g * CPG + jj
                        cu = NFC + cg
                        bg_, bu_ = (it % 2) * 2, (it % 2) * 2 + 1
                        for (col0, bank) in ((jj * 128, bg_), (CPG * 128 + jj * 128, bu_)):
                            for kc in range(16):
                                fw.op("pe", lambda col0=col0, bank=bank, kc=kc: nc.tensor.matmul(ps(bank, n), lhsT=wg[:, kc, col0:col0 + 128],
                                                                                                 rhs=h_t[:, kc, 0:n], start=(kc == 0), stop=(kc == 15)),
                                      [wg, h_t], [PB[bank]], signal=(kc == 15))
                        a_g, a_u, s_t, f_t = ag[it % 2], au[it % 2], sgt[it % 2], fo[it % 3]
                        for (cc_, bank, acc) in ((cg, bg_, a_g), (cu, bu_, a_u)):
                            if edge:
                                u_t = sgt[it % 2]
                                fw.op("dve", lambda: nc.vector.scalar_tensor_tensor(out=u_t[:, 0:n], in0=ps(bank, n), scalar=fcp(cc_, 0), in1=m_t[:, 0:n],
                                                                                    op0=ALU.add, op1=ALU.mult), [PB[bank], colp_t, m_t], [u_t])
                                src0, src1, src2 = u_t[:, 0:M], u_t[:, 1:M + 1], u_t[:, 2:M + 2]
                                sb = [u_t]
                            else:
                                pp = ps(bank, n)
                                src0, src1, src2 = pp[:, 0:M], pp[:, 1:M + 1], pp[:, 2:M + 2]
                                sb = [PB[bank]]
                            fw.op("dve", lambda: nc.vector.tensor_scalar(out=acc[:, 0:M], in0=src0, scalar1=fcp(cc_, 1), scalar2=None, op0=ALU.mult),
                                  sb + [colp_t], [acc])
                            fw.op("dve", lambda: nc.vector.scalar_tensor_tensor(out=acc[:, 0:M], in0=src1, scalar=fcp(cc_, 2), in1=acc[:, 0:M],
                                                                                op0=ALU.mult, op1=ALU.add), sb + [colp_t, acc], [acc])
                            fw.op("dve", lambda: nc.vector.scalar_tensor_tensor(out=acc[:, 0:M], in0=src2, scalar=fcp(cc_, 3), in1=acc[:, 0:M],
                                                                                op0=ALU.mult, op1=ALU.add), sb + [colp_t, acc], [acc])
                        bias_g = fcp(cg, 4) if edge else ccn[:, cg:cg + 1]
                        bias_u = fcp(cu, 4) if edge else ccn[:, cu:cu + 1]
                        fw.op("act", lambda: nc.scalar.activation(out=s_t[:, 0:M], in_=a_g[:, 0:M], func=AF.Silu, bias=bias_g, scale=1.0),
                              [a_g, colp_t, ccn], [s_t])
                        fw.op("dve", lambda: nc.vector.scalar_tensor_tensor(out=f_t[:, 0:M], in0=a_u[:, 0:M], scalar=bias_u, in1=s_t[:, 0:M],
                                                                            op0=ALU.add, op1=ALU.mult), [a_u, colp_t, ccn, s_t], [f_t])
                        fw.dma("sp", S[sg]["fT"][:, cg, s0 + 1:s0 + 1 + M], f_t[:, 0:M], [f_t], [], f_t.ds)
                        it += 1
            fw.barrier()

        with ExitStack() as es:
            g_bc = bcast_row(es, "ln2g", ROWS["ln2_g"])
            b_bc = bcast_row(es, "ln2b", ROWS["ln2_b"])
            fb = [fw.tile(es, [128, NFC, 128], BF16, "fb%d" % i, dma=True) for i in range(4)]
            pa = [fw.tile(es, [128, 1024], F32, "pa%d" % i, dma=True) for i in range(3)]
            hs_ = [fw.tile(es, [128, D], F32, "p8hs%d" % i, dma=True) for i in range(3)]
            lb = [ln_bufs(es, "c%d" % i, dma=True) for i in range(2)]
            yo = [lb[i]["xn"] for i in range(2)]
            blks = []
            for sg, ne, no in segs:
                t = 0
                while t < ne:
                    n = min(2, ne - t)
                    blks.append((sg, t, n))
                    t += n
            work = [(bi, tl) for bi, (sg, t, n) in enumerate(blks) for tl in range(n)]
            for half in range(2):
                if half == 1:
                    load_w(es, "wdn", w_down, 0, NFC, half * 1024, 1024, dst=wd)

                def p8_load(wn, half=half):
                    bi, tl = work[wn]
                    sg, t, n = blks[bi]
                    ti = t + tl
                    f_t = fb[wn % 4]
                    fw.dma("sp", f_t[:], S[sg]["fT"][:, :, ti * 128:(ti + 1) * 128], [], [f_t], f_t.ds)
                    if half == 1:
                        fw.dma("sp", hs_[wn % 3][:], S[sg]["h1s"][ti], [], [hs_[wn % 3]], hs_[wn % 3].ds)
                        fw.dma("sp", pa[wn % 3][:], S[sg]["ffa"][ti], [], [pa[wn % 3]], pa[wn % 3].ds)

                def p8_tile(wn, half=half):
                    bi, tl = work[wn]
                    sg, t, n = blks[bi]
                    f_t = fb[wn % 4]
                    s = wn % 2
                    ti = t + tl
                    b0 = 2 * s + (4 if half else 0)
                    hs_t, pa_t = hs_[wn % 3], pa[wn % 3]
                    for nb in range(2):
                        for kc in range(NFC):
                            fw.op("pe", lambda nb=nb, kc=kc: nc.tensor.matmul(ps(b0 + nb), lhsT=f_t[:, kc, :],
                                                                              rhs=wd[:, kc, nb * 512:(nb + 1) * 512], start=(kc == 0), stop=(kc == NFC - 1)),
                                  [f_t, wd], [PB[b0 + nb]], signal=(kc == NFC - 1))
                        yield
                    if half == 0:
                        fw.op("act", lambda: nc.scalar.copy(out=pa_t[:].rearrange("p (a b) -> p a b", a=2), in_=ps(b0, nb=2)),
                              [PB[b0], PB[b0 + 1]], [pa_t])
                        fw.dma("sp", S[sg]["ffa"][ti], pa_t[:], [pa_t], [], pa_t.ds)
                        yield
                    else:
                        fw.op("pool", lambda: nc.gpsimd.tensor_tensor(out=hs_t[:, 0:1024], in0=pa_t[:], in1=hs_t[:, 0:1024], op=ALU.add),
                              [pa_t, hs_t], [hs_t])
                        fw.op("dve", lambda: nc.vector.tensor_tensor(out=hs_t[:, 1024:2048].rearrange("p (a b) -> p a b", a=2), in0=ps(b0, nb=2),
                                                                    in1=hs_t[:, 1024:2048].rearrange("p (a b) -> p a b", a=2), op=ALU.add),
                              [PB[b0], PB[b0 + 1], hs_t], [hs_t])
                        yield
                        yield from layer_norm_tm(es, hs_t[:], [hs_t], lb[s], g_bc, b_bc, yo[s])
                        fw.dma("sp", Y[sg][ti * 128:(ti + 1) * 128, :], yo[s][:], [yo[s]], [], yo[s].ds)
                        yield

                pipeline([(lambda wn=wn: p8_tile(wn)) for wn in range(len(work))], lag=(2 if half == 0 else 4), width=(None if '8' in PIPE_PH else 1), loads=p8_load, pf=1)
                fw.barrier()
        es78.close()
        fw.barrier()
    build.nops = fw.nops
    return nc


def geometry(S_p, S_s):
    L = S_p + N_META
    W = L // 2
    E = -(-(W + 32) // 128) * 128
    O_valid = max(L - (E - 16), S_p // 2 - 16)
    O = -(-O_valid // 128) * 128
    L2 = S_s + N_META
    E2 = -(-(L2 + 32) // 128) * 128
    return dict(L=L, W=W, E=E, O=O, O_valid=O_valid, L2=L2, E2=E2)


def rope_table(pos):
    rot = HD // 4
    inv = ROPE_THETA ** (-np.arange(0, rot, 2, dtype=np.float32) / rot)
    ang = pos.astype(np.float32)[:, None] * inv[None, :]
    cos, sin = np.cos(ang).astype(np.float32), np.sin(ang).astype(np.float32)
    return np.concatenate([np.tile(cos, (1, 16)), np.tile(sin, (1, 16))], axis=1).astype(np.float32)


_CACHE = {}


def kernel(x_prompt, x_sample, meta_tokens, ln_emb_g, ln_emb_b, w_in, b_in, lambda_q1, lambda_k1, lambda_q2, lambda_k2,
           subln_g, conv_w, conv_b, conv_ln_g, conv_ln_b, w_out, b_out, ln1_g, ln1_b, w_up, b_up, ffn_conv_w, ffn_conv_b,
           w_down, b_down, ln2_g, ln2_b, _debug=(), _return_raw=False):
    f = lambda a: np.ascontiguousarray(np.asarray(a, dtype=np.float32))
    x_prompt, x_sample, meta = f(x_prompt), f(x_sample), f(meta_tokens)
    B, S_p, _ = x_prompt.shape
    B2, S_s, _ = x_sample.shape
    assert B * 2 == NCORES and B2 == NCORES
    G = geometry(S_p, S_s)
    L, E, O, L2, E2 = G["L"], G["E"], G["O"], G["L2"], G["E2"]
    cfg = Cfg(E // 128, O // 128, E2 // 128, L2, debug=_debug, half_p=S_p // 2)
    key = (E, O, E2, tuple(sorted(_debug)))
    if key not in _CACHE:
        _CACHE[key] = build(cfg)
    nc = _CACHE[key]

    w_in0, w_out0, w_up0, w_down0 = f(w_in)[0], f(w_out)[0], f(w_up)[0], f(w_down)[0]
    b_in0 = f(b_in)[0]
    rowp = np.zeros((16, D), np.float32)
    for i, a in enumerate([ln_emb_g, ln_emb_b, f(b_out)[0], f(ln1_g)[0], f(ln1_b)[0], f(b_down)[0], f(ln2_g)[0], f(ln2_b)[0]]):
        rowp[i] = f(a)
    b_qkv = b_in0[None, :3072].copy()
    lamv = np.stack([f(lambda_q1)[0], f(lambda_k1)[0], f(lambda_q2)[0], f(lambda_k2)[0]]).astype(np.float32)
    subg = f(subln_g)[0][None, :].copy()
    colp = np.zeros((128, 16 + 8 * 31 + 8 * 3 + 88 * 5 + 32), np.float32)
    colp[:, 0:16] = b_in0[3072:].reshape(16, 128).T
    cw = f(conv_w)[0]
    colp[:, 16:16 + 248] = cw.reshape(31, 8, 128).transpose(2, 1, 0).reshape(128, 248)
    c3 = np.stack([f(conv_b)[0], f(conv_ln_g)[0], f(conv_ln_b)[0]])
    colp[:, 264:264 + 24] = c3.reshape(3, 8, 128).transpose(2, 1, 0).reshape(128, 24)
    f5 = np.concatenate([f(b_up)[0][None], f(ffn_conv_w)[0], f(ffn_conv_b)[0][None]], axis=0)
    colp[:, 288:288 + 440] = f5.reshape(5, 88, 128).transpose(2, 1, 0).reshape(128, 440)
    colp[:, 728:744] = f(ln_emb_g).reshape(16, 128).T
    colp[:, 744:760] = f(ln_emb_b).reshape(16, 128).T
    ident = np.eye(128, dtype=np.float32)

    in_maps = []
    info = []
    for c in range(NCORES):
        b, half = c // 2, c % 2
        seq = np.concatenate([meta, x_prompt[b]], axis=0)
        start = -16 if half == 0 else S_p // 2 - 16
        pos_e = np.arange(start, start + E)
        val_e = (pos_e >= 0) & (pos_e < L)
        xe = np.zeros((E + O, D), np.float32)
        xe[:E][val_e] = seq[pos_e[val_e]]
        other = np.arange(E - 16, L) if half == 0 else np.arange(0, S_p // 2 - 16)
        pos_o = np.zeros(O, np.int64)
        val_o = np.zeros(O, bool)
        pos_o[:len(other)] = other
        val_o[:len(other)] = True
        xe[E:E + len(other)] = seq[other]
        pos_all = np.concatenate([np.where(val_e, pos_e, 0), pos_o])
        val_all = np.concatenate([val_e, val_o])
        seq2 = np.concatenate([meta, x_sample[c]], axis=0)
        pos_s = np.arange(-16, -16 + E2)
        val_s = (pos_s >= 0) & (pos_s < L2)
        xs = np.zeros((E2, D), np.float32)
        xs[val_s] = seq2[pos_s[val_s]]
        in_maps.append({
            "xe_p": xe, "xe_s": xs,
            "mask8_p": np.repeat(val_all[:, None], 8, axis=1).astype(np.float32),
            "mask8_s": np.repeat(val_s[:, None], 8, axis=1).astype(np.float32),
            "mrow_p": val_e[None, :].astype(np.float32), "mrow_s": val_s[None, :].astype(np.float32),
            "rope_p": rope_table(pos_all), "rope_s": rope_table(np.where(val_s, pos_s, 0)),
            "w_in": w_in0, "w_out": w_out0, "w_up": w_up0, "w_down": w_down0, "ident": ident, "rowp": rowp,
            "b_qkv": b_qkv, "lamv": lamv, "subg": subg, "colp": colp,
        })
        r0 = 32
        info.append((b, half, r0))
    res = run_bass_kernel_spmd(nc, in_maps, core_ids=list(range(NCORES)))
    if _return_raw:
        return res.results, info, G
    y_p = np.empty((B, S_p, D), np.float32)
    y_s = np.empty((B2, S_s, D), np.float32)
    for c in range(NCORES):
        b, half, r0 = info[c]
        r = res.results[c]
        y_p[b, half * (S_p // 2):(half + 1) * (S_p // 2)] = r["y_p"][r0:r0 + S_p // 2]
        y_s[c] = r["y_s"][32:32 + S_s]
    return (y_p, y_s)
```

```python
import math
from contextlib import ExitStack
import numpy as np
import concourse.bass as bass
import concourse.mybir as mybir
from concourse.bass_utils import run_bass_kernel_spmd

F32 = mybir.dt.float32
BF16 = mybir.dt.bfloat16
AF = mybir.ActivationFunctionType
ALU = mybir.AluOpType
AX = mybir.AxisListType

D = 2048
N_META = 16
NH = 8
HD = 64
D_ATT = 1024
D_QK = 1024
D_CONV = 1024
D_IN = 5120
CW = 31
D_FF = 5632
NFC = D_FF // 128
LN_EPS = 1e-5
ALPHA = 2.0 ** 0.25
LAM_INIT = 0.8 - 0.6 * math.exp(0.0)
ROPE_THETA = 500000.0
NCORES = 8
import os
PIPE_WIDTH = int(os.environ.get('K_PIPE', '2'))
PIPE_PH = os.environ.get('K_PH', '1268')
EPI_INTERLEAVE = int(os.environ.get('K_EPI', '1'))


class Buf:
    __slots__ = ("name", "lw", "rd")

    def __init__(self, name=""):
        self.name = name
        self.lw = None
        self.rd = {}


class DSem:
    def __init__(self, h, key):
        self.h = h
        self.key = key
        self.count = 0


class T:
    def __init__(self, t, name, ds=None):
        self.t = t
        self.b = Buf(name)
        self.ds = ds

    def __getitem__(self, idx):
        return self.t[idx]


class FW:
    ENG = ("pe", "act", "dve", "pool", "sp")
    STRICT = True

    def __init__(self, nc, es):
        self.nc = nc
        self.es = es
        self.eng = {"pe": nc.tensor, "act": nc.scalar, "dve": nc.vector, "pool": nc.gpsimd, "sp": nc.sync}
        self.sem = {e: es.enter_context(nc.semaphore("s_" + e)) for e in ("pe", "act", "dve", "pool")}
        self.cnt = {e: 0 for e in self.sem}
        self.waited = {e: {} for e in self.ENG}
        self.dsems = []
        self.nops = 0

    def new_dsem(self, name):
        h = self.es.enter_context(self.nc.semaphore("d_" + name))
        d = DSem(h, "d_" + name + str(len(self.dsems)))
        self.dsems.append(d)
        return d

    def tile(self, es, shape, dtype, name, dma=False, psum=False):
        self.ntile = getattr(self, "ntile", 0) + 1
        name = "t%d_%s" % (self.ntile, name)
        if psum:
            t = es.enter_context(self.nc.psum_tensor(name, list(shape), dtype))
        else:
            t = es.enter_context(self.nc.sbuf_tensor(name, list(shape), dtype))
        return T(t, name, self.new_dsem(name) if dma else None)

    def _need(self, e, dep):
        key, h, val = dep
        if self.waited[e].get(key, 0) >= val:
            return
        if key == e:
            if val > self.cnt[e]:
                assert FW.STRICT, "same-engine wait on future value"
                return
        elif key in self.cnt:
            assert val <= self.cnt[key], "wait on a not-yet-emitted signal (deadlock hazard)"
        self.eng[e].wait_ge(h, val)
        self.waited[e][key] = val

    def _deps(self, e, reads, writes, own):
        for t in reads:
            b = t.b
            if b.lw is not None:
                self._need(e, b.lw)
        strict = FW.STRICT
        for t in writes:
            b = t.b
            if b.lw is not None and (strict or b.lw[0] != own):
                self._need(e, b.lw)
            for key, (h, val) in b.rd.items():
                if strict or key != own:
                    self._need(e, (key, h, val))

    def _record(self, dep, reads, writes):
        key, h, val = dep
        for t in reads:
            t.b.rd[key] = (h, val)
        for t in writes:
            t.b.lw = dep
            t.b.rd = {}

    def op(self, e, fn, reads=(), writes=(), signal=True):
        self._deps(e, reads, writes, e)
        inst = fn()
        self.nops += 1
        if signal:
            self.cnt[e] += 1
            inst.then_inc(self.sem[e], 1)
            v = self.cnt[e]
        else:
            v = self.cnt[e] + 1
        self._record((e, self.sem[e], v), reads, writes)

    def dma(self, q, out, in_, reads, writes, ds, **kw):
        self._deps(q, reads, writes, None)
        self._need(q, (ds.key, ds.h, ds.count))
        inst = self.eng[q].dma_start(out=out, in_=in_, **kw)
        self.nops += 1
        ds.count += 16
        inst.then_inc(ds.h, 16)
        self._record((ds.key, ds.h, ds.count), reads, writes)

    def barrier(self):
        for e in self.ENG:
            for k in self.sem:
                if self.cnt[k] > 0:
                    self._need(e, (k, self.sem[k], self.cnt[k])) if k != e else None
            for d in self.dsems:
                if d.count > 0:
                    self._need(e, (d.key, d.h, d.count))


class Cfg:
    def __init__(self, NE, NO, NE2, L2, debug=(), half_p=None):
        self.NE, self.NO, self.NE2 = NE, NO, NE2
        self.vhi = {"p": 32 + half_p, "s": 16 + L2}
        self.rhi = {"p": 32 + half_p, "s": 16 + L2}
        self.E, self.O, self.E2 = NE * 128, NO * 128, NE2 * 128
        self.debug = set(debug)


def blocks512(ntiles):
    out = []
    t = 0
    while t < ntiles:
        n = min(4, ntiles - t)
        out.append((t * 128, n))
        t += n
    return out


def ffn_blocks(rhi):
    out = []
    s = 31
    end = rhi + 1
    while s + 1 < rhi:
        n = min(512, end - s)
        out.append((s, n))
        s += n - 2
    return out


def build(cfg):
    nc = bass.Bass("TRN2", target_bir_lowering=False)
    NE, NO, NE2 = cfg.NE, cfg.NO, cfg.NE2
    E, O, E2 = cfg.E, cfg.O, cfg.E2
    segs = [("p", NE, NO), ("s", NE2, 0)]

    def din(name, shape, dt=F32):
        return nc.dram_tensor(name, list(shape), dt, kind="ExternalInput").ap()

    def dscr(name, shape, dt):
        kind = "ExternalOutput" if name in cfg.debug else "Internal"
        return nc.dram_tensor(name, list(shape), dt, kind=kind).ap()

    X = {"p": din("xe_p", [E + O, D]), "s": din("xe_s", [E2, D])}
    MASK8 = {"p": din("mask8_p", [E + O, 8]), "s": din("mask8_s", [E2, 8])}
    MROW = {"p": din("mrow_p", [1, E]), "s": din("mrow_s", [1, E2])}
    ROPE = {"p": din("rope_p", [E + O, 256]), "s": din("rope_s", [E2, 256])}
    w_in = din("w_in", [D, D_IN])
    w_out = din("w_out", [D, D])
    w_up = din("w_up", [D, 2 * D_FF])
    w_down = din("w_down", [D_FF, D])
    ident_in = din("ident", [128, 128])
    rowp = din("rowp", [16, D])
    b_qkv = din("b_qkv", [1, 3072])
    lamv = din("lamv", [4, 64])
    subg = din("subg", [1, 128])
    colp = din("colp", [128, 16 + 8 * 31 + 8 * 3 + 88 * 5 + 32])
    ROWS = {"emb_g": 0, "emb_b": 1, "b_out": 2, "ln1_g": 3, "ln1_b": 4, "b_down": 5, "ln2_g": 6, "ln2_b": 7}

    Y = {"p": nc.dram_tensor("y_p", [E, D], F32, kind="ExternalOutput").ap(),
         "s": nc.dram_tensor("y_s", [E2, D], F32, kind="ExternalOutput").ap()}

    S = {}
    for sg, ne, no in segs:
        nt = ne + no
        nb = len(blocks512(ne))
        S[sg] = dict(
            h0s=dscr("h0s_" + sg, [ne, 128, D], F32),
            h0T=dscr("h0T_" + sg, [nt, 128, 16, 128], BF16),
            qT=dscr("qT_" + sg, [NH, 128, ne * 128], BF16),
            kT=dscr("kT_" + sg, [NH, 128, nt * 128], BF16),
            v=dscr("v_" + sg, [NH, 128, nt, 129], BF16),
            c=dscr("c_" + sg, [8, 128, ne * 128], BF16),
            mixT=dscr("mixT_" + sg, [nb, 128, 16, 512], BF16),
            h1s=dscr("h1s_" + sg, [ne, 128, D], F32),
            h1T=dscr("h1T_" + sg, [128, 16, ne * 128], BF16),
            fT=dscr("fT_" + sg, [128, NFC, ne * 128], BF16),
            ffa=dscr("ffa_" + sg, [ne, 128, 1024], F32),
        )

    with ExitStack() as es0:
        fw = FW(nc, es0)
        ps_t = es0.enter_context(nc.psum_tensor("ps", [128, 8, 512], F32))
        ps_bf = ps_t.bitcast(BF16)
        PB = [T(None, "psb%d" % i) for i in range(8)]

        def ps(b, n=512, nb=1):
            if nb == 1:
                return ps_t[:, b, 0:n]
            return ps_t[:, b:b + nb, :]

        ident_f = fw.tile(es0, [128, 128], F32, "ident_f", dma=True)
        ident_b = fw.tile(es0, [128, 128], BF16, "ident_b")
        colp_t = fw.tile(es0, [128, colp.shape[1]], F32, "colp")
        neglam = fw.tile(es0, [128, 1], F32, "neglam")
        gsub = fw.tile(es0, [128, 128], F32, "gsub")
        colp_t.ds = ident_f.ds
        gsub.ds = ident_f.ds
        mhalf = fw.tile(es0, [128, 1], F32, "mhalf")
        epsc = fw.tile(es0, [128, 1], F32, "epsc")
        fw.dma("sp", ident_f[:], ident_in, [], [ident_f], ident_f.ds)
        fw.dma("sp", colp_t[:], colp, [], [colp_t], colp_t.ds)
        fw.dma("sp", gsub[:], subg.partition_broadcast(128), [], [gsub], gsub.ds)
        fw.op("dve", lambda: nc.vector.tensor_copy(out=ident_b[:], in_=ident_f[:]), [ident_f], [ident_b])
        fw.op("dve", lambda: nc.vector.memset(mhalf[:], -0.5), [], [mhalf])
        fw.op("dve", lambda: nc.vector.memset(epsc[:], LN_EPS), [], [epsc])
        fw.op("dve", lambda: nc.vector.tensor_scalar(out=gsub[:], in0=gsub[:], scalar1=1.0 - LAM_INIT, scalar2=None,
                                                    op0=ALU.mult), [gsub], [gsub])
        CP_BU = 0
        CP_CW = 16
        CP_C3 = CP_CW + 8 * 31
        CP_F = CP_C3 + 8 * 3
        CP_EG = CP_F + 88 * 5
        CP_EB = CP_EG + 16

        def cp(i):
            return colp_t[:, i:i + 1]

        if True:
            es = es0
            lt = fw.tile(es, [128, 4, 64], F32, "lamt")
            lt.ds = ident_f.ds
            lp = fw.tile(es, [128, 2, 64], F32, "lamp")
            ls = fw.tile(es, [128, 2], F32, "lams")
            le = fw.tile(es, [128, 2], F32, "lame")
            fw.dma("sp", lt[:].rearrange("p a b -> p (a b)"), lamv.rearrange("(o a) b -> o (a b)", o=1).partition_broadcast(128),
                   [], [lt], lt.ds)
            fw.op("dve", lambda: nc.vector.tensor_tensor(out=lp[:, 0, :], in0=lt[:, 0, :], in1=lt[:, 1, :], op=ALU.mult), [lt], [lp])
            fw.op("dve", lambda: nc.vector.tensor_tensor(out=lp[:, 1, :], in0=lt[:, 2, :], in1=lt[:, 3, :], op=ALU.mult), [lt], [lp])
            fw.op("dve", lambda: nc.vector.tensor_reduce(out=ls[:], in_=lp[:], axis=AX.X, op=ALU.add), [lp], [ls])
            fw.op("act", lambda: nc.scalar.activation(out=le[:], in_=ls[:], func=AF.Exp), [ls], [le])
            fw.op("dve", lambda: nc.vector.tensor_tensor(out=neglam[:], in0=le[:, 1:2], in1=le[:, 0:1], op=ALU.subtract), [le], [neglam])
            fw.op("dve", lambda: nc.vector.tensor_scalar(out=neglam[:], in0=neglam[:], scalar1=-LAM_INIT, scalar2=None,
                                                        op0=ALU.add), [neglam], [neglam])

        def load_w(es, name, src, r0, nk, c0, ncols, dst=None, dstc0=0):
            if dst is None:
                dst = fw.tile(es, [128, nk, ncols], BF16, name, dma=True)
            KG = int(os.environ.get('K_WKG', '16'))
            c = 0
            while c < ncols:
                n = min(1024, ncols - c)
                k0 = 0
                while k0 < nk:
                    kk = min(KG, nk - k0)
                    fw.dma("pool", dst[:, k0:k0 + kk, dstc0 + c:dstc0 + c + n],
                           src[r0 + k0 * 128:r0 + (k0 + kk) * 128, c0 + c:c0 + c + n].rearrange("(k p) c -> p k c", p=128),
                           [], [dst], dst.ds, max_dma_last_dim=4096)
                    k0 += kk
                c += n
            return dst

        def bcast_row(es, name, r):
            t = fw.tile(es, [128, D], F32, name, dma=True)
            fw.dma("sp", t[:], rowp[r:r + 1, :].partition_broadcast(128), [], [t], t.ds)
            return t

        def layer_norm_tm(es_unused, src_ap, src_bufs, bufs, g_bc, b_bc, out_t, mul_eng="pool"):
            st, mv, rs, nm, xn = bufs["st"], bufs["mv"], bufs["rs"], bufs["nm"], bufs["xn"]
            for i in range(4):
                fw.op("dve", lambda i=i: nc.vector.bn_stats(out=st[:, i * 6:(i + 1) * 6], in_=src_ap[:, i * 512:(i + 1) * 512]),
                      src_bufs, [st])
            yield
            fw.op("dve", lambda: nc.vector.bn_aggr(out=mv[:], in_=st[:]), [st], [mv])
            yield
            fw.op("dve", lambda: nc.vector.tensor_scalar(out=rs[:], in0=mv[:, 1:2], scalar1=LN_EPS, scalar2=None, op0=ALU.add),
                  [mv], [rs])
            yield
            fw.op("pool", lambda: nc.gpsimd.tensor_tensor(out=rs[:], in0=rs[:], in1=mhalf[:], op=ALU.pow), [rs, mhalf], [rs])
            yield
            fw.op("dve", lambda: nc.vector.tensor_scalar(out=nm[:], in0=mv[:, 0:1], scalar1=rs[:], scalar2=-1.0,
                                                        op0=ALU.mult, op1=ALU.mult), [mv, rs], [nm])
            yield
            fw.op("act", lambda: nc.scalar.activation(out=xn[:], in_=src_ap, func=AF.Identity, bias=nm[:], scale=rs[:]),
                  list(src_bufs) + [nm, rs], [xn])
            yield
            me = fw.eng[mul_eng]
            fw.op(mul_eng, lambda: me.tensor_tensor(out=xn[:], in0=xn[:], in1=g_bc[:], op=ALU.mult), [xn, g_bc], [xn])
            yield
            fw.op("dve", lambda: nc.vector.tensor_tensor(out=out_t[:], in0=xn[:], in1=b_bc[:], op=ALU.add), [xn, b_bc], [out_t])
            yield

        def ln_bufs(es, tag, dma=False):
            return dict(st=fw.tile(es, [128, 24], F32, "st" + tag), mv=fw.tile(es, [128, 2], F32, "mv" + tag),
                        rs=fw.tile(es, [128, 1], F32, "rs" + tag), nm=fw.tile(es, [128, 1], F32, "nm" + tag),
                        xn=fw.tile(es, [128, D], F32, "xn" + tag, dma=dma))

        def transpose16(src_bf, src_bufs, dstT, pbanks, evac_eng):
            for half in range(2):
                pb = pbanks[half]
                for k in range(8):
                    kc = half * 8 + k
                    fw.op("pe", lambda kc=kc, k=k, pb=pb: nc.tensor.transpose(out=ps_bf[:, pb, k * 128:(k + 1) * 128],
                                                                             in_=src_bf[:, kc * 128:(kc + 1) * 128], identity=ident_b[:]),
                          list(src_bufs) + [ident_b], [PB[pb]], signal=(k == 7))
                yield
                if evac_eng == "act":
                    fw.op("act", lambda half=half, pb=pb: nc.scalar.copy(out=dstT[:, half * 8:(half + 1) * 8, :].rearrange("p a b -> p (a b)"),
                                                                         in_=ps_bf[:, pb, :]), [PB[pb]], [dstT])
                else:
                    fw.op("dve", lambda half=half, pb=pb: nc.vector.tensor_copy(out=dstT[:, half * 8:(half + 1) * 8, :].rearrange("p a b -> p (a b)"),
                                                                               in_=ps_bf[:, pb, :]), [PB[pb]], [dstT])
                yield

        def pipeline(factories, lag, width=None, loads=None, pf=0):
            width = PIPE_WIDTH if width is None else width
            nitems = len(factories)
            active = []
            state = {"next": 0}
            if loads is not None:
                for n in range(min(pf, nitems)):
                    loads(n)

            def start():
                n = state["next"]
                state["next"] += 1
                if loads is not None and n + pf < nitems:
                    loads(n + pf)
                active.append([factories[n](), 0])

            start()
            while active or state["next"] < nitems:
                for g in list(active):
                    try:
                        next(g[0])
                        g[1] += 1
                    except StopIteration:
                        active.remove(g)
                if state["next"] < nitems and len(active) < width and (not active or active[-1][1] >= lag):
                    start()

        es12 = ExitStack()
        wq = fw.tile(es12, [128, 16, 3072], BF16, "wqkv", dma=True)
        with ExitStack() as es:
            g_bc = bcast_row(es, "embg", ROWS["emb_g"])
            b_bc = bcast_row(es, "embb", ROWS["emb_b"])
            bo_bc = bcast_row(es, "bout", ROWS["b_out"])
            fw.op("dve", lambda: nc.vector.tensor_scalar(out=g_bc[:], in0=g_bc[:], scalar1=ALPHA, scalar2=None, op0=ALU.mult), [g_bc], [g_bc])
            fw.op("dve", lambda: nc.vector.scalar_tensor_tensor(out=b_bc[:], in0=b_bc[:], scalar=ALPHA, in1=bo_bc[:],
                                                                op0=ALU.mult, op1=ALU.add), [b_bc, bo_bc], [b_bc])
            W1 = int(os.environ.get('K_W1', '3'))
            NX = W1 + 2
            xt = [fw.tile(es, [128, D], F32, "xt%d" % i, dma=True) for i in range(NX)]
            h0s = [fw.tile(es, [128, D], F32, "h0s%d" % i, dma=True) for i in range(W1)]
            hT = [fw.tile(es, [128, 16, 128], BF16, "hT%d" % i, dma=True) for i in range(W1)]
            sm = [dict(st=fw.tile(es, [128, 24], F32, "st1_%d" % i), mv=fw.tile(es, [128, 2], F32, "mv1_%d" % i),
                       rs=fw.tile(es, [128, 1], F32, "rs1_%d" % i), nm=fw.tile(es, [128, 1], F32, "nm1_%d" % i)) for i in range(W1)]
            tiles = [(sg, i, i < ne) for sg, ne, no in segs for i in range(ne + no)]

            def p1_load(n):
                sg, i, isE = tiles[n]
                x_t = xt[n % NX]
                fw.dma("sp", x_t[:], X[sg][i * 128:(i + 1) * 128, :], [], [x_t], x_t.ds)

            def p1_tile(n):
                sg, i, isE = tiles[n]
                s = n % W1
                if n == min(6, len(tiles) - 1):
                    load_w(es12, "wqkv", w_in, 0, 16, 0, 3072, dst=wq)
                xn = xt[n % NX]
                st, mv, rs, nm = sm[s]["st"], sm[s]["mv"], sm[s]["rs"], sm[s]["nm"]
                for q4 in range(4):
                    fw.op("dve", lambda q4=q4: nc.vector.bn_stats(out=st[:, q4 * 6:(q4 + 1) * 6], in_=xn[:, q4 * 512:(q4 + 1) * 512]), [xn], [st])
                yield
                fw.op("dve", lambda: nc.vector.bn_aggr(out=mv[:], in_=st[:]), [st], [mv])
                yield
                fw.op("dve", lambda: nc.vector.tensor_scalar(out=rs[:], in0=mv[:, 1:2], scalar1=LN_EPS, scalar2=None, op0=ALU.add), [mv], [rs])
                yield
                fw.op("pool", lambda: nc.gpsimd.tensor_tensor(out=rs[:], in0=rs[:], in1=mhalf[:], op=ALU.pow), [rs, mhalf], [rs])
                yield
                fw.op("dve", lambda: nc.vector.tensor_scalar(out=nm[:], in0=mv[:, 0:1], scalar1=rs[:], scalar2=-1.0,
                                                            op0=ALU.mult, op1=ALU.mult), [mv, rs], [nm])
                yield
                fw.op("act", lambda: nc.scalar.activation(out=xn[:], in_=xn[:], func=AF.Identity, bias=nm[:], scale=rs[:]), [xn, nm, rs], [xn])
                yield
                pb0 = 4 * (n % 2)
                for half in range(2):
                    for k4 in range(2):
                        bank = pb0 + 2 * half + k4
                        for k in range(4):
                            kc = half * 8 + k4 * 4 + k
                            fw.op("pe", lambda kc=kc, k=k, bank=bank: nc.tensor.transpose(out=ps_t[:, bank, k * 128:(k + 1) * 128],
                                                                                         in_=xn[:, kc * 128:(kc + 1) * 128], identity=ident_f[:]),
                                  [xn, ident_f], [PB[bank]], signal=(k == 3))
                        yield
                        for k in range(4):
                            kc = half * 8 + k4 * 4 + k
                            if k % 2 == 0:
                                fw.op("dve", lambda kc=kc, k=k, bank=bank: nc.vector.tensor_scalar(
                                    out=hT[s][:, kc, :], in0=ps_t[:, bank, k * 128:(k + 1) * 128], scalar1=cp(CP_EG + kc), scalar2=cp(CP_EB + kc),
                                    op0=ALU.mult, op1=ALU.add), [PB[bank], colp_t], [hT[s]])
                            else:
                                fw.op("act", lambda kc=kc, k=k, bank=bank: nc.scalar.activation(
                                    out=hT[s][:, kc, :], in_=ps_t[:, bank, k * 128:(k + 1) * 128], func=AF.Identity, bias=cp(CP_EB + kc), scale=cp(CP_EG + kc)),
                                    [PB[bank], colp_t], [hT[s]])
                        yield
                fw.dma("sp", S[sg]["h0T"][i], hT[s][:], [hT[s]], [], hT[s].ds)
                yield
                if isE:
                    fw.op("pool", lambda: nc.gpsimd.tensor_tensor(out=h0s[s][:], in0=xn[:], in1=g_bc[:], op=ALU.mult), [xn, g_bc], [h0s[s]])
                    yield
                    fw.op("pool", lambda: nc.gpsimd.tensor_tensor(out=h0s[s][:], in0=h0s[s][:], in1=b_bc[:], op=ALU.add), [h0s[s], b_bc], [h0s[s]])
                    fw.dma("sp", S[sg]["h0s"][i], h0s[s][:], [h0s[s]], [], h0s[s].ds)
                    yield

            pipeline([(lambda n=n: p1_tile(n)) for n in range(len(tiles))], lag=6, width=(W1 if '1' in PIPE_PH else 1), loads=p1_load, pf=2)
            fw.barrier()

        with ExitStack() as es:
            bq_bc = fw.tile(es, [128, 3072], F32, "bqkv", dma=True)
            fw.dma("sp", bq_bc[:], b_qkv.partition_broadcast(128), [], [bq_bc], bq_bc.ds)
            hT = [fw.tile(es, [128, 16, 128], BF16, "p2hT%d" % i, dma=True) for i in range(4)]
            rp = [fw.tile(es, [128, 256], F32, "rp%d" % i, dma=True) for i in range(4)]
            m8 = [fw.tile(es, [128, 8], F32, "m8_%d" % i, dma=True) for i in range(4)]
            qf = [fw.tile(es, [128, 1024], F32, "qf%d" % i) for i in range(2)]
            kf = [fw.tile(es, [128, 1024], F32, "kf%d" % i) for i in range(2)]
            vf = [fw.tile(es, [128, 1024], F32, "vf%d" % i) for i in range(2)]
            tq = [fw.tile(es, [128, 4, 16, 8], F32, "tq%d" % i) for i in range(2)]
            tk = [fw.tile(es, [128, 4, 16, 8], F32, "tk%d" % i) for i in range(2)]
            qb = [fw.tile(es, [128, 1024], BF16, "qb%d" % i) for i in range(2)]
            kb = [fw.tile(es, [128, 1024], BF16, "kb%d" % i) for i in range(2)]
            qT = [fw.tile(es, [128, 8, 128], BF16, "qTt%d" % i, dma=True) for i in range(2)]
            kT = [fw.tile(es, [128, 8, 128], BF16, "kTt%d" % i, dma=True) for i in range(2)]
            va = [fw.tile(es, [128, 8, 129], BF16, "va%d" % i, dma=True) for i in range(2)]
            tiles = [(sg, i, i < ne) for sg, ne, no in segs for i in range(ne + no)]

            def rope(eng, x, tmp, rps):
                e = fw.eng[eng]
                xv = x[:].rearrange("p (g d) -> p g d", d=64)
                x1, x2 = xv[:, :, 0:8], xv[:, :, 8:16]
                C = rps[:, 0:128].rearrange("p (g d) -> p g d", d=8)
                Sn = rps[:, 128:256].rearrange("p (g d) -> p g d", d=8)
                fw.op(eng, lambda: e.tensor_tensor(out=tmp[:, 0], in0=x1, in1=C, op=ALU.mult), [x, rps], [tmp])
                fw.op(eng, lambda: e.tensor_tensor(out=tmp[:, 1], in0=x2, in1=Sn, op=ALU.mult), [x, rps], [tmp])
                fw.op(eng, lambda: e.tensor_tensor(out=tmp[:, 2], in0=x2, in1=C, op=ALU.mult), [x, rps], [tmp])
                fw.op(eng, lambda: e.tensor_tensor(out=tmp[:, 3], in0=x1, in1=Sn, op=ALU.mult), [x, rps], [tmp])
                yield
                fw.op(eng, lambda: e.tensor_tensor(out=x1, in0=tmp[:, 0], in1=tmp[:, 1], op=ALU.subtract), [tmp], [x])
                fw.op(eng, lambda: e.tensor_tensor(out=x2, in0=tmp[:, 2], in1=tmp[:, 3], op=ALU.add), [tmp], [x])
                yield

            def p2_load(n):
                sg, i, isE = tiles[n]
                l = n % 4
                h_t, r_t, m_t = hT[l], rp[l], m8[l]
                fw.dma("sp", h_t[:], S[sg]["h0T"][i], [], [h_t], h_t.ds)
                fw.dma("sp", r_t[:], ROPE[sg][i * 128:(i + 1) * 128, :], [], [r_t], r_t.ds)
                fw.dma("sp", m_t[:], MASK8[sg][i * 128:(i + 1) * 128, :], [], [m_t], m_t.ds)

            def p2_tile(n):
                sg, i, isE = tiles[n]
                s = n % 2
                l = n % 4
                h_t, r_t, m_t = hT[l], rp[l], m8[l]
                groups = [2, 3, 4, 5, 0, 1] if isE else [2, 3, 4, 5]
                for g in groups:
                    for kc in range(16):
                        fw.op("pe", lambda g=g, kc=kc: nc.tensor.matmul(ps(g), lhsT=h_t[:, kc, :], rhs=wq[:, kc, g * 512:(g + 1) * 512],
                                                                        start=(kc == 0), stop=(kc == 15)),
                              [h_t, wq], [PB[g]], signal=(kc == 15))
                    yield
                    if g == 3:
                        fw.op("dve", lambda: nc.vector.tensor_tensor(out=kf[s][:].rearrange("p (a b) -> p a b", a=2), in0=ps(2, nb=2),
                                                                    in1=bq_bc[:, 1024:2048].rearrange("p (a b) -> p a b", a=2), op=ALU.add),
                              [PB[2], PB[3], bq_bc], [kf[s]])
                    if g == 5:
                        fw.op("dve", lambda: nc.vector.tensor_tensor(out=vf[s][:].rearrange("p (a b) -> p a b", a=2), in0=ps(4, nb=2),
                                                                    in1=bq_bc[:, 2048:3072].rearrange("p (a b) -> p a b", a=2), op=ALU.add),
                              [PB[4], PB[5], bq_bc], [vf[s]])
                    if g == 1:
                        fw.op("dve", lambda: nc.vector.tensor_tensor(out=qf[s][:].rearrange("p (a b) -> p a b", a=2), in0=ps(0, nb=2),
                                                                    in1=bq_bc[:, 0:1024].rearrange("p (a b) -> p a b", a=2), op=ALU.add),
                              [PB[0], PB[1], bq_bc], [qf[s]])
                fw.op("pool", lambda: nc.gpsimd.tensor_scalar(out=va[s][:, :, 0:128], in0=vf[s][:].rearrange("p (h e) -> p h e", e=128),
                                                             scalar1=m_t[:, 0:1], scalar2=1.0, op0=ALU.mult, op1=ALU.mult), [vf[s], m_t], [va[s]])
                fw.op("pool", lambda: nc.gpsimd.tensor_copy(out=va[s][:, :, 128:129], in_=m_t[:].unsqueeze(2)), [m_t], [va[s]])
                fw.dma("sp", S[sg]["v"][:, :, i, :].rearrange("h p e -> p h e"), va[s][:], [va[s]], [], va[s].ds)
                yield
                yield from rope("pool", kf[s], tk[s], r_t)
                fw.op("act", lambda: nc.scalar.copy(out=kb[s][:], in_=kf[s][:]), [kf[s]], [kb[s]])
                yield
                if isE:
                    yield from rope("dve", qf[s], tq[s], r_t)
                    fw.op("act", lambda: nc.scalar.activation(out=qb[s][:], in_=qf[s][:], func=AF.Copy, scale=HD ** -0.5), [qf[s]], [qb[s]])
                    yield
                for (src, dstT, pb, dkey, on) in ((kb[s], kT[s], 6, "kT", True), (qb[s], qT[s], 7, "qT", isE)):
                    if not on:
                        continue
                    for h in range(8):
                        fw.op("pe", lambda h=h, src=src, pb=pb: nc.tensor.transpose(out=ps_bf[:, pb, h * 128:(h + 1) * 128],
                                                                                   in_=src[:, h * 128:(h + 1) * 128], identity=ident_b[:]),
                              [src, ident_b], [PB[pb]], signal=(h == 7))
                    yield
                    fw.op("act", lambda dstT=dstT, pb=pb: nc.scalar.copy(out=dstT[:].rearrange("p a b -> p (a b)"), in_=ps_bf[:, pb, :]),
                          [PB[pb]], [dstT])
                    fw.dma("sp", S[sg][dkey][:, :, i * 128:(i + 1) * 128].rearrange("h p t -> p h t"), dstT[:], [dstT], [], dstT.ds)
                    yield

            pipeline([(lambda n=n: p2_tile(n)) for n in range(len(tiles))], lag=8, width=(None if '2' in PIPE_PH else 1), loads=p2_load, pf=2)
            fw.barrier()

        es12.close()
        es34 = ExitStack()
        dg = fw.tile(es34, [128, 8, 31, 128], BF16, "dg")
        onesm = fw.tile(es34, [128, 128], F32, "onesm")
        with ExitStack() as es:
            wu = load_w(es, "wu", w_in, 0, 16, 3072, 2048)
            fw.op("pool", lambda: nc.gpsimd.memset(onesm[:], 1.0 / D_CONV), [], [onesm])
            for j in range(8):
                for tp in range(31):
                    fw.op("pool", lambda j=j, tp=tp: nc.gpsimd.tensor_scalar(out=dg[:, j, tp, :], in0=ident_f[:], scalar1=cp(CP_CW + j * 31 + tp),
                                                                            scalar2=1.0, op0=ALU.mult, op1=ALU.mult), [ident_f, colp_t], [dg])
            hb = [fw.tile(es, [128, 4, 16, 128], BF16, "p3h%d" % i, dma=True) for i in range(2)]
            mr = [fw.tile(es, [128, 512], F32, "p3m%d" % i, dma=True) for i in range(2)]
            sig = [fw.tile(es, [128, 512], F32, "sig%d" % i) for i in range(2)]
            cf = [fw.tile(es, [128, 512], F32, "cf%d" % i) for i in range(2)]
            ct = [fw.tile(es, [128, 512], BF16, "ct%d" % i, dma=True) for i in range(3)]
            blks = [(sg, t0, n, ne) for sg, ne, no in segs for (t0, n) in blocks512(ne)]

            def p3_load(bi):
                sg, t0, n, ne = blks[bi]
                s = bi % 2
                fw.dma("sp", hb[s][:, 0:n], S[sg]["h0T"][t0 // 128:t0 // 128 + n].rearrange("t p k c -> p t k c"), [], [hb[s]], hb[s].ds)

            p3_load(0)
            it = 0
            for bi, (sg, t0, n, ne) in enumerate(blks):
                if bi + 1 < len(blks):
                    p3_load(bi + 1)
                s = bi % 2
                N = n * 128
                Etot = ne * 128
                edge = (t0 < 16) or (t0 + N > cfg.vhi[sg])
                if edge:
                    fw.dma("sp", mr[s][:, 0:N], MROW[sg][0:1, t0:t0 + N].partition_broadcast(128), [], [mr[s]], mr[s].ds)
                for j in range(8):
                    ba, bg = (it % 2) * 2, (it % 2) * 2 + 1
                    for part, bank in ((j, ba), (8 + j, bg)):
                        for kc in range(16):
                            fw.op("pe", lambda part=part, bank=bank, kc=kc: nc.tensor.matmul(
                                ps_t[:, bank, 0:N].rearrange("p (t c) -> p t c", c=128), lhsT=wu[:, kc, part * 128:(part + 1) * 128],
                                rhs=hb[s][:, 0:n, kc, :], start=(kc == 0), stop=(kc == 15)),
                                [wu, hb[s]], [PB[bank]], signal=(kc == 15))
                    sg_t, cf_t, ct_t = sig[it % 2], cf[it % 2], ct[it % 3]
                    fw.op("act", lambda: nc.scalar.activation(out=sg_t[:, 0:N], in_=ps(bg, N), func=AF.Sigmoid, bias=cp(CP_BU + 8 + j), scale=1.0),
                          [PB[bg], colp_t], [sg_t])
                    dst = cf_t if edge else ct_t
                    fw.op("dve", lambda: nc.vector.scalar_tensor_tensor(out=dst[:, 0:N], in0=ps(ba, N), scalar=cp(CP_BU + j), in1=sg_t[:, 0:N],
                                                                        op0=ALU.add, op1=ALU.mult), [PB[ba], colp_t, sg_t], [dst])
                    if edge:
                        fw.op("dve", lambda: nc.vector.tensor_tensor(out=ct_t[:, 0:N], in0=cf_t[:, 0:N], in1=mr[s][:, 0:N], op=ALU.mult),
                              [cf_t, mr[s]], [ct_t])
                    fw.dma("sp", S[sg]["c"][j, :, t0:t0 + N], ct_t[:, 0:N], [ct_t], [], ct_t.ds)
                    it += 1
            fw.barrier()

        with ExitStack() as es:
            cb = [fw.tile(es, [128, 542], BF16, "cb%d" % i, dma=True) for i in range(3)]
            xcs = [fw.tile(es, [128, 8, 512], F32, "xc%d" % i) for i in range(2)]
            xqs = [fw.tile(es, [128, 8, 512], F32, "xq%d" % i) for i in range(2)]
            msq = fw.tile(es, [128, 512], F32, "msq")
            var = fw.tile(es, [128, 512], F32, "var")
            tt = [fw.tile(es, [128, 512], F32, "tt%d" % i) for i in range(2)]
            co = [fw.tile(es, [128, 512], BF16, "co%d" % i, dma=True) for i in range(2)]
            blks = [(sg, bi, t0, n, ne) for sg, ne, no in segs for bi, (t0, n) in enumerate(blocks512(ne))]
            it = 0
            for bn, (sg, bidx, t0, n, ne) in enumerate(blks):
                N = n * 128
                Etot = ne * 128
                xc, xq = xcs[bn % 2], xqs[bn % 2]
                bm, bv = 2 + 2 * (bn % 2), 3 + 2 * (bn % 2)
                for j in range(8):
                    c_t = cb[it % 3]
                    lo, hi = t0 - 15, t0 + N + 15
                    clo, chi = max(lo, 0), min(hi, Etot)
                    if clo > lo:
                        fw.op("dve", lambda: nc.vector.memset(c_t[:, 0:clo - lo], 0.0), [], [c_t])
                    if chi < hi:
                        fw.op("dve", lambda: nc.vector.memset(c_t[:, chi - lo:hi - lo], 0.0), [], [c_t])
                    fw.dma("sp", c_t[:, clo - lo:chi - lo], S[sg]["c"][j, :, clo:chi], [], [c_t], c_t.ds)
                    bank = it % 2
                    for tp in range(31):
                        fw.op("pe", lambda tp=tp: nc.tensor.matmul(ps(bank, N), lhsT=dg[:, j, tp, :], rhs=c_t[:, tp:tp + N],
                                                                   start=(tp == 0), stop=(tp == 30)), [dg, c_t], [PB[bank]], signal=(tp == 30))
                    fw.op("act", lambda: nc.scalar.activation(out=xc[:, j, 0:N], in_=ps(bank, N), func=AF.Identity, bias=cp(CP_C3 + j * 3), scale=1.0),
                          [PB[bank], colp_t], [xc])
                    fw.op("act", lambda: nc.scalar.activation(out=xq[:, j, 0:N], in_=ps(bank, N), func=AF.Square, bias=cp(CP_C3 + j * 3), scale=1.0),
                          [PB[bank], colp_t], [xq])
                    it += 1
                for j in range(8):
                    fw.op("pe", lambda: nc.tensor.matmul(ps(bm, N), lhsT=onesm[:], rhs=xc[:, j, 0:N], start=(j == 0), stop=(j == 7)),
                          [onesm, xc], [PB[bm]], signal=(j == 7))
                for j in range(8):
                    fw.op("pe", lambda: nc.tensor.matmul(ps(bv, N), lhsT=onesm[:], rhs=xq[:, j, 0:N], start=(j == 0), stop=(j == 7)),
                          [onesm, xq], [PB[bv]], signal=(j == 7))
                fw.op("act", lambda: nc.scalar.activation(out=msq[:, 0:N], in_=ps(bm, N), func=AF.Square), [PB[bm]], [msq])
                fw.op("dve", lambda: nc.vector.tensor_tensor(out=var[:, 0:N], in0=ps(bv, N), in1=msq[:, 0:N], op=ALU.subtract), [PB[bv], msq], [var])
                fw.op("act", lambda: nc.scalar.activation(out=var[:, 0:N], in_=var[:, 0:N], func=AF.Sqrt, bias=epsc[:], scale=1.0), [var, epsc], [var])
                fw.op("dve", lambda: nc.vector.reciprocal(out=var[:, 0:N], in_=var[:, 0:N]), [var], [var])
                for j in range(8):
                    t_t, o_t = tt[j % 2], co[j % 2]
                    fw.op("dve", lambda: nc.vector.tensor_tensor(out=t_t[:, 0:N], in0=xc[:, j, 0:N], in1=ps(bm, N), op=ALU.subtract), [xc, PB[bm]], [t_t])
                    fw.op("dve", lambda: nc.vector.tensor_tensor(out=t_t[:, 0:N], in0=t_t[:, 0:N], in1=var[:, 0:N], op=ALU.mult), [t_t, var], [t_t])
                    fw.op("act", lambda: nc.scalar.activation(out=o_t[:, 0:N], in_=t_t[:, 0:N], func=AF.Silu, bias=cp(CP_C3 + j * 3 + 2),
                                                              scale=cp(CP_C3 + j * 3 + 1)), [t_t, colp_t], [o_t])
                    fw.dma("sp", S[sg]["mixT"][bidx, :, 8 + j, 0:N], o_t[:, 0:N], [o_t], [], o_t.ds)
            fw.barrier()

        es34.close()
        es56 = ExitStack()
        wo = load_w(es56, "wo", w_out, 0, 16, 0, 2048)
        with ExitStack() as es:
            zt = fw.tile(es, [128, NFC, 96], BF16, "zt", dma=True)
            fw.op("pool", lambda: nc.gpsimd.memset(zt[:], 0.0), [], [zt])
            for sg, ne, no in segs:
                fw.dma("pool", S[sg]["fT"][:, :, 0:32], zt[:, :, 0:32], [zt], [], zt.ds)
                r = cfg.rhi[sg]
                while r < ne * 128:
                    w_ = min(96, ne * 128 - r)
                    fw.dma("pool", S[sg]["fT"][:, :, r:r + w_], zt[:, :, 0:w_], [zt], [], zt.ds)
                    r += w_
            LKmax = max(ne + no for _, ne, no in segs)
            kTh = [fw.tile(es, [128, LKmax * 128], BF16, "kTh%d" % i, dma=True) for i in range(2)]
            vh = [fw.tile(es, [128, LKmax, 129], BF16, "vh%d" % i, dma=True) for i in range(2)]
            qblk = [fw.tile(es, [128, 512], BF16, "qblk%d" % i, dma=True) for i in range(3)]
            NP = 3
            P = [fw.tile(es, [128, 2, 512], BF16, "P%d" % i) for i in range(NP)]
            accS = [fw.tile(es, [128, 3, 396], F32, "accS%d" % i) for i in range(2)]
            rr = [fw.tile(es, [128, 4], F32, "rr%d" % i) for i in range(2)]
            ot = [fw.tile(es, [128, 128], F32, "ot%d" % i) for i in range(2)]
            osq = [fw.tile(es, [128, 128], F32, "osq%d" % i) for i in range(2)]
            ab = [fw.tile(es, [128, 128], BF16, "ab%d" % i) for i in range(2)]
            aT = [fw.tile(es, [128, 512], BF16, "aT%d" % i, dma=True) for i in range(2)]

            def acc_loc(c, j):
                idx = c * 4 + j
                return idx // 3, (idx % 3) * 132

            def epilogue(sg, h, t0, n, a_sb, a_s, ep_n):
                N = n * 128
                for j in range(n):
                    e2 = (ep_n * 4 + j) % 2
                    bk0, o0 = acc_loc(0, j)
                    bk1, o1 = acc_loc(1, j)
                    r_t, o_t, q_t, b_t = rr[e2], ot[e2], osq[e2], ab[e2]
                    fw.op("dve", lambda: nc.vector.reciprocal(out=r_t[:, 0:1], in_=a_sb[:, bk0, o0 + 128:o0 + 129]), [a_sb], [r_t])
                    fw.op("dve", lambda: nc.vector.reciprocal(out=r_t[:, 1:2], in_=a_sb[:, bk1, o1 + 128:o1 + 129]), [a_sb], [r_t])
                    yield
                    fw.op("dve", lambda: nc.vector.tensor_tensor(out=r_t[:, 1:2], in0=r_t[:, 1:2], in1=neglam[:], op=ALU.mult), [r_t, neglam], [r_t])
                    yield
                    fw.op("dve", lambda: nc.vector.tensor_scalar(out=o_t[:], in0=a_sb[:, bk0, o0:o0 + 128], scalar1=r_t[:, 0:1], scalar2=None, op0=ALU.mult),
                          [a_sb, r_t], [o_t])
                    yield
                    fw.op("dve", lambda: nc.vector.scalar_tensor_tensor(out=o_t[:], in0=a_sb[:, bk1, o1:o1 + 128], scalar=r_t[:, 1:2], in1=o_t[:],
                                                                        op0=ALU.mult, op1=ALU.add), [a_sb, r_t, o_t], [o_t])
                    yield
                    fw.op("dve", lambda: nc.vector.tensor_tensor(out=q_t[:], in0=o_t[:], in1=o_t[:], op=ALU.mult), [o_t], [q_t])
                    yield
                    fw.op("dve", lambda: nc.vector.tensor_reduce(out=r_t[:, 2:3], in_=q_t[:], axis=AX.X, op=ALU.add), [q_t], [r_t])
                    yield
                    fw.op("dve", lambda: nc.vector.tensor_scalar(out=r_t[:, 2:3], in0=r_t[:, 2:3], scalar1=1.0 / 128, scalar2=LN_EPS,
                                                                op0=ALU.mult, op1=ALU.add), [r_t], [r_t])
                    yield
                    fw.op("pool", lambda: nc.gpsimd.tensor_tensor(out=r_t[:, 3:4], in0=r_t[:, 2:3], in1=mhalf[:], op=ALU.pow), [r_t, mhalf], [r_t])
                    yield
                    fw.op("dve", lambda: nc.vector.scalar_tensor_tensor(out=b_t[:], in0=o_t[:], scalar=r_t[:, 3:4], in1=gsub[:],
                                                                        op0=ALU.mult, op1=ALU.mult), [o_t, r_t, gsub], [b_t])
                    yield
                    yield
                    fw.op("pe", lambda: nc.tensor.transpose(out=ps_bf[:, 7, j * 128:(j + 1) * 128], in_=b_t[:], identity=ident_b[:]),
                          [b_t, ident_b], [PB[7]])
                    yield
                fw.op("dve", lambda: nc.vector.tensor_copy(out=a_s[:, 0:N], in_=ps_bf[:, 7, 0:N]), [PB[7]], [a_s])
                fw.dma("sp", S[sg]["mixT"][t0 // 512, :, h, 0:N], a_s[:, 0:N], [a_s], [], a_s.ds)
                yield

            hi_n = 0
            qi_n = 0
            ep_n = 0
            pi_n = 0
            epi = None
            for sg, ne, no in segs:
                nt = ne + no
                for h in range(NH):
                    hs = hi_n % 2
                    hi_n += 1
                    fw.dma("sp", kTh[hs][:, 0:nt * 128], S[sg]["kT"][h], [], [kTh[hs]], kTh[hs].ds)
                    fw.dma("sp", vh[hs][:, 0:nt, :], S[sg]["v"][h], [], [vh[hs]], vh[hs].ds)
                    for (t0, n) in blocks512(ne):
                        N = n * 128
                        qs = qi_n % 3
                        qi_n += 1
                        fw.dma("sp", qblk[qs][:, 0:N], S[sg]["qT"][h, :, t0:t0 + N], [], [qblk[qs]], qblk[qs].ds)

                        def scores(kt):
                            sl = kt % 2
                            for c in range(2):
                                fw.op("pe", lambda c=c: nc.tensor.matmul(ps(2 * sl + c, N), lhsT=kTh[hs][c * 64:(c + 1) * 64, kt * 128:(kt + 1) * 128],
                                                                         rhs=qblk[qs][c * 64:(c + 1) * 64, 0:N], start=True, stop=True),
                                      [kTh[hs], qblk[qs]], [PB[2 * sl + c]], signal=(c == 1))

                        scores(0)
                        started = set()
                        for kt in range(nt):
                            if kt + 1 < nt:
                                scores(kt + 1)
                            sl = kt % 2
                            p_t = P[pi_n % NP]
                            pi_n += 1
                            fw.op("act", lambda: nc.scalar.activation(out=p_t[:, :, 0:N], in_=ps_t[:, 2 * sl:2 * sl + 2, 0:N], func=AF.Exp),
                                  [PB[2 * sl], PB[2 * sl + 1]], [p_t])
                            for c in range(2):
                                for j in range(n):
                                    bk, o = acc_loc(c, j)
                                    b = 4 + bk
                                    first = b not in started
                                    started.add(b)
                                    last = (c == 1 and j == n - 1)
                                    fw.op("pe", lambda: nc.tensor.matmul(ps_t[:, b, o:o + 129], lhsT=p_t[:, c, j * 128:(j + 1) * 128], rhs=vh[hs][:, kt, :],
                                                                         start=first, stop=(kt == nt - 1), skip_group_check=True),
                                          [p_t, vh[hs]], [PB[b]], signal=last)
                            if epi is not None and EPI_INTERLEAVE:
                                for _ in range(2):
                                    if next(epi, "end") == "end":
                                        epi = None
                                        break
                        if epi is not None:
                            for _ in epi:
                                pass
                            epi = None
                        a_sb = accS[ep_n % 2]
                        used = sorted({acc_loc(c, j)[0] for c in range(2) for j in range(n)})
                        for bk in used:
                            fw.op("dve", lambda bk=bk: nc.vector.tensor_copy(out=a_sb[:, bk, :], in_=ps_t[:, 4 + bk, 0:396]), [PB[4 + bk]], [a_sb])
                        epi = epilogue(sg, h, t0, n, a_sb, aT[ep_n % 2], ep_n)
                        ep_n += 1
            if epi is not None:
                for _ in epi:
                    pass
            fw.barrier()

        with ExitStack() as es:
            g_bc = bcast_row(es, "ln1g", ROWS["ln1_g"])
            b_bc = bcast_row(es, "ln1b", ROWS["ln1_b"])
            bd_bc = bcast_row(es, "bdn", ROWS["b_down"])
            mx = [fw.tile(es, [128, 16, 512], BF16, "mx%d" % i, dma=True) for i in range(3)]
            hs_ = [fw.tile(es, [128, D], F32, "p6hs%d" % i, dma=True) for i in range(4)]
            h1b = [fw.tile(es, [128, D], BF16, "h1b%d" % i) for i in range(2)]
            hT = [fw.tile(es, [128, 16, 128], BF16, "p6hT%d" % i, dma=True) for i in range(2)]
            lb = [ln_bufs(es, "b%d" % i) for i in range(2)]
            blks = [(sg, bi, t0, n) for sg, ne, no in segs for bi, (t0, n) in enumerate(blocks512(ne))]
            work = [(bi, tl) for bi, (sg, bidx, t0, n) in enumerate(blks) for tl in range(n)]

            def p6_load(wn):
                bi, tl = work[wn]
                sg, bidx, t0, n = blks[bi]
                m_t = mx[bi % 3]
                if tl == 0:
                    fw.dma("sp", m_t[:, :, 0:n * 128], S[sg]["mixT"][bidx, :, :, 0:n * 128], [], [m_t], m_t.ds)
                hs_t = hs_[wn % 4]
                fw.dma("sp", hs_t[:], S[sg]["h0s"][t0 // 128 + tl], [], [hs_t], hs_t.ds)

            def p6_tile(wn):
                bi, tl = work[wn]
                sg, bidx, t0, n = blks[bi]
                m_t = mx[bi % 3]
                s = wn % 2
                ti = t0 // 128 + tl
                hs_t = hs_[wn % 4]
                for nb in range(4):
                    for kc in range(16):
                        fw.op("pe", lambda nb=nb, kc=kc: nc.tensor.matmul(ps(nb), lhsT=m_t[:, kc, tl * 128:(tl + 1) * 128],
                                                                          rhs=wo[:, kc, nb * 512:(nb + 1) * 512], start=(kc == 0), stop=(kc == 15)),
                              [m_t, wo], [PB[nb]], signal=(kc == 15))
                    yield
                    if nb % 2 == 1:
                        hf = nb // 2
                        fw.op("dve", lambda hf=hf: nc.vector.tensor_tensor(out=hs_t[:, hf * 1024:(hf + 1) * 1024].rearrange("p (a b) -> p a b", a=2),
                                                                          in0=ps(2 * hf, nb=2),
                                                                          in1=hs_t[:, hf * 1024:(hf + 1) * 1024].rearrange("p (a b) -> p a b", a=2),
                                                                          op=ALU.add), [PB[2 * hf], PB[2 * hf + 1], hs_t], [hs_t])
                h1_t = lb[s]["xn"]
                yield from layer_norm_tm(es, hs_t[:], [hs_t], lb[s], g_bc, b_bc, h1_t, mul_eng="dve")
                fw.op("act", lambda: nc.scalar.copy(out=h1b[s][:], in_=h1_t[:]), [h1_t], [h1b[s]])
                yield
                fw.op("dve", lambda: nc.vector.scalar_tensor_tensor(out=hs_t[:], in0=h1_t[:], scalar=ALPHA, in1=bd_bc[:],
                                                                    op0=ALU.mult, op1=ALU.add), [h1_t, bd_bc], [hs_t])
                fw.dma("sp", S[sg]["h1s"][ti], hs_t[:], [hs_t], [], hs_t.ds)
                yield
                yield from transpose16(h1b[s], [h1b[s]], hT[s], [4 + 2 * s, 5 + 2 * s], "act")
                fw.dma("sp", S[sg]["h1T"][:, :, ti * 128:(ti + 1) * 128], hT[s][:], [hT[s]], [], hT[s].ds)
                yield

            pipeline([(lambda wn=wn: p6_tile(wn)) for wn in range(len(work))], lag=7, width=(None if '6' in PIPE_PH else 1), loads=p6_load, pf=2)
            fw.barrier()

        es56.close()
        es78 = ExitStack()
        wd = [fw.tile(es78, [128, NFC, 512], BF16, "wdn%d" % i, dma=True) for i in range(2)]
        with ExitStack() as es:
            NG = 11
            CPG = NFC // NG
            wgs = [fw.tile(es, [128, 16, 2 * CPG * 128], BF16, "wup%d" % i, dma=True) for i in range(2)]

            def p7_wload(g):
                load_w(es, "wup", w_up, 0, 16, g * CPG * 128, CPG * 128, dst=wgs[g % 2], dstc0=0)
                load_w(es, "wup", w_up, 0, 16, D_FF + g * CPG * 128, CPG * 128, dst=wgs[g % 2], dstc0=CPG * 128)

            p7_wload(0)
            hb = [fw.tile(es, [128, 16, 512], BF16, "p7h%d" % i, dma=True) for i in range(2)]
            mr = [fw.tile(es, [128, 512], F32, "p7m%d" % i, dma=True) for i in range(1)] * 2
            ag = [fw.tile(es, [128, 512], F32, "ag%d" % i) for i in range(2)]
            au = [fw.tile(es, [128, 512], F32, "au%d" % i) for i in range(2)]
            sgt = [fw.tile(es, [128, 512], F32, "sgt%d" % i) for i in range(2)]
            fo = [fw.tile(es, [128, 512], BF16, "fo%d" % i, dma=True) for i in range(3)]
            ccn = fw.tile(es, [128, 88], F32, "ccn")
            fv = colp_t[:, CP_F:CP_F + 440].rearrange("p (c k) -> p c k", k=5)
            fw.op("dve", lambda: nc.vector.tensor_tensor(out=ccn[:].unsqueeze(2), in0=fv[:, :, 1:2], in1=fv[:, :, 2:3], op=ALU.add), [colp_t], [ccn])
            fw.op("dve", lambda: nc.vector.tensor_tensor(out=ccn[:].unsqueeze(2), in0=ccn[:].unsqueeze(2), in1=fv[:, :, 3:4], op=ALU.add), [colp_t, ccn], [ccn])
            fw.op("dve", lambda: nc.vector.tensor_tensor(out=ccn[:].unsqueeze(2), in0=ccn[:].unsqueeze(2), in1=fv[:, :, 0:1], op=ALU.mult), [colp_t, ccn], [ccn])
            fw.op("dve", lambda: nc.vector.tensor_tensor(out=ccn[:].unsqueeze(2), in0=ccn[:].unsqueeze(2), in1=fv[:, :, 4:5], op=ALU.add), [colp_t, ccn], [ccn])

            def fcp(c, k):
                return colp_t[:, CP_F + c * 5 + k:CP_F + c * 5 + k + 1]

            it = 0
            ld = 0
            for g in range(NG):
                wg = wgs[g % 2]
                if g + 1 < NG:
                    p7_wload(g + 1)
                else:
                    for i_ in range(2):
                        load_w(es78, "wdn", w_down, 0, NFC, i_ * 512, 512, dst=wd[i_])
                blks = [(sg, s0, n, ne) for sg, ne, no in segs for (s0, n) in ffn_blocks(cfg.rhi[sg])]

                def p7_load(bi, ld):
                    sg, s0, n, ne = blks[bi]
                    fw.dma("sp", hb[ld % 2][:, :, 0:n], S[sg]["h1T"][:, :, s0:s0 + n], [], [hb[ld % 2]], hb[ld % 2].ds)

                p7_load(0, ld)
                for bi, (sg, s0, n, ne) in enumerate(blks):
                    if bi + 1 < len(blks):
                        p7_load(bi + 1, ld + 1)
                    h_t = hb[ld % 2]
                    m_t = mr[ld % 2]
                    ld += 1
                    Etot = ne * 128
                    edge = (s0 < 16) or (s0 + n > cfg.vhi[sg])
                    M = n - 2
                    if edge:
                        fw.dma("sp", m_t[:, 0:n], MROW[sg][0:1, s0:s0 + n].partition_broadcast(128), [], [m_t], m_t.ds)
                    for jj in range(CPG):
                        cg = g * CPG + jj
                        cu = NFC + cg
                        bg_, bu_ = (it % 2) * 2, (it % 2) * 2 + 1
                        for (col0, bank) in ((jj * 128, bg_), (CPG * 128 + jj * 128, bu_)):
                            for kc in range(16):
                                fw.op("pe", lambda col0=col0, bank=bank, kc=kc: nc.tensor.matmul(ps(bank, n), lhsT=wg[:, kc, col0:col0 + 128],
                                                                                                 rhs=h_t[:, kc, 0:n], start=(kc == 0), stop=(kc == 15)),
                                      [wg, h_t], [PB[bank]], signal=(kc == 15))
                        a_g, a_u, s_t, f_t = ag[it % 2], au[it % 2], sgt[it % 2], fo[it % 3]
                        for (cc_, bank, acc) in ((cg, bg_, a_g), (cu, bu_, a_u)):
                            if edge:
                                u_t = sgt[it % 2]
                                fw.op("dve", lambda: nc.vector.scalar_tensor_tensor(out=u_t[:, 0:n], in0=ps(bank, n), scalar=fcp(cc_, 0), in1=m_t[:, 0:n],
                                                                                    op0=ALU.add, op1=ALU.mult), [PB[bank], colp_t, m_t], [u_t])
                                src0, src1, src2 = u_t[:, 0:M], u_t[:, 1:M + 1], u_t[:, 2:M + 2]
                                sb = [u_t]
                            else:
                                pp = ps(bank, n)
                                src0, src1, src2 = pp[:, 0:M], pp[:, 1:M + 1], pp[:, 2:M + 2]
                                sb = [PB[bank]]
                            fw.op("dve", lambda: nc.vector.tensor_scalar(out=acc[:, 0:M], in0=src0, scalar1=fcp(cc_, 1), scalar2=None, op0=ALU.mult),
                                  sb + [colp_t], [acc])
                            fw.op("dve", lambda: nc.vector.scalar_tensor_tensor(out=acc[:, 0:M], in0=src1, scalar=fcp(cc_, 2), in1=acc[:, 0:M],
                                                                                op0=ALU.mult, op1=ALU.add), sb + [colp_t, acc], [acc])
                            fw.op("dve", lambda: nc.vector.scalar_tensor_tensor(out=acc[:, 0:M], in0=src2, scalar=fcp(cc_, 3), in1=acc[:, 0:M],
                                                                                op0=ALU.mult, op1=ALU.add), sb + [colp_t, acc], [acc])
                        bias_g = fcp(cg, 4) if edge else ccn[:, cg:cg + 1]
                        bias_u = fcp(cu, 4) if edge else ccn[:, cu:cu + 1]
                        fw.op("act", lambda: nc.scalar.activation(out=s_t[:, 0:M], in_=a_g[:, 0:M], func=AF.Silu, bias=bias_g, scale=1.0),
                              [a_g, colp_t, ccn], [s_t])
                        fw.op("dve", lambda: nc.vector.scalar_tensor_tensor(out=f_t[:, 0:M], in0=a_u[:, 0:M], scalar=bias_u, in1=s_t[:, 0:M],
                                                                            op0=ALU.add, op1=ALU.mult), [a_u, colp_t, ccn, s_t], [f_t])
                        fw.dma("sp", S[sg]["fT"][:, cg, s0 + 1:s0 + 1 + M], f_t[:, 0:M], [f_t], [], f_t.ds)
                        it += 1
            fw.barrier()

        with ExitStack() as es:
            g_bc = bcast_row(es, "ln2g", ROWS["ln2_g"])
            b_bc = bcast_row(es, "ln2b", ROWS["ln2_b"])
            fb = [fw.tile(es, [128, NFC, 128], BF16, "fb%d" % i, dma=True) for i in range(4)]
            pa = [fw.tile(es, [128, 1024], F32, "pa%d" % i, dma=True) for i in range(3)]
            hs_ = [fw.tile(es, [128, D], F32, "p8hs%d" % i, dma=True) for i in range(3)]
            lb = [ln_bufs(es, "c%d" % i, dma=True) for i in range(2)]
            yo = [lb[i]["xn"] for i in range(2)]
            blks = []
            for sg, ne, no in segs:
                t = 0
                while t < ne:
                    n = min(2, ne - t)
                    blks.append((sg, t, n))
                    t += n
            work = [(bi, tl) for bi, (sg, t, n) in enumerate(blks) for tl in range(n)]
            for half in range(2):
                if half == 1:
                    for i_ in range(2):
                        load_w(es, "wdn", w_down, 0, NFC, half * 1024 + i_ * 512, 512, dst=wd[i_])

                def p8_load(wn, half=half):
                    bi, tl = work[wn]
                    sg, t, n = blks[bi]
                    ti = t + tl
                    f_t = fb[wn % 4]
                    fw.dma("sp", f_t[:], S[sg]["fT"][:, :, ti * 128:(ti + 1) * 128], [], [f_t], f_t.ds)
                    if half == 1:
                        fw.dma("sp", hs_[wn % 3][:], S[sg]["h1s"][ti], [], [hs_[wn % 3]], hs_[wn % 3].ds)
                        fw.dma("sp", pa[wn % 3][:], S[sg]["ffa"][ti], [], [pa[wn % 3]], pa[wn % 3].ds)

                def p8_tile(wn, half=half):
                    bi, tl = work[wn]
                    sg, t, n = blks[bi]
                    f_t = fb[wn % 4]
                    s = wn % 2
                    ti = t + tl
                    b0 = 2 * s + (4 if half else 0)
                    hs_t, pa_t = hs_[wn % 3], pa[wn % 3]
                    for nb in range(2):
                        for kc in range(NFC):
                            fw.op("pe", lambda nb=nb, kc=kc: nc.tensor.matmul(ps(b0 + nb), lhsT=f_t[:, kc, :],
                                                                              rhs=wd[nb][:, kc, :], start=(kc == 0), stop=(kc == NFC - 1)),
                                  [f_t, wd[nb]], [PB[b0 + nb]], signal=(kc == NFC - 1))
                        yield
                    if half == 0:
                        fw.op("act", lambda: nc.scalar.copy(out=pa_t[:].rearrange("p (a b) -> p a b", a=2), in_=ps(b0, nb=2)),
                              [PB[b0], PB[b0 + 1]], [pa_t])
                        fw.dma("sp", S[sg]["ffa"][ti], pa_t[:], [pa_t], [], pa_t.ds)
                        yield
                    else:
                        fw.op("pool", lambda: nc.gpsimd.tensor_tensor(out=hs_t[:, 0:1024], in0=pa_t[:], in1=hs_t[:, 0:1024], op=ALU.add),
                              [pa_t, hs_t], [hs_t])
                        fw.op("dve", lambda: nc.vector.tensor_tensor(out=hs_t[:, 1024:2048].rearrange("p (a b) -> p a b", a=2), in0=ps(b0, nb=2),
                                                                    in1=hs_t[:, 1024:2048].rearrange("p (a b) -> p a b", a=2), op=ALU.add),
                              [PB[b0], PB[b0 + 1], hs_t], [hs_t])
                        yield
                        yield from layer_norm_tm(es, hs_t[:], [hs_t], lb[s], g_bc, b_bc, yo[s])
                        fw.dma("sp", Y[sg][ti * 128:(ti + 1) * 128, :], yo[s][:], [yo[s]], [], yo[s].ds)
                        yield

                pipeline([(lambda wn=wn: p8_tile(wn)) for wn in range(len(work))], lag=(2 if half == 0 else 4), width=(None if '8' in PIPE_PH else 1), loads=p8_load, pf=1)
                fw.barrier()
        es78.close()
        fw.barrier()
    build.nops = fw.nops
    return nc


def geometry(S_p, S_s):
    L = S_p + N_META
    W = L // 2
    E = -(-(W + 32) // 128) * 128
    O_valid = max(L - (E - 16), S_p // 2 - 16)
    O = -(-O_valid // 128) * 128
    L2 = S_s + N_META
    E2 = -(-(L2 + 32) // 128) * 128
    return dict(L=L, W=W, E=E, O=O, O_valid=O_valid, L2=L2, E2=E2)


def rope_table(pos):
    rot = HD // 4
    inv = ROPE_THETA ** (-np.arange(0, rot, 2, dtype=np.float32) / rot)
    ang = pos.astype(np.float32)[:, None] * inv[None, :]
    cos, sin = np.cos(ang).astype(np.float32), np.sin(ang).astype(np.float32)
    return np.concatenate([np.tile(cos, (1, 16)), np.tile(sin, (1, 16))], axis=1).astype(np.float32)


_CACHE = {}


def kernel(x_prompt, x_sample, meta_tokens, ln_emb_g, ln_emb_b, w_in, b_in, lambda_q1, lambda_k1, lambda_q2, lambda_k2,
           subln_g, conv_w, conv_b, conv_ln_g, conv_ln_b, w_out, b_out, ln1_g, ln1_b, w_up, b_up, ffn_conv_w, ffn_conv_b,
           w_down, b_down, ln2_g, ln2_b, _debug=(), _return_raw=False):
    f = lambda a: np.ascontiguousarray(np.asarray(a, dtype=np.float32))
    x_prompt, x_sample, meta = f(x_prompt), f(x_sample), f(meta_tokens)
    B, S_p, _ = x_prompt.shape
    B2, S_s, _ = x_sample.shape
    assert B * 2 == NCORES and B2 == NCORES
    G = geometry(S_p, S_s)
    L, E, O, L2, E2 = G["L"], G["E"], G["O"], G["L2"], G["E2"]
    cfg = Cfg(E // 128, O // 128, E2 // 128, L2, debug=_debug, half_p=S_p // 2)
    key = (E, O, E2, tuple(sorted(_debug)))
    if key not in _CACHE:
        _CACHE[key] = build(cfg)
    nc = _CACHE[key]

    w_in0, w_out0, w_up0, w_down0 = f(w_in)[0], f(w_out)[0], f(w_up)[0], f(w_down)[0]
    b_in0 = f(b_in)[0]
    rowp = np.zeros((16, D), np.float32)
    for i, a in enumerate([ln_emb_g, ln_emb_b, f(b_out)[0], f(ln1_g)[0], f(ln1_b)[0], f(b_down)[0], f(ln2_g)[0], f(ln2_b)[0]]):
        rowp[i] = f(a)
    b_qkv = b_in0[None, :3072].copy()
    lamv = np.stack([f(lambda_q1)[0], f(lambda_k1)[0], f(lambda_q2)[0], f(lambda_k2)[0]]).astype(np.float32)
    subg = f(subln_g)[0][None, :].copy()
    colp = np.zeros((128, 16 + 8 * 31 + 8 * 3 + 88 * 5 + 32), np.float32)
    colp[:, 0:16] = b_in0[3072:].reshape(16, 128).T
    cw = f(conv_w)[0]
    colp[:, 16:16 + 248] = cw.reshape(31, 8, 128).transpose(2, 1, 0).reshape(128, 248)
    c3 = np.stack([f(conv_b)[0], f(conv_ln_g)[0], f(conv_ln_b)[0]])
    colp[:, 264:264 + 24] = c3.reshape(3, 8, 128).transpose(2, 1, 0).reshape(128, 24)
    f5 = np.concatenate([f(b_up)[0][None], f(ffn_conv_w)[0], f(ffn_conv_b)[0][None]], axis=0)
    colp[:, 288:288 + 440] = f5.reshape(5, 88, 128).transpose(2, 1, 0).reshape(128, 440)
    colp[:, 728:744] = f(ln_emb_g).reshape(16, 128).T
    colp[:, 744:760] = f(ln_emb_b).reshape(16, 128).T
    ident = np.eye(128, dtype=np.float32)

    in_maps = []
    info = []
    for c in range(NCORES):
        b, half = c // 2, c % 2
        seq = np.concatenate([meta, x_prompt[b]], axis=0)
        start = -16 if half == 0 else S_p // 2 - 16
        pos_e = np.arange(start, start + E)
        val_e = (pos_e >= 0) & (pos_e < L)
        xe = np.zeros((E + O, D), np.float32)
        xe[:E][val_e] = seq[pos_e[val_e]]
        other = np.arange(E - 16, L) if half == 0 else np.arange(0, S_p // 2 - 16)
        pos_o = np.zeros(O, np.int64)
        val_o = np.zeros(O, bool)
        pos_o[:len(other)] = other
        val_o[:len(other)] = True
        xe[E:E + len(other)] = seq[other]
        pos_all = np.concatenate([np.where(val_e, pos_e, 0), pos_o])
        val_all = np.concatenate([val_e, val_o])
        seq2 = np.concatenate([meta, x_sample[c]], axis=0)
        pos_s = np.arange(-16, -16 + E2)
        val_s = (pos_s >= 0) & (pos_s < L2)
        xs = np.zeros((E2, D), np.float32)
        xs[val_s] = seq2[pos_s[val_s]]
        in_maps.append({
            "xe_p": xe, "xe_s": xs,
            "mask8_p": np.repeat(val_all[:, None], 8, axis=1).astype(np.float32),
            "mask8_s": np.repeat(val_s[:, None], 8, axis=1).astype(np.float32),
            "mrow_p": val_e[None, :].astype(np.float32), "mrow_s": val_s[None, :].astype(np.float32),
            "rope_p": rope_table(pos_all), "rope_s": rope_table(np.where(val_s, pos_s, 0)),
            "w_in": w_in0, "w_out": w_out0, "w_up": w_up0, "w_down": w_down0, "ident": ident, "rowp": rowp,
            "b_qkv": b_qkv, "lamv": lamv, "subg": subg, "colp": colp,
        })
        r0 = 32
        info.append((b, half, r0))
    res = run_bass_kernel_spmd(nc, in_maps, core_ids=list(range(NCORES)))
    if _return_raw:
        return res.results, info, G
    y_p = np.empty((B, S_p, D), np.float32)
    y_s = np.empty((B2, S_s, D), np.float32)
    for c in range(NCORES):
        b, half, r0 = info[c]
        r = res.results[c]
        y_p[b, half * (S_p // 2):(half + 1) * (S_p // 2)] = r["y_p"][r0:r0 + S_p // 2]
        y_s[c] = r["y_s"][32:32 + S_s]
    return (y_p, y_s)
```
